# Optimizing a Trainium2 kernel written in Bass

```python
import math
import jax, jax.numpy as jnp
from jax import lax
import numpy as np

D_MODEL = 1024
BATCH = 8
SEQ = 4096
DEPTH = 2
DEC_BATCH = 32
DEC_SEQ = 8
PAST_LEN = 16384
PAGE_SIZE = 128

N_A_LAYERS = DEPTH // 2
N_B_LAYERS = DEPTH - N_A_LAYERS
MLSTM_HEADS = 4
MLSTM_INNER = 2 * D_MODEL
MLSTM_HEAD_DIM = MLSTM_INNER // MLSTM_HEADS
MLSTM_CHUNK = 64
GROUPS = ((128, 1), (512, 4), (2048, 16))
N_GROUPS = len(GROUPS)
GROUP_HEADS = 8
ATTN_HEAD_DIM = 64
N_Q_HEADS = N_GROUPS * GROUP_HEADS
ATTN_OUT = GROUP_HEADS * ATTN_HEAD_DIM
ATTN_BLOCK = 128
N_BUCKETS = 32
MAX_DISTANCE = 2048
EPS = 1e-6

kernel_name = 'yoco_mlstm_dilated_swa_step'


def _rmsnorm(x, g):
    xf = x.astype(jnp.float32)
    y = xf * lax.rsqrt(jnp.mean(xf * xf, axis=-1, keepdims=True) + EPS)
    return (y * g.astype(jnp.float32)).astype(x.dtype)


def _t5_bucket(dist):
    exact = N_BUCKETS // 2
    d = jnp.maximum(dist, 1).astype(jnp.float32)
    large = exact + (jnp.log(d / exact) / math.log(MAX_DISTANCE / exact) * (N_BUCKETS - exact)).astype(jnp.int32)
    return jnp.where(dist < exact, dist, jnp.minimum(large, N_BUCKETS - 1))


def _mlstm_scan(q, k, v, ig, lf, C0, n0, m0):
    B, T, H, Dh = q.shape
    L = min(MLSTM_CHUNK, T)
    pad = (-T) % L
    if pad:
        q = jnp.pad(q, ((0, 0), (0, pad), (0, 0), (0, 0)))
        k = jnp.pad(k, ((0, 0), (0, pad), (0, 0), (0, 0)))
        v = jnp.pad(v, ((0, 0), (0, pad), (0, 0), (0, 0)))
        ig = jnp.pad(ig, ((0, 0), (0, pad), (0, 0)), constant_values=-jnp.inf)
        lf = jnp.pad(lf, ((0, 0), (0, pad), (0, 0)))
    nc = (T + pad) // L

    def chunks(a):
        a = a.reshape((B, nc, L) + a.shape[2:])
        return jnp.moveaxis(a, (1, 2), (0, 3))

    causal = jnp.tril(jnp.ones((L, L), dtype=bool))

    def step(carry, xs):
        C, n, m = carry
        qc, kc, vc, igc, lfc = xs
        b = jnp.cumsum(lfc, axis=-1)
        log_inter = b + m[..., None]
        log_d = jnp.where(causal, b[..., :, None] - b[..., None, :] + igc[..., None, :], -jnp.inf)
        m_t = jnp.maximum(log_inter, jnp.max(log_d, axis=-1))
        dmat = jnp.exp(log_d - m_t[..., None])
        inter = jnp.exp(log_inter - m_t)
        s = jnp.einsum('bhtd,bhsd->bhts', qc, kc) * dmat
        num = jnp.einsum('bhts,bhse->bhte', s, vc) + inter[..., None] * jnp.einsum('bhtd,bhde->bhte', qc, C)
        den = jnp.sum(s, axis=-1) + inter * jnp.einsum('bhtd,bhd->bht', qc, n)
        h = num / jnp.maximum(jnp.abs(den), jnp.exp(-m_t))[..., None]
        m_new = m_t[..., -1]
        w_in = jnp.exp(b[..., -1:] - b + igc - m_new[..., None])
        decay = jnp.exp(b[..., -1] + m - m_new)
        C_new = decay[..., None, None] * C + jnp.einsum('bhsd,bhse->bhde', kc * w_in[..., None], vc)
        n_new = decay[..., None] * n + jnp.einsum('bhsd,bhs->bhd', kc, w_in)
        return (C_new, n_new, m_new), h

    (C, n, m), hs = lax.scan(step, (C0, n0, m0), (chunks(q), chunks(k), chunks(v), chunks(ig), chunks(lf)))
    hs = jnp.moveaxis(hs, (0, 3), (1, 2)).reshape(B, nc * L, H, Dh)[:, :T]
    return hs, (C, n, m)


def _mlstm_layer(x, C0, n0, m0, norm_g, w_in, b_gates, h_gain, w_out):
    f32 = jnp.float32
    B, T, _ = x.shape
    H, Dh, DI = MLSTM_HEADS, MLSTM_HEAD_DIM, MLSTM_INNER
    p = _rmsnorm(x, norm_g) @ w_in
    q, k, v, o, z = (p[..., i * DI:(i + 1) * DI] for i in range(5))
    gates = p[..., 5 * DI:].astype(f32) + b_gates.astype(f32)
    ig = gates[..., :H]
    lf = jax.nn.log_sigmoid(gates[..., H:])
    heads = lambda a: a.astype(f32).reshape(B, T, H, Dh)
    h, (C, n, m) = _mlstm_scan(heads(q), heads(k) * (Dh ** -0.5), heads(v), ig, lf,
                               C0.astype(f32), n0.astype(f32), m0.astype(f32))
    h = h * lax.rsqrt(jnp.mean(h * h, axis=-1, keepdims=True) + EPS)
    h = h.reshape(B, T, DI) * h_gain.astype(f32)
    y = (h * jax.nn.sigmoid(o.astype(f32)) * jax.nn.silu(z.astype(f32))).astype(x.dtype) @ w_out
    return x + y, C, n, m


def _shared_kv(x, norm_g, w_kv, k_gain):
    B, T, _ = x.shape
    p = _rmsnorm(x, norm_g) @ w_kv
    k = _rmsnorm(p[..., :N_Q_HEADS * ATTN_HEAD_DIM].reshape(B, T, N_Q_HEADS, ATTN_HEAD_DIM), k_gain)
    v = p[..., N_Q_HEADS * ATTN_HEAD_DIM:].reshape(B, T, N_Q_HEADS, ATTN_HEAD_DIM)
    return k, v


def _b_query(x, norm_g, w_in, q_gain):
    B, T, _ = x.shape
    p = _rmsnorm(x, norm_g) @ w_in
    q = _rmsnorm(p[..., :N_Q_HEADS * ATTN_HEAD_DIM].reshape(B, T, N_Q_HEADS, ATTN_HEAD_DIM), q_gain)
    z = p[..., N_Q_HEADS * ATTN_HEAD_DIM:]
    return q, z


def _group_prompt(q, k, v, bias_tab, win, dil):
    B, T, GH, HD = q.shape
    J = win // dil
    span = dil * ATTN_BLOCK
    Tp = -(-T // span) * span
    pad = Tp - T
    S = Tp // dil
    nb = S // ATTN_BLOCK
    padt = lambda a: jnp.pad(a, ((0, 0), (0, pad), (0, 0), (0, 0)))

    def blocks(a):
        a = a.reshape(B, S, dil, GH, HD).transpose(0, 2, 1, 3, 4)
        return a.reshape(B * dil, nb, ATTN_BLOCK, GH, HD)

    def with_prev(a):
        prev = jnp.concatenate([jnp.zeros_like(a[:, :1]), a[:, :-1]], axis=1)
        return jnp.concatenate([prev, a], axis=2)

    qb = blocks(padt(q))
    kc = with_prev(blocks(padt(k)))
    vc = with_prev(blocks(padt(v)))
    qi = jnp.arange(ATTN_BLOCK)[:, None]
    kj = jnp.arange(2 * ATTN_BLOCK)[None, :]
    rel = qi + ATTN_BLOCK - kj
    band = (rel >= 0) & (rel <= J)
    valid = band[None] & ((jnp.arange(nb) > 0)[:, None, None] | (kj >= ATTN_BLOCK)[None])
    bias = bias_tab.astype(jnp.float32)[_t5_bucket(jnp.clip(rel, 0, J) * dil)]
    s = jnp.einsum('bnqhd,bnkhd->bnhqk', qb, kc).astype(jnp.float32) * (HD ** -0.5)
    s = s + jnp.moveaxis(bias, -1, 0)[None, None]
    s = jnp.where(valid[None, :, None], s, -jnp.inf)
    lse = jax.nn.logsumexp(s, axis=-1)
    p = jnp.exp(s - lse[..., None])
    o = jnp.einsum('bnhqk,bnkhd->bnqhd', p.astype(vc.dtype), vc)
    o = o.reshape(B, dil, S, GH, HD).transpose(0, 2, 1, 3, 4).reshape(B, Tp, GH, HD)[:, :T]
    lse = jnp.moveaxis(lse, 2, 3).reshape(B, dil, S, GH).transpose(0, 2, 1, 3).reshape(B, Tp, GH)[:, :T]
    return o, lse


def _group_sample(q, k_new, v_new, buf, bias_tab, win, dil):
    B, S, GH, HD = q.shape
    L = buf.shape[1]
    J = win // dil
    k_all = jnp.concatenate([buf[:, :, 0].astype(k_new.dtype), k_new], axis=1)
    v_all = jnp.concatenate([buf[:, :, 1].astype(v_new.dtype), v_new], axis=1)
    j = jnp.arange(J + 1)
    idx = L + jnp.arange(S)[:, None] - dil * j[None, :]
    valid = idx >= 0
    idx = jnp.maximum(idx, 0)
    kg = k_all[:, idx]
    vg = v_all[:, idx]
    bias = bias_tab.astype(jnp.float32)[_t5_bucket(dil * j)]
    s = jnp.einsum('bshd,bsjhd->bshj', q, kg).astype(jnp.float32) * (HD ** -0.5) + bias.T[None, None]
    s = jnp.where(valid[None, :, None, :], s, -jnp.inf)
    lse = jax.nn.logsumexp(s, axis=-1)
    p = jnp.exp(s - lse[..., None])
    o = jnp.einsum('bshj,bsjhd->bshd', p.astype(vg.dtype), vg)
    return o, lse


def _merge(x, outs, lses, z, w_out):
    B, T, _ = x.shape
    w = jax.nn.softmax(jnp.stack(lses, axis=0), axis=0)
    o = jnp.sum(w[..., None] * jnp.stack(outs, axis=0).astype(jnp.float32), axis=0)
    y = (o.reshape(B, T, ATTN_OUT) * jax.nn.silu(z.astype(jnp.float32))).astype(x.dtype) @ w_out
    return x + y


def _dilated_layer_prompt(x, k, v, norm_g, w_in, q_gain, rel_bias, w_out):
    q, z = _b_query(x, norm_g, w_in, q_gain)
    outs, lses = [], []
    for g, (win, dil) in enumerate(GROUPS):
        sl = slice(g * GROUP_HEADS, (g + 1) * GROUP_HEADS)
        o, lse = _group_prompt(q[:, :, sl], k[:, :, sl], v[:, :, sl], rel_bias[:, sl], win, dil)
        outs.append(o)
        lses.append(lse)
    return _merge(x, outs, lses, z, w_out)


def _dilated_layer_sample(x, k, v, bufs, norm_g, w_in, q_gain, rel_bias, w_out):
    q, z = _b_query(x, norm_g, w_in, q_gain)
    outs, lses = [], []
    for g, (win, dil) in enumerate(GROUPS):
        sl = slice(g * GROUP_HEADS, (g + 1) * GROUP_HEADS)
        o, lse = _group_sample(q[:, :, sl], k[:, :, sl], v[:, :, sl], bufs[g], rel_bias[:, sl], win, dil)
        outs.append(o)
        lses.append(lse)
    return _merge(x, outs, lses, z, w_out)


def _window_rows(k, v, g):
    win = GROUPS[g][0]
    sl = slice(g * GROUP_HEADS, (g + 1) * GROUP_HEADS)
    rows = min(win, k.shape[1])
    return jnp.stack([k[:, -rows:, sl], v[:, -rows:, sl]], axis=2)


def _new_rows(k, v, g):
    sl = slice(g * GROUP_HEADS, (g + 1) * GROUP_HEADS)
    return jnp.stack([k[:, :, sl], v[:, :, sl]], axis=2)


def setup_inputs(seed: int = 0) -> dict:
    key = jax.random.key(seed)
    ks = jax.random.split(key, 24)
    f32 = jnp.float32
    nrm = lambda kk, shape, scale=1.0: scale * jax.random.normal(kk, shape, f32)
    H, Dh, DI = MLSTM_HEADS, MLSTM_HEAD_DIM, MLSTM_INNER
    QW = N_Q_HEADS * ATTN_HEAD_DIM
    wl = [min(w, PAST_LEN) for w, _ in GROUPS]
    b_gates = jnp.concatenate([nrm(ks[11], (N_A_LAYERS, H), 0.1),
                               jnp.linspace(3.0, 6.0, H)[None, :] + nrm(ks[12], (N_A_LAYERS, H), 0.1)], axis=-1)
    return {
        'x_prompt': nrm(ks[0], (BATCH, SEQ, D_MODEL)),
        'x_sample': nrm(ks[1], (DEC_BATCH, DEC_SEQ, D_MODEL)),
        'state_mlstm_C': nrm(ks[2], (N_A_LAYERS, DEC_BATCH, H, Dh, Dh), Dh ** -0.5),
        'state_mlstm_n': nrm(ks[3], (N_A_LAYERS, DEC_BATCH, H, Dh), Dh ** -0.5),
        'state_mlstm_m': nrm(ks[4], (N_A_LAYERS, DEC_BATCH, H)),
        'cache_kv_w128': nrm(ks[5], (DEC_BATCH, wl[0], 2, GROUP_HEADS, ATTN_HEAD_DIM)),
        'cache_kv_w512': nrm(ks[6], (DEC_BATCH, wl[1], 2, GROUP_HEADS, ATTN_HEAD_DIM)),
        'cache_kv_w2048': nrm(ks[7], (DEC_BATCH, wl[2], 2, GROUP_HEADS, ATTN_HEAD_DIM)),
        'norm_a': 1.0 + nrm(ks[8], (N_A_LAYERS, D_MODEL), 0.02),
        'w_in_a': nrm(ks[9], (N_A_LAYERS, D_MODEL, 5 * DI + 2 * H), D_MODEL ** -0.5),
        'b_gates_a': b_gates,
        'hnorm_a': 1.0 + nrm(ks[10], (N_A_LAYERS, DI), 0.02),
        'w_out_a': nrm(ks[13], (N_A_LAYERS, DI, D_MODEL), DI ** -0.5),
        'norm_kv': 1.0 + nrm(ks[14], (D_MODEL,), 0.02),
        'w_kv': nrm(ks[15], (D_MODEL, 2 * QW), D_MODEL ** -0.5),
        'k_norm': 1.0 + nrm(ks[16], (ATTN_HEAD_DIM,), 0.02),
        'norm_b': 1.0 + nrm(ks[17], (N_B_LAYERS, D_MODEL), 0.02),
        'w_in_b': nrm(ks[18], (N_B_LAYERS, D_MODEL, QW + ATTN_OUT), D_MODEL ** -0.5),
        'q_norm': 1.0 + nrm(ks[19], (N_B_LAYERS, ATTN_HEAD_DIM), 0.02),
        'rel_bias': nrm(ks[20], (N_BUCKETS, N_Q_HEADS), 0.5),
        'w_out_b': nrm(ks[21], (N_B_LAYERS, ATTN_OUT, D_MODEL), ATTN_OUT ** -0.5),
    }


def reference(x_prompt, x_sample, state_mlstm_C, state_mlstm_n, state_mlstm_m,
              cache_kv_w128, cache_kv_w512, cache_kv_w2048,
              norm_a, w_in_a, b_gates_a, hnorm_a, w_out_a,
              norm_kv, w_kv, k_norm,
              norm_b, w_in_b, q_norm, rel_bias, w_out_b):
    f32 = jnp.float32
    bufs = (cache_kv_w128, cache_kv_w512, cache_kv_w2048)
    n_p = x_prompt.shape[0]
    xp, xs = x_prompt, x_sample
    Cp, Np, Mp, Cs, Ns, Ms = [], [], [], [], [], []
    kp = vp = ks = vs = None
    for layer in range(DEPTH):
        if layer < N_A_LAYERS:
            a = (norm_a[layer], w_in_a[layer], b_gates_a[layer], hnorm_a[layer], w_out_a[layer])
            zero_C = jnp.zeros((n_p, MLSTM_HEADS, MLSTM_HEAD_DIM, MLSTM_HEAD_DIM), f32)
            zero_n = jnp.zeros((n_p, MLSTM_HEADS, MLSTM_HEAD_DIM), f32)
            zero_m = jnp.zeros((n_p, MLSTM_HEADS), f32)
            xp, c, n, m = _mlstm_layer(xp, zero_C, zero_n, zero_m, *a)
            Cp.append(c)
            Np.append(n)
            Mp.append(m)
            xs, c, n, m = _mlstm_layer(xs, state_mlstm_C[layer], state_mlstm_n[layer], state_mlstm_m[layer], *a)
            Cs.append(c)
            Ns.append(n)
            Ms.append(m)
            if layer == N_A_LAYERS - 1:
                kp, vp = _shared_kv(xp, norm_kv, w_kv, k_norm)
                ks, vs = _shared_kv(xs, norm_kv, w_kv, k_norm)
        else:
            i = layer - N_A_LAYERS
            b = (norm_b[i], w_in_b[i], q_norm[i], rel_bias, w_out_b[i])
            xp = _dilated_layer_prompt(xp, kp, vp, *b)
            xs = _dilated_layer_sample(xs, ks, vs, bufs, *b)
    kv128_p, kv512_p, kv2048_p = [_window_rows(kp, vp, g) for g in range(N_GROUPS)]
    kv128_s, kv512_s, kv2048_s = [_new_rows(ks, vs, g) for g in range(N_GROUPS)]
    return (xp, xs, jnp.stack(Cp), jnp.stack(Np), jnp.stack(Mp), jnp.stack(Cs), jnp.stack(Ns), jnp.stack(Ms),
            kv128_p, kv512_p, kv2048_p, kv128_s, kv512_s, kv2048_s)
```

```python
import contextlib
import math
import numpy as np
import concourse.bass as bass
import concourse.mybir as mybir
from concourse.bass_utils import run_bass_kernel_spmd

F32 = mybir.dt.float32
BF16 = mybir.dt.bfloat16
AF = mybir.ActivationFunctionType
ALU = mybir.AluOpType
AX = mybir.AxisListType

D = 1024
DI = 2048
H = 4
DH = 512
NQH = 24
HD = 64
QW = 1536
NS = 4
DS = 8
NST = NS * DS
EPS = 1e-6
GROUPS = ((128, 1), (512, 4), (2048, 16))
NBUCK = 32
MAXDIST = 2048


class Buf:
    __slots__ = ("name", "w", "r", "dsem", "dcnt")

    def __init__(self, name):
        self.name = name
        self.w = []
        self.r = []
        self.dsem = None
        self.dcnt = 0


class Sched:
    ENG = ("pe", "act", "dve", "pool", "sp")

    def __init__(self, nc, stack):
        self.nc = nc
        self.stack = stack
        self.sem = {}
        self.cnt = {}
        self.waited = {}
        for e in self.ENG:
            self.sem[e] = stack.enter_context(nc.semaphore("sem_" + e))
            self.cnt[e] = 0
            self.waited[e] = {}
        self.semeng = {id(self.sem[e]): e for e in self.ENG}
        self.hw = {"pe": nc.tensor, "act": nc.scalar, "dve": nc.vector, "pool": nc.gpsimd, "sp": nc.sync}
        self.dbufs = []
        self.dbufs_all = []
        self.free_sems = []
        self.nb = 0

    def buf(self, name=None):
        self.nb += 1
        return Buf(name or ("b%d" % self.nb))

    def _emit(self, eng, waits, fn, inc):
        engine = self.hw[eng]
        for (s_, v) in waits:
            engine.wait_ge(s_, v)
        if fn is not None:
            ins = fn(engine)
            if inc is not None:
                ins.then_inc(inc[0], inc[1])

    def _waits(self, eng, deps):
        need = {}
        for (s, v) in deps:
            k = id(s)
            if self.semeng.get(k) == eng and eng in ("pe", "sp"):
                continue
            if self.waited[eng].get(k, 0) >= v:
                continue
            if k not in need or need[k][1] < v:
                need[k] = (s, v)
        out = []
        for k, (s, v) in need.items():
            self.waited[eng][k] = v
            out.append((s, v))
        return out

    def _deps(self, reads, writes):
        deps = []
        for b in reads:
            deps += b.w
        for b in writes:
            deps += b.w
            deps += b.r
        return deps

    def op(self, eng, fn, reads=(), writes=()):
        waits = self._waits(eng, self._deps(reads, writes))
        self.cnt[eng] += 1
        tok = (self.sem[eng], self.cnt[eng])
        self._emit(eng, waits, fn, (self.sem[eng], 1))
        for b in reads:
            b.r.append(tok)
        for b in writes:
            b.w = [tok]
            b.r = []
        return tok

    def dma(self, eng, pairs, owner, reads=(), writes=(), **kw):
        if owner.dsem is None:
            if self.free_sems:
                owner.dsem, owner.dcnt = self.free_sems.pop()
            else:
                owner.dsem = self.stack.enter_context(self.nc.semaphore("dsem_%d" % len(self.dbufs_all)))
                owner.dcnt = 0
            self.dbufs.append(owner)
            self.dbufs_all.append(owner)
        deps = self._deps(reads, writes)
        if owner.dcnt:
            deps.append((owner.dsem, owner.dcnt))
        waits = self._waits(eng, deps)
        for i, (o, i_) in enumerate(pairs):
            def fn(e, o=o, i_=i_):
                return e.dma_start(out=o, in_=i_, **kw)
            self._emit(eng, waits if i == 0 else [], fn, (owner.dsem, 16))
        owner.dcnt += 16 * len(pairs)
        tok = (owner.dsem, owner.dcnt)
        for b in reads:
            b.r.append(tok)
        for b in writes:
            b.w = [tok]
            b.r = []
        return tok

    def barrier(self):
        toks = [(self.sem[e], self.cnt[e]) for e in self.ENG if self.cnt[e]]
        for b in self.dbufs:
            if b.dcnt:
                toks.append((b.dsem, b.dcnt))
        for e in self.ENG:
            w = self._waits(e, [t for t in toks if self.semeng.get(id(t[0])) != e])
            if w:
                self._emit(e, w, None, None)
        for b in self.dbufs:
            self.free_sems.append((b.dsem, b.dcnt))
            b.dsem = None
            b.dcnt = 0
        self.dbufs = []

    def finish(self):
        toks = [(self.sem[e], self.cnt[e]) for e in self.ENG if self.cnt[e] and e != "sp"]
        for b in self.dbufs:
            if b.dcnt:
                toks.append((b.dsem, b.dcnt))
        w = self._waits("sp", toks)
        self._emit("sp", w, None, None)


def _t5_bucket_np(dist):
    exact = NBUCK // 2
    d = np.maximum(dist, 1).astype(np.float32)
    large = exact + (np.log(d / np.float32(exact)) / np.float32(math.log(MAXDIST / exact))
                     * np.float32(NBUCK - exact)).astype(np.int32)
    return np.where(dist < exact, dist, np.minimum(large, NBUCK - 1))


def make_consts(T):
    NC = T // 128
    TT = T + NST
    NCH = NC + NS
    c = {}
    c["c_ident"] = np.eye(128, dtype=np.float32)
    c["c_J"] = np.eye(128, dtype=np.float32)[::-1].copy()
    s_ = np.arange(128)[:, None]
    t_ = np.arange(128)[None, :]
    c["c_maskT"] = (s_ <= t_).astype(np.float32)
    rm = np.ones((4, TT), np.float32)
    rm[:, 0:T:128] = 0.0
    rm[:, T:TT:DS] = 0.0
    c["c_rm"] = rm
    dm = np.zeros((4, 4, NCH), np.float32)
    for h in range(4):
        dm[h, h, :] = 1.0
    c["c_dmask"] = dm.reshape(4, 4 * NCH)
    ohp = np.zeros((NBUCK, 3, 384), np.float32)
    for g, (win, dil) in enumerate(GROUPS):
        J = win // dil
        for j in range(384):
            delta = j - 127
            if 0 <= delta <= J:
                b = int(_t5_bucket_np(np.array([delta * dil]))[0])
                ohp[b, g, j] = 1.0
    c["c_ohp"] = ohp.reshape(NBUCK, 3 * 384)
    ohs = np.zeros((NBUCK, 13, 8, 128), np.float32)
    ohn = np.zeros((NBUCK, 3, 8, 8), np.float32)
    gr = 0
    for g, (win, dil) in enumerate(GROUPS):
        L = win
        J = win // dil
        for r in range(dil if g < 2 else 8):
            for s in range(8):
                if s % dil != r % dil:
                    continue
                if g == 2 and s != r:
                    continue
                for i in range(128):
                    row = r + dil * i
                    num = L + s - row
                    if num < 0 or num % dil:
                        continue
                    j = num // dil
                    if 0 <= j <= J:
                        b = int(_t5_bucket_np(np.array([dil * j]))[0])
                        ohs[b, gr, s, i] = 1.0
            gr += 1
        for s in range(8):
            for k in range(8):
                num = s - k
                if num < 0 or num % dil:
                    continue
                j = num // dil
                if j <= J:
                    b = int(_t5_bucket_np(np.array([dil * j]))[0])
                    ohn[b, g, s, k] = 1.0
    assert gr == 13
    c["c_ohs"] = ohs.reshape(NBUCK, 13 * 8 * 128)
    c["c_ohn"] = ohn.reshape(NBUCK, 3 * 8 * 8)
    return c


CLASSES = [(0, 0)] + [(1, r) for r in range(4)] + [(2, r) for r in range(8)]


def build(T, phases="ABCD", dbg=()):
    NT = T // 128
    NC = NT
    TT = T + NST
    NCH = NC + NS
    NTT = NT + 1
    nc = bass.Bass("TRN2", target_bir_lowering=False)

    def din(name, shape, dt=F32):
        return nc.dram_tensor(name, list(shape), dt, kind="ExternalInput").ap()

    def dout(name, shape, dt=F32):
        return nc.dram_tensor(name, list(shape), dt, kind="ExternalOutput").ap()

    def dscr(name, shape, dt):
        return nc.dram_tensor(name, list(shape), dt, kind="Internal").ap()

    xp = din("xp", [T, D])
    xs = din("xs", [NST, D])
    sC = din("sC", [NS, H, DH, DH])
    sn = din("sn", [NS, H, DH])
    sm = din("sm", [NS, H])
    caches = [din("c128", [NS, 128, 2, 8, HD]), din("c512", [NS, 512, 2, 8, HD]), din("c2048", [NS, 2048, 2, 8, HD])]
    norm_a = din("norm_a", [1, D])
    w_in_a = din("w_in_a", [D, 5 * DI + 2 * H])
    b_gates = din("b_gates", [1, 2 * H])
    hnorm = din("hnorm", [DI])
    w_out_a = din("w_out_a", [DI, D])
    norm_kv = din("norm_kv", [D])
    w_kv = din("w_kv", [D, 2 * QW])
    k_norm = din("k_norm", [1, HD])
    norm_b = din("norm_b", [D])
    w_in_b = din("w_in_b", [D, QW + 512])
    q_norm = din("q_norm", [1, HD])
    rel_bias = din("rel_bias", [NBUCK, NQH])
    w_out_b = din("w_out_b", [512, D])
    c_ident = din("c_ident", [128, 128])
    c_J = din("c_J", [128, 128])
    c_maskT = din("c_maskT", [128, 128])
    c_rm = din("c_rm", [4, TT])
    c_dmask = din("c_dmask", [4, 4 * NCH])
    c_ohp = din("c_ohp", [NBUCK, 3 * 384])
    c_ohs = din("c_ohs", [NBUCK, 13 * 8 * 128])
    c_ohn = din("c_ohn", [NBUCK, 3 * 8 * 8])

    y_p = dout("y_p", [T, D])
    y_s = dout("y_s", [NST, D])
    C_p = dout("C_p", [H, DH, DH])
    n_p = dout("n_p", [H, DH])
    m_p = dout("m_p", [H, 1])
    C_s = dout("C_s", [NS, H, DH, DH])
    n_s = dout("n_s", [NS, H, DH])
    m_s = dout("m_s", [NS, H])
    kvp = [dout("kv128_p", [min(128, T), 2, 8, HD]), dout("kv512_p", [min(512, T), 2, 8, HD]),
           dout("kv2048_p", [min(2048, T), 2, 8, HD])]
    kvs = [dout("kv128_s", [NST, 2, 8, HD]), dout("kv512_s", [NST, 2, 8, HD]), dout("kv2048_s", [NST, 2, 8, HD])]

    hf = dscr("hf_scr", [TT, DI], BF16)
    x1s = dscr("x1_scr", [TT, D], F32)
    KTs = dscr("KT_scr", [QW, TT], BF16)
    QTs = dscr("QT_scr", [QW, TT], BF16)
    QTse = dscr("QTe_scr", [QW, TT], BF16)
    QTso = dscr("QTo_scr", [QW, TT], BF16)
    Vs = dscr("V_scr", [TT, NQH, HD + 1], BF16)
    zss = dscr("zs_scr", [TT, 512], BF16)
    osc = dscr("o_scr", [3, TT, 8 * (HD + 1)], F32)
    vecs = dscr("vec_scr", [NQH, 384], F32)

    dbg_out = {}

    with contextlib.ExitStack() as top:
        S = Sched(nc, top)

        def sbt(stack, name, shape, dt=F32):
            return stack.enter_context(nc.sbuf_tensor(name, list(shape), dt))

        psall = top.enter_context(nc.psum_tensor("psall", [128, 8 * 512], F32))
        pbanks = []
        for i in range(8):
            pbanks.append((psall[:, i * 512:(i + 1) * 512], S.buf("pb%d" % i)))
        rr = [0]

        def bank():
            t, b = pbanks[rr[0] % 7]
            rr[0] += 1
            return t, b
        psm = pbanks[7][0]
        B_psS = B_psD = B_psn = pbanks[7][1]

        identf = sbt(top, "identf", [128, 128])
        identb = sbt(top, "identb", [128, 128], BF16)
        cm05 = sbt(top, "cm05", [128, 1])
        B_const = S.buf("const")
        S.dma("sp", [(identf[:], c_ident[:, :])], B_const, writes=[B_const])
        B_identb = S.buf("identb")
        S.op("dve", lambda e: e.tensor_copy(out=identb[:], in_=identf[:]), reads=[B_const], writes=[B_identb])
        B_cm05 = S.buf("cm05")
        S.op("pool", lambda e: e.memset(cm05[:], -0.5), writes=[B_cm05])
        ZW = 516
        assert TT % ZW == 0 or True
        zt_ = sbt(top, "zeroT", [64, ZW], BF16)
        B_zt_ = S.buf("zeroT")
        S.op("pool", lambda e: e.memset(zt_[:], 0.0), writes=[B_zt_])
        nz = TT // ZW
        rem = TT - nz * ZW
        zp = []
        for (dst_, lo) in ((QTse, 64), (QTso, 0)):
            v_ = dst_.rearrange("(rg p) t -> p rg t", p=128)[lo:lo + 64, :, :]
            for rg_ in range(12):
                if nz:
                    zp.append((v_[:, rg_, 0:nz * ZW].rearrange("p (a b) -> p a b", b=ZW),
                               zt_[:].unsqueeze(1).to_broadcast([64, nz, ZW])))
            if rem:
                zp.append((v_[:, :, nz * ZW:TT], zt_[:, 0:rem].unsqueeze(1).to_broadcast([64, 12, rem])))
        zero_fill = [lambda: S.dma("pool", zp, B_zt_, reads=[B_zt_])]
        if "A" not in phases:
            zero_fill.pop()()

        def dump(name, ap_sb, shape, owner, dt=F32):
            o = dout("dbg_" + name, shape, dt)
            dbg_out[name] = o
            S.dma("sp", [(o, ap_sb)], owner, reads=[owner])

        def rsqrt_small(rows, out_ap, in_ap, scale, add, B_in, B_out, tmp_ap, B_tmp):
            S.op("dve", lambda e: e.tensor_scalar(out=tmp_ap, in0=in_ap, scalar1=scale, scalar2=add,
                                                  op0=ALU.mult, op1=ALU.add), reads=[B_in], writes=[B_tmp])
            S.op("pool", lambda e: e.tensor_tensor(out=out_ap, in0=tmp_ap, in1=cm05[:rows], op=ALU.pow),
                 reads=[B_tmp, B_cm05], writes=[B_out])

        if "A" in phases:
            with contextlib.ExitStack() as pa:
                xnT = sbt(pa, "xnT", [128, 8, TT], BF16)
                B_xnT = S.buf("xnT")
                uf_tok = sbt(pa, "uf_tok", [128, NTT * 8])
                ub_tok = sbt(pa, "ub_tok", [128, NTT * 8], BF16)
                ufs = sbt(pa, "ufs", [8, NS * 8])
                ubs = sbt(pa, "ubs", [8, NS * 8], BF16)
                a_bc = sbt(pa, "a_bc", [128, 4 * NCH])
                B_uf = S.buf("uf")
                B_abc = S.buf("abc")
                maskT = sbt(pa, "maskT", [128, 128])
                B_maskT = S.buf("maskT")
                S.dma("sp", [(maskT[:], c_maskT[:, :])], B_maskT, writes=[B_maskT])

                with contextlib.ExitStack() as p0:
                    g_bc = sbt(p0, "g_bc", [128, D])
                    B_gbc = S.buf("gbc")
                    S.dma("sp", [(g_bc[:], norm_a[0:1, :].to_broadcast([128, D]))], B_gbc, writes=[B_gbc])
                    xt = [sbt(p0, "xt%d" % i, [128, D]) for i in range(3)]
                    B_xt = [S.buf() for _ in range(3)]
                    xn = [sbt(p0, "xn%d" % i, [128, D], BF16) for i in range(2)]
                    B_xn = [S.buf() for _ in range(2)]
                    junk = sbt(p0, "junk0", [128, D], BF16)
                    B_junk = S.buf()
                    smt = sbt(p0, "smt0", [128, 9])
                    B_ss = [S.buf() for _ in range(3)]
                    B_tt = [S.buf() for _ in range(3)]
                    B_rs = [S.buf() for _ in range(3)]
                    a0bank = {}

                    def a0_ld(i):
                        rows = 128 if i < NT else NST
                        src = xp[i * 128:(i + 1) * 128, :] if i < NT else xs[:, :]
                        s3 = i % 3
                        S.dma("sp", [(xt[s3][:rows], src)], B_xt[s3], writes=[B_xt[s3]])

                    def a0_s0(i):
                        rows = 128 if i < NT else NST
                        s3 = i % 3
                        S.op("act", lambda e: e.activation(
                            out=junk[:rows], in_=xt[s3][:rows], func=AF.Square, accum_out=smt[:rows, s3:s3 + 1]),
                            reads=[B_xt[s3]], writes=[B_junk, B_ss[s3]])
                        rsqrt_small(rows, smt[:rows, 6 + s3:7 + s3], smt[:rows, s3:s3 + 1], 1.0 / D, EPS,
                                    B_ss[s3], B_rs[s3], smt[:rows, 3 + s3:4 + s3], B_tt[s3])

                    def a0_s1(i):
                        rows = 128 if i < NT else NST
                        s3, s2 = i % 3, i % 2
                        S.op("dve", lambda e: e.scalar_tensor_tensor(
                            out=xn[s2][:rows], in0=xt[s3][:rows], scalar=smt[:rows, 6 + s3:7 + s3], in1=g_bc[:rows],
                            op0=ALU.mult, op1=ALU.mult), reads=[B_xt[s3], B_rs[s3], B_gbc], writes=[B_xn[s2]])
                        pb, Bpb = bank()
                        pbb = pb[:].bitcast(BF16)
                        a0bank[i] = (pbb, Bpb)

                        def tr(e):
                            ins = None
                            for kc in range(8):
                                ins = e.transpose(out=pbb[:, kc * 128:kc * 128 + rows],
                                                  in_=xn[s2][:rows, kc * 128:(kc + 1) * 128],
                                                  identity=identb[:rows, :rows])
                            return ins
                        S.op("pe", tr, reads=[B_xn[s2], B_identb], writes=[Bpb])

                    def a0_s2(i):
                        rows = 128 if i < NT else NST
                        c0 = i * 128
                        pbb, Bpb = a0bank.pop(i)
                        S.op("act", lambda e: e.copy(
                            out=xnT[:, :, c0:c0 + rows],
                            in_=pbb.rearrange("p (k t) -> p k t", t=128)[:, :, :rows]),
                            reads=[Bpb], writes=[B_xnT])
                    a0st = [a0_ld, a0_s0, a0_s1, a0_s2]
                    for step in range(NTT + len(a0st) - 1):
                        for st in range(len(a0st) - 1, -1, -1):
                            i = step - st
                            if 0 <= i < NTT:
                                a0st[st](i)

                    Wg = sbt(p0, "Wg", [128, 8, 8], BF16)
                    B_Wg = S.buf("Wg")
                    S.dma("pool", [(Wg[:], w_in_a[:, 5 * DI:5 * DI + 8].rearrange("(k p) c -> p k c", p=128))],
                          B_Wg, writes=[B_Wg])
                    bgt = sbt(p0, "bgt", [4, 2])
                    B_bg = S.buf("bg")
                    S.dma("sp", [(bgt[:, 0:1], b_gates[0:1, 0:4].rearrange("o c -> c o")),
                                 (bgt[:, 1:2], b_gates[0:1, 4:8].rearrange("o c -> c o"))], B_bg, writes=[B_bg])
                    GA = sbt(p0, "GA", [4, TT])
                    GB = sbt(p0, "GB", [4, TT])
                    GC = sbt(p0, "GC", [4, TT])
                    rm = sbt(p0, "rm", [4, TT])
                    B_GA, B_GB, B_GC, B_rm = S.buf("GA"), S.buf("GB"), S.buf("GC"), S.buf("rm")
                    S.dma("sp", [(rm[:], c_rm[:, :])], B_rm, writes=[B_rm])
                    dmk = sbt(p0, "dmk", [4, 4 * NCH])
                    B_dmk = S.buf("dmk")
                    S.dma("sp", [(dmk[:], c_dmask[:, :])], B_dmk, writes=[B_dmk])
                    col = 0
                    while col < TT:
                        w = min(512, TT - col)
                        for which, dst, Bd in ((0, GA, B_GA), (1, GB, B_GB)):
                            pb, Bpb = bank()

                            def gm(e, which=which, col=col, w=w, pb=pb):
                                ins = None
                                for kc in range(8):
                                    ins = e.matmul(pb[0:4, 0:w], lhsT=Wg[:, kc, which * 4:which * 4 + 4],
                                                   rhs=xnT[:, kc, col:col + w], start=(kc == 0), stop=(kc == 7))
                                return ins
                            S.op("pe", gm, reads=[B_Wg, B_xnT], writes=[Bpb])
                            S.op("act", lambda e, which=which, col=col, w=w, pb=pb, dst=dst: e.activation(
                                out=dst[:, col:col + w], in_=pb[0:4, 0:w], func=AF.Identity,
                                bias=bgt[:, which:which + 1]), reads=[Bpb, B_bg], writes=[Bd])
                        col += w
                    S.op("act", lambda e: e.activation(out=GB[:], in_=GB[:], func=AF.Exp, scale=-1.0),
                         reads=[B_GB], writes=[B_GB])
                    S.op("act", lambda e: e.activation(out=GB[:], in_=GB[:], func=AF.Ln, bias=1.0),
                         reads=[B_GB], writes=[B_GB])
                    S.op("dve", lambda e: e.tensor_tensor_scan(out=GC[:], data0=rm[:], data1=GB[:], initial=0.0,
                                                               op0=ALU.mult, op1=ALU.add),
                         reads=[B_rm, B_GB], writes=[B_GC])
                    S.op("dve", lambda e: e.tensor_tensor(out=GA[:], in0=GA[:], in1=GC[:], op=ALU.add),
                         reads=[B_GA, B_GC], writes=[B_GA])
                    gsm = sbt(p0, "gsm", [4, 8 * NCH + 8])
                    B_gsm = S.buf("gsm")
                    Acol = gsm[:, 0:NCH]
                    bLc = gsm[:, NCH:2 * NCH]
                    mcol = gsm[:, 2 * NCH:3 * NCH]
                    Mcol = gsm[:, 3 * NCH:4 * NCH]
                    mpr = gsm[:, 4 * NCH:5 * NCH]
                    acol = gsm[:, 5 * NCH:6 * NCH]
                    m0T = gsm[:, 6 * NCH:6 * NCH + NS]
                    S.dma("sp", [(m0T, sm.rearrange("s h -> h s"))], B_gsm, writes=[B_gsm],
                          allow_slow_non_contiguous=True)
                    S.op("dve", lambda e: e.tensor_reduce(out=gsm[:, 0:NC], in_=GA[:, 0:T].rearrange("p (c k) -> p c k", k=128),
                                                          axis=AX.X, op=ALU.max), reads=[B_GA], writes=[B_gsm])
                    S.op("dve", lambda e: e.tensor_reduce(out=gsm[:, NC:NCH], in_=GA[:, T:TT].rearrange("p (c k) -> p c k", k=DS),
                                                          axis=AX.X, op=ALU.max), reads=[B_GA], writes=[B_gsm])
                    S.op("dve", lambda e: e.tensor_copy(out=gsm[:, NCH:NCH + NC],
                                                        in_=GC[:, 0:T].rearrange("p (c k) -> p c k", k=128)[:, :, 127]),
                         reads=[B_GC], writes=[B_gsm])
                    S.op("dve", lambda e: e.tensor_copy(out=gsm[:, NCH + NC:2 * NCH],
                                                        in_=GC[:, T:TT].rearrange("p (c k) -> p c k", k=DS)[:, :, DS - 1]),
                         reads=[B_GC], writes=[B_gsm])
                    S.op("dve", lambda e: e.tensor_tensor_scan(out=gsm[:, 2 * NCH:2 * NCH + NC], data0=gsm[:, 0:NC],
                                                               data1=gsm[:, NCH:NCH + NC], initial=0.0,
                                                               op0=ALU.max, op1=ALU.subtract),
                         reads=[B_gsm], writes=[B_gsm])
                    S.op("dve", lambda e: e.memset(gsm[:, 4 * NCH:4 * NCH + 1], 0.0), reads=[B_gsm], writes=[B_gsm])
                    if NC > 1:
                        S.op("dve", lambda e: e.tensor_copy(out=gsm[:, 4 * NCH + 1:4 * NCH + NC],
                                                            in_=gsm[:, 2 * NCH:2 * NCH + NC - 1]),
                             reads=[B_gsm], writes=[B_gsm])
                    S.op("dve", lambda e: e.tensor_copy(out=gsm[:, 4 * NCH + NC:5 * NCH], in_=m0T),
                         reads=[B_gsm], writes=[B_gsm])
                    S.op("dve", lambda e: e.tensor_tensor(out=Mcol, in0=mpr, in1=Acol, op=ALU.max),
                         reads=[B_gsm], writes=[B_gsm])
                    S.op("dve", lambda e: e.tensor_tensor(out=gsm[:, 2 * NCH + NC:3 * NCH], in0=gsm[:, 3 * NCH + NC:4 * NCH],
                                                          in1=gsm[:, NCH + NC:2 * NCH], op=ALU.subtract),
                         reads=[B_gsm], writes=[B_gsm])
                    S.op("dve", lambda e: e.tensor_tensor(out=acol, in0=mpr, in1=Mcol, op=ALU.subtract),
                         reads=[B_gsm], writes=[B_gsm])
                    S.op("act", lambda e: e.activation(out=acol, in_=acol, func=AF.Exp), reads=[B_gsm], writes=[B_gsm])
                    S.dma("sp", [(m_p[:, :], gsm[:, 2 * NCH + NC - 1:2 * NCH + NC])], B_gsm, reads=[B_gsm])
                    S.dma("sp", [(m_s.rearrange("s h -> h s"), gsm[:, 2 * NCH + NC:3 * NCH])], B_gsm, reads=[B_gsm],
                          allow_slow_non_contiguous=True)
                    for (dst, Bd, srcg, Bs) in ((GB, B_GB, GA, B_GA), (GC, B_GC, GC, B_GC)):
                        S.op("dve", lambda e, dst=dst, srcg=srcg: e.tensor_tensor(
                            out=dst[:, 0:T].rearrange("p (c k) -> p c k", k=128),
                            in0=srcg[:, 0:T].rearrange("p (c k) -> p c k", k=128),
                            in1=gsm[:, 3 * NCH:3 * NCH + NC].unsqueeze(2).to_broadcast([4, NC, 128]), op=ALU.subtract),
                            reads=[Bs, B_gsm], writes=[Bd])
                        S.op("dve", lambda e, dst=dst, srcg=srcg: e.tensor_tensor(
                            out=dst[:, T:TT].rearrange("p (c k) -> p c k", k=DS),
                            in0=srcg[:, T:TT].rearrange("p (c k) -> p c k", k=DS),
                            in1=gsm[:, 3 * NCH + NC:4 * NCH].unsqueeze(2).to_broadcast([4, NS, DS]), op=ALU.subtract),
                            reads=[Bs, B_gsm], writes=[Bd])
                        S.op("act", lambda e, dst=dst: e.activation(out=dst[:], in_=dst[:], func=AF.Exp),
                             reads=[Bd], writes=[Bd])
                    pb, Bpb = bank()

                    def trg(e, pb=pb):
                        ins = None
                        for i in range(NTT):
                            rows = 128 if i < NT else NST
                            for k, srcg in enumerate((GB, GC)):
                                ins = e.matmul(pb[:rows, i * 8 + 4 * k:i * 8 + 4 * k + 4],
                                               lhsT=srcg[:, i * 128:i * 128 + rows], rhs=identf[0:4, 0:4],
                                               start=True, stop=True)
                        return ins
                    S.op("pe", trg, reads=[B_GB, B_GC, B_const], writes=[Bpb])
                    S.op("dve", lambda e, pb=pb: e.tensor_copy(out=uf_tok[:], in_=pb[:, 0:NTT * 8]), reads=[Bpb], writes=[B_uf])
                    S.op("dve", lambda e: e.tensor_copy(out=ub_tok[:], in_=uf_tok[:]), reads=[B_uf], writes=[B_uf])
                    pb, Bpb = bank()

                    def trs(e, pb=pb):
                        ins = None
                        for j in range(NS):
                            for k, srcg in enumerate((GB, GC)):
                                ins = e.matmul(pb[0:DS, j * 8 + 4 * k:j * 8 + 4 * k + 4],
                                               lhsT=srcg[:, T + j * DS:T + (j + 1) * DS], rhs=identf[0:4, 0:4],
                                               start=True, stop=True)
                        return ins
                    S.op("pe", trs, reads=[B_GB, B_GC, B_const], writes=[Bpb])
                    S.op("dve", lambda e, pb=pb: e.tensor_copy(out=ufs[:], in_=pb[0:DS, 0:NS * 8]), reads=[Bpb], writes=[B_uf])
                    S.op("dve", lambda e: e.tensor_copy(out=ubs[:], in_=ufs[:]), reads=[B_uf], writes=[B_uf])
                    adg = sbt(p0, "adg", [4, 4 * NCH])
                    ones4 = sbt(p0, "ones4", [4, 128])
                    B_adg = S.buf("adg")
                    S.op("dve", lambda e: e.memset(ones4[:], 1.0), writes=[B_adg])
                    S.op("dve", lambda e: e.tensor_tensor(out=adg[:].rearrange("p (a c) -> p a c", a=4),
                                                          in0=acol.unsqueeze(1).to_broadcast([4, 4, NCH]),
                                                          in1=dmk[:].rearrange("p (a c) -> p a c", a=4), op=ALU.mult),
                         reads=[B_gsm, B_dmk, B_adg], writes=[B_adg])
                    pb, Bpb = bank()
                    S.op("pe", lambda e, pb=pb: e.matmul(pb[:, 0:4 * NCH], lhsT=ones4[:, :], rhs=adg[:, :], start=True, stop=True),
                         reads=[B_adg], writes=[Bpb])
                    S.op("dve", lambda e, pb=pb: e.tensor_copy(out=a_bc[:], in_=pb[:, 0:4 * NCH]), reads=[Bpb], writes=[B_abc])
                    if "uf" in dbg:
                        dump("uf", uf_tok[:], [128, NTT * 8], B_uf)
                        dump("abc", a_bc[:], [128, 4 * NCH], B_abc)
                        dump("ufs", ufs[:], [8, NS * 8], B_uf)
                    S.barrier()

                with contextlib.ExitStack() as p1:
                    W5 = [sbt(p1, "W5_%d" % i, [128, 8, 5, 512], BF16) for i in range(2)]
                    B_W = [[S.buf("W5_%d_%d" % (i, j)) for j in range(5)] for i in range(2)]
                    qT = [sbt(p1, "qT%d" % i, [128, 4, 512], BF16) for i in range(2)]
                    kT = [sbt(p1, "kT%d" % i, [128, 4, 512], BF16) for i in range(2)]
                    B_qT = [S.buf() for _ in range(2)]
                    B_kT = [S.buf() for _ in range(2)]
                    Cst = sbt(p1, "Cst", [128, 4, 512])
                    Css = sbt(p1, "Css", [128, 4, 512])
                    B_C = [S.buf() for _ in range(4)]
                    B_Cs = [S.buf() for _ in range(4)]
                    nst = sbt(p1, "nst", [128, 4])
                    nss = sbt(p1, "nss", [128, 4])
                    B_n = S.buf("n")
                    B_ns = S.buf("ns")
                    Cbf = sbt(p1, "Cbf", [128, 4, 512], BF16)
                    nbf = [sbt(p1, "nbf%d" % i, [128, 4], BF16) for i in range(2)]
                    B_Cbf = S.buf("Cbf")
                    B_nbf = [S.buf() for _ in range(2)]
                    vaug = [sbt(p1, "vaug%d" % i, [128, 512], BF16) for i in range(2)]
                    so = [sbt(p1, "so%d" % i, [128, 512], BF16) for i in range(2)]
                    sz = [sbt(p1, "sz%d" % i, [128, 512], BF16) for i in range(2)]
                    G1 = [sbt(p1, "G1%d" % i, [128, 512], BF16) for i in range(2)]
                    G = [sbt(p1, "G%d" % i, [128, 512], BF16) for i in range(2)]
                    ktok = [sbt(p1, "ktok%d" % i, [128, 512], BF16) for i in range(2)]
                    SpT = [sbt(p1, "SpT%d" % i, [128, 128], BF16) for i in range(2)]
                    hfin = [sbt(p1, "hfin%d" % i, [128, 512], BF16) for i in range(2)]
                    sml = [sbt(p1, "sml%d" % i, [128, 8]) for i in range(2)]
                    junk1 = sbt(p1, "junk1", [128, 512], BF16)
                    B_junk1 = S.buf()
                    B_vaug = [S.buf() for _ in range(2)]
                    B_so = [S.buf() for _ in range(2)]
                    B_sz = [S.buf() for _ in range(2)]
                    B_G1 = [S.buf() for _ in range(2)]
                    B_G = [S.buf() for _ in range(2)]
                    B_ktok = [S.buf() for _ in range(2)]
                    B_SpT = [S.buf() for _ in range(2)]
                    B_hfin = [S.buf("hfin%d" % i) for i in range(2)]
                    B_sml = [[S.buf() for _ in range(8)] for _ in range(2)]
                    cnt = [0]

                    def chunk(h, L, xcols, qt, kt, Bq, Bk, qcols, ucol, flcol, ubcol, acolp, Ct, BCs, nt, Bn, hf_rows, W5c, B_Wc):
                        s = cnt[0] % 2
                        cnt[0] += 1
                        pb, Bpb = bank()
                        pbb = pb[:].bitcast(BF16)

                        def ktr(e, pbb=pbb):
                            ins = None
                            for dc in range(4):
                                ins = e.transpose(out=pbb[:L, dc * 128:(dc + 1) * 128], in_=kt[:, dc, qcols],
                                                  identity=identb[:, :])
                            return ins
                        S.op("pe", ktr, reads=[Bk, B_identb], writes=[Bpb])
                        S.op("act", lambda e, pbb=pbb: e.copy(out=ktok[s][:L], in_=pbb[:L, 0:512]), reads=[Bpb], writes=[B_ktok[s]])

                        def smm(e):
                            ins = None
                            for dc in range(4):
                                ins = e.matmul(psm[:L, 0:L], lhsT=kt[:, dc, qcols], rhs=qt[:, dc, qcols],
                                               start=(dc == 0), stop=(dc == 3))
                            return ins
                        S.op("pe", smm, reads=[Bq, Bk], writes=[B_psS])
                        S.op("dve", lambda e: e.tensor_tensor(out=SpT[s][:L, :L], in0=psm[:L, 0:L], in1=maskT[:L, :L], op=ALU.mult),
                             reads=[B_psS, B_maskT], writes=[B_SpT[s]])
                        S.op("act", lambda e: e.activation(out=Cbf[:].rearrange("p a b -> p (a b)"),
                                                           in_=Ct[:].rearrange("p a b -> p (a b)"), func=AF.Copy, scale=acolp),
                             reads=list(BCs) + [B_abc], writes=[B_Cbf])
                        S.op("act", lambda e: e.activation(out=nbf[s][:], in_=nt[:], func=AF.Copy, scale=acolp),
                             reads=[Bn, B_abc], writes=[B_nbf[s]])
                        pv = []
                        for j in (2, 3, 4):
                            pb, Bpb = bank()

                            def pj(e, j=j, pb=pb):
                                ins = None
                                for kc in range(8):
                                    ins = e.matmul(pb[:L, :], lhsT=xnT[:, kc, xcols], rhs=W5c[:, kc, j, :],
                                                   start=(kc == 0), stop=(kc == 7))
                                return ins
                            S.op("pe", pj, reads=[B_xnT, B_Wc[j]], writes=[Bpb])
                            pv.append((pb, Bpb))
                            if j == 2:
                                pvv, Bpv = pb, Bpb
                                S.op("dve", lambda e: e.tensor_scalar(out=vaug[s][:L], in0=pvv[:L, :], scalar1=ucol, scalar2=None,
                                                                      op0=ALU.mult), reads=[Bpv, B_uf], writes=[B_vaug[s]])
                            elif j == 3:
                                po, Bpo = pb, Bpb
                                S.op("act", lambda e: e.activation(out=so[s][:L], in_=po[:L, :], func=AF.Sigmoid),
                                     reads=[Bpo], writes=[B_so[s]])
                            else:
                                pz, Bpz = pb, Bpb
                                S.op("act", lambda e: e.activation(out=sz[s][:L], in_=pz[:L, :], func=AF.Sigmoid),
                                     reads=[Bpz], writes=[B_sz[s]])
                                S.op("dve", lambda e: e.tensor_tensor(out=G1[s][:L], in0=pz[:L, :], in1=sz[s][:L], op=ALU.mult),
                                     reads=[Bpz, B_sz[s]], writes=[B_G1[s]])
                                S.op("pool", lambda e: e.tensor_tensor(out=G[s][:L], in0=G1[s][:L], in1=so[s][:L], op=ALU.mult),
                                     reads=[B_G1[s], B_so[s]], writes=[B_G[s]])
                        for dc in range(4):
                            pC, BpC = bank()
                            S.op("pe", lambda e, dc=dc, pC=pC: e.matmul(pC[:, :], lhsT=ktok[s][:L, dc * 128:(dc + 1) * 128],
                                                                        rhs=vaug[s][:L], start=True, stop=True),
                                 reads=[B_ktok[s], B_vaug[s]], writes=[BpC])
                            S.op("dve", lambda e, dc=dc, pC=pC: e.scalar_tensor_tensor(
                                out=Ct[:, dc, :], in0=Ct[:, dc, :], scalar=acolp, in1=pC[:, :], op0=ALU.mult, op1=ALU.add),
                                reads=[BCs[dc], BpC, B_abc], writes=[BCs[dc]])
                        pN, BpN = bank()

                        def nmm(e, pN=pN):
                            e.matmul(pN[:L, :], lhsT=SpT[s][:L, :L], rhs=vaug[s][:L], start=True, stop=False)
                            ins = None
                            for dc in range(4):
                                ins = e.matmul(pN[:L, :], lhsT=qt[:, dc, qcols], rhs=Cbf[:, dc, :],
                                               start=False, stop=(dc == 3))
                            return ins
                        S.op("pe", nmm, reads=[B_SpT[s], B_vaug[s], Bq, B_Cbf], writes=[BpN])

                        def dmm(e):
                            e.matmul(psm[:L, 128:129], lhsT=SpT[s][:L, :L], rhs=ubcol, start=True, stop=False)
                            ins = None
                            for dc in range(4):
                                ins = e.matmul(psm[:L, 128:129], lhsT=qt[:, dc, qcols], rhs=nbf[s][:, dc:dc + 1],
                                               start=False, stop=(dc == 3))
                            for dc in range(4):
                                ins = e.matmul(psm[:, 132 + dc:133 + dc], lhsT=ktok[s][:L, dc * 128:(dc + 1) * 128],
                                               rhs=ubcol, start=True, stop=True)
                            return ins
                        S.op("pe", dmm, reads=[B_SpT[s], B_uf, Bq, B_nbf[s], B_ktok[s]], writes=[B_psD])
                        S.op("dve", lambda e: e.scalar_tensor_tensor(out=nt[:], in0=nt[:], scalar=acolp, in1=psm[:, 132:136],
                                                                     op0=ALU.mult, op1=ALU.add),
                             reads=[Bn, B_psn, B_abc], writes=[Bn])
                        bs = B_sml[s]
                        sm_ = sml[s]
                        S.op("act", lambda e, pN=pN: e.activation(out=junk1[:L], in_=pN[:L, :], func=AF.Square,
                                                                   accum_out=sm_[:L, 0:1]),
                             reads=[BpN], writes=[B_junk1, bs[0]])
                        S.op("dve", lambda e: e.tensor_scalar(out=sm_[:L, 1:2], in0=psm[:L, 128:129], scalar1=-1.0, scalar2=flcol,
                                                              op0=ALU.mult, op1=ALU.max),
                             reads=[B_psD, B_uf], writes=[bs[1]])
                        S.op("dve", lambda e: e.tensor_tensor(out=sm_[:L, 2:3], in0=psm[:L, 128:129], in1=sm_[:L, 1:2], op=ALU.max),
                             reads=[B_psD, bs[1]], writes=[bs[2]])
                        S.op("dve", lambda e: e.scalar_tensor_tensor(out=sm_[:L, 3:4], in0=sm_[:L, 2:3], scalar=EPS,
                                                                     in1=sm_[:L, 2:3], op0=ALU.mult, op1=ALU.mult),
                             reads=[bs[2]], writes=[bs[3]])
                        S.op("dve", lambda e: e.scalar_tensor_tensor(out=sm_[:L, 4:5], in0=sm_[:L, 0:1], scalar=1.0 / DH,
                                                                     in1=sm_[:L, 3:4], op0=ALU.mult, op1=ALU.add),
                             reads=[bs[0], bs[3]], writes=[bs[4]])
                        S.op("pool", lambda e: e.tensor_tensor(out=sm_[:L, 5:6], in0=sm_[:L, 4:5], in1=cm05[:L], op=ALU.pow),
                             reads=[bs[4], B_cm05], writes=[bs[5]])
                        S.op("dve", lambda e, pN=pN: e.scalar_tensor_tensor(out=hfin[s][:L], in0=pN[:L, :], scalar=sm_[:L, 5:6],
                                                                            in1=G[s][:L], op0=ALU.mult, op1=ALU.mult),
                             reads=[BpN, bs[5], B_G[s]], writes=[B_hfin[s]])
                        S.dma("sp", [(hf[hf_rows, h * 512:(h + 1) * 512], hfin[s][:L])], B_hfin[s], reads=[B_hfin[s]])

                    def qkproj(h, slot, col, w, W5c, B_Wc, dsts=None):
                        if dsts is None:
                            dsts = ((0, qT[slot], B_qT[slot]), (1, kT[slot], B_kT[slot]))
                        for which, dst, Bd in dsts:
                            for dc in range(4):
                                pb, Bpb = bank()

                                def pm(e, which=which, dc=dc, pb=pb):
                                    ins = None
                                    for kc in range(8):
                                        ins = e.matmul(pb[:, 0:w], lhsT=W5c[:, kc, which, dc * 128:(dc + 1) * 128],
                                                       rhs=xnT[:, kc, col:col + w], start=(kc == 0), stop=(kc == 7))
                                    return ins
                                S.op("pe", pm, reads=[B_Wc[which], B_xnT], writes=[Bpb])
                                if which == 0:
                                    S.op("act", lambda e, dc=dc, pb=pb, dst=dst: e.copy(out=dst[:, dc, 0:w], in_=pb[:, 0:w]),
                                         reads=[Bpb], writes=[Bd])
                                else:
                                    S.op("act", lambda e, dc=dc, pb=pb, dst=dst: e.activation(
                                        out=dst[:, dc, 0:w], in_=pb[:, 0:w], func=AF.Copy, scale=float(DH) ** -0.5),
                                        reads=[Bpb], writes=[Bd])

                    gcount = 0

                    def load_w5(h):
                        sl = h % 2
                        for j in range(5):
                            c0 = j * DI + h * DH
                            S.dma("pool", [(W5[sl][:, :, j, :], w_in_a[:, c0:c0 + DH].rearrange("(k p) c -> p k c", p=128))],
                                  B_W[sl][j], writes=[B_W[sl][j]])
                    load_w5(0)
                    NG = (T + 511) // 512
                    qTs = sbt(p1, "qTs", [128, 4, NST], BF16)
                    kTs = sbt(p1, "kTs", [128, 4, NST], BF16)
                    B_qTs, B_kTs = S.buf("qTs"), S.buf("kTs")
                    spos = [(j + 1) * NC // (NS + 1) for j in range(NS)]

                    def sample_load(h, j):
                        S.dma("sp", [(Css[:], sC[j, h].rearrange("(dc p) e -> p dc e", p=128))], B_Cs[0], writes=B_Cs)
                        S.dma("sp", [(nss[:], sn[j, h].rearrange("(dc p) -> p dc", p=128))], B_ns, writes=[B_ns],
                              allow_slow_non_contiguous=True)

                    def sample_seq(h, j, W5c, B_Wc):
                        chunk(h, DS, slice(T + j * DS, T + (j + 1) * DS), qTs, kTs, B_qTs, B_kTs,
                              slice(j * DS, (j + 1) * DS),
                              ufs[:, j * 8 + h:j * 8 + h + 1], ufs[:, j * 8 + 4 + h:j * 8 + 5 + h],
                              ubs[:, j * 8 + h:j * 8 + h + 1], a_bc[:, h * NCH + NC + j:h * NCH + NC + j + 1],
                              Css, B_Cs, nss, B_ns, slice(T + j * DS, T + (j + 1) * DS), W5c, B_Wc)
                        S.dma("sp", [(C_s[j, h].rearrange("(dc p) e -> p dc e", p=128), Css[:])], B_Cs[0], reads=B_Cs)
                        S.dma("sp", [(n_s[j, h].rearrange("(dc p) -> p dc", p=128), nss[:])], B_ns, reads=[B_ns],
                              allow_slow_non_contiguous=True)

                    for h in range(H):
                        W5c, B_Wc = W5[h % 2], B_W[h % 2]
                        if h + 1 < H:
                            load_w5(h + 1)
                        if zero_fill:
                            zero_fill.pop()()
                        for dc in range(4):
                            S.op("pool", lambda e, dc=dc: e.memset(Cst[:, dc, :], 0.0), writes=[B_C[dc]])
                        S.op("pool", lambda e: e.memset(nst[:], 0.0), writes=[B_n])
                        slot = gcount % 2
                        gcount += 1
                        qkproj(h, slot, 0, min(512, T), W5c, B_Wc)
                        qkproj(h, None, T, NST, W5c, B_Wc, dsts=((0, qTs, B_qTs), (1, kTs, B_kTs)))
                        sample_load(h, 0)
                        for grp in range(NG):
                            col = grp * 512
                            w = min(512, T - col)
                            nch = w // 128
                            nslot = slot
                            for cc in range(nch):
                                c = grp * 4 + cc
                                if cc == nch - 1 and grp + 1 < NG:
                                    nslot = gcount % 2
                                    gcount += 1
                                    qkproj(h, nslot, col + 512, min(512, T - col - 512), W5c, B_Wc)
                                chunk(h, 128, slice(c * 128, (c + 1) * 128), qT[slot], kT[slot], B_qT[slot], B_kT[slot],
                                      slice(cc * 128, (cc + 1) * 128),
                                      uf_tok[:, c * 8 + h:c * 8 + h + 1], uf_tok[:, c * 8 + 4 + h:c * 8 + 5 + h],
                                      ub_tok[:, c * 8 + h:c * 8 + h + 1], a_bc[:, h * NCH + c:h * NCH + c + 1],
                                      Cst, B_C, nst, B_n, slice(c * 128, (c + 1) * 128), W5c, B_Wc)
                                for j in range(NS):
                                    if spos[j] == c:
                                        sample_seq(h, j, W5c, B_Wc)
                                        if j + 1 < NS:
                                            sample_load(h, j + 1)
                            slot = nslot
                        S.dma("sp", [(C_p[h].rearrange("(dc p) e -> p dc e", p=128), Cst[:])], B_C[0], reads=B_C)
                        S.dma("sp", [(n_p[h].rearrange("(dc p) -> p dc", p=128), nst[:])], B_n, reads=[B_n],
                              allow_slow_non_contiguous=True)
                    S.barrier()
        if dbg and "hf" in dbg:
            o = dout("dbg_hf", [TT, DI], BF16)
            dbg_out["hf"] = o
            Bd = S.buf("dbghf")
            S.dma("sp", [(o[:, :], hf[:, :])], Bd)
            S.barrier()


        gk_bc = sbt(top, "gk_bc", [128, HD])
        gq_bc = sbt(top, "gq_bc", [128, HD])
        B_gqk = S.buf("gqk")
        S.dma("sp", [(gk_bc[:], k_norm[0:1, :].to_broadcast([128, HD])),
                     (gq_bc[:], q_norm[0:1, :].to_broadcast([128, HD]))], B_gqk, writes=[B_gqk])

        gcolK = sbt(top, "gcolK", [128, 1])
        gcolQ = sbt(top, "gcolQ", [128, 1])
        S.dma("sp", [(gcolK[0:64, :], k_norm[0:1, :].rearrange("o d -> d o")), (gcolK[64:128, :], k_norm[0:1, :].rearrange("o d -> d o")),
                     (gcolQ[0:64, :], q_norm[0:1, :].rearrange("o d -> d o")), (gcolQ[64:128, :], q_norm[0:1, :].rearrange("o d -> d o"))],
              B_gqk, writes=[B_gqk])
        def tile_rows(i):
            return (128 if i < NT else NST), i * 128

        if "B" in phases:
            with contextlib.ExitStack() as pb_:
                x1nT = sbt(pb_, "x1nT", [128, 8, TT], BF16)
                B_x1nT = S.buf("x1nT")
                gkv = sbt(pb_, "gkv", [128, 8])
                gnb = sbt(pb_, "gnb", [128, 8])
                B_gn = S.buf("gn")
                S.dma("sp", [(gkv[:], norm_kv.rearrange("(k p) -> p k", p=128)),
                             (gnb[:], norm_b.rearrange("(k p) -> p k", p=128))], B_gn, writes=[B_gn],
                      allow_slow_non_contiguous=True)
                Wt0 = sbt(pb_, "WtB0", [128, 8, QW], BF16)
                stgB = [sbt(pb_, "stgB%d" % i, [128, QW]) for i in range(2)]
                B_stgB = [S.buf() for _ in range(2)]
                B_Wt0 = S.buf("WtB0")
                for kc in range(8):
                    s2_ = kc % 2
                    S.dma("sp", [(stgB[s2_][:, 0:QW], w_kv[kc * 128:(kc + 1) * 128, 0:QW])], B_stgB[s2_], writes=[B_stgB[s2_]])
                    S.op("act", lambda e, kc=kc, s2_=s2_: e.activation(out=Wt0[:, kc, 0:QW], in_=stgB[s2_][:, 0:QW],
                                                                       func=AF.Copy, scale=gkv[:, kc:kc + 1]),
                         reads=[B_stgB[s2_], B_gn], writes=[B_Wt0])
                with contextlib.ExitStack() as p1:
                    woA = sbt(p1, "woA", [128, 16, D], BF16)
                    B_woA = S.buf("woA")
                    stg = [sbt(p1, "stgA%d" % i, [128, D]) for i in range(2)]
                    B_stg = [S.buf() for _ in range(2)]
                    hng = sbt(p1, "hng", [128, 16])
                    B_hng = S.buf("hng")
                    S.dma("sp", [(hng[:], hnorm.rearrange("(k p) -> p k", p=128))], B_hng, writes=[B_hng],
                          allow_slow_non_contiguous=True)
                    for ec in range(16):
                        s2 = ec % 2
                        S.dma("sp", [(stg[s2][:], w_out_a[ec * 128:(ec + 1) * 128, :])], B_stg[s2], writes=[B_stg[s2]])
                        S.op("act", lambda e, ec=ec, s2=s2: e.activation(out=woA[:, ec, :], in_=stg[s2][:], func=AF.Copy,
                                                                         scale=hng[:, ec:ec + 1]),
                             reads=[B_stg[s2], B_hng], writes=[B_woA])
                    hft = [sbt(p1, "hft%d" % i, [128, DI], BF16) for i in range(3)]
                    hfT = [sbt(p1, "hfT%d" % i, [128, 16, 128], BF16) for i in range(2)]
                    xt = [sbt(p1, "xtB%d" % i, [128, D]) for i in range(4)]
                    x1 = [sbt(p1, "x1B%d" % i, [128, D]) for i in range(3)]
                    x1n = [sbt(p1, "x1n%d" % i, [128, D], BF16) for i in range(2)]
                    junkb = sbt(p1, "junkB", [128, D], BF16)
                    smb = sbt(p1, "smB", [128, 9])
                    B_hft = [S.buf("hft%d" % i) for i in range(3)]
                    B_hfT = [[S.buf(), S.buf()] for _ in range(2)]
                    B_xtb = [S.buf("xtB%d" % i) for i in range(4)]
                    B_x1 = [S.buf("x1B%d" % i) for i in range(3)]
                    B_x1n = [S.buf() for _ in range(2)]
                    B_junkb = S.buf()
                    B_smb = [[S.buf() for _ in range(3)] for _ in range(3)]

                    def run_skewed1(stages, n):
                        ns = len(stages)
                        for step in range(n + ns - 1):
                            for st in range(ns - 1, -1, -1):
                                i = step - st
                                if 0 <= i < n:
                                    stages[st](i)

                    def b1_load(i):
                        rows, c0 = tile_rows(i)
                        S.dma("sp", [(hft[i % 3][:rows], hf[c0:c0 + rows, :])], B_hft[i % 3], writes=[B_hft[i % 3]])
                        src = xp[c0:c0 + rows, :] if i < NT else xs[:, :]
                        S.dma("sp", [(xt[i % 4][:rows], src)], B_xtb[i % 4], writes=[B_xtb[i % 4]])

                    def b1_s0(i):
                        rows, c0 = tile_rows(i)
                        s2, s3 = i % 2, i % 3
                        for hb in range(2):
                            pb, Bpb = bank()
                            pbb = pb[:].bitcast(BF16)

                            def trh(e, hb=hb, pbb=pbb):
                                ins = None
                                for j in range(8):
                                    ec = hb * 8 + j
                                    ins = e.transpose(out=pbb[:, j * 128:j * 128 + rows],
                                                      in_=hft[s3][:rows, ec * 128:(ec + 1) * 128], identity=identb[:rows, :rows])
                                return ins
                            S.op("pe", trh, reads=[B_hft[s3], B_identb], writes=[Bpb])
                            S.op("act" if hb == 0 else "dve", lambda e, hb=hb, pbb=pbb: (e.copy if hb == 0 else e.tensor_copy)(
                                out=hfT[s2][:, hb * 8:(hb + 1) * 8, :rows],
                                in_=pbb.rearrange("p (k t) -> p k t", t=128)[:, :, :rows]),
                                reads=[Bpb], writes=[B_hfT[s2][hb]])

                    def b1_s1(i):
                        rows, c0 = tile_rows(i)
                        s2, s3, s4 = i % 2, i % 3, i % 4
                        for half in range(2):
                            pb, Bpb = bank()

                            def ym(e, half=half, pb=pb):
                                ins = None
                                for ec in range(16):
                                    ins = e.matmul(pb[:rows, :], lhsT=hfT[s2][:, ec, :rows],
                                                   rhs=woA[:, ec, half * 512:(half + 1) * 512], start=(ec == 0), stop=(ec == 15))
                                return ins
                            S.op("pe", ym, reads=[B_hfT[s2][0], B_hfT[s2][1], B_woA], writes=[Bpb])
                            S.op("dve", lambda e, half=half, pb=pb: e.tensor_tensor(
                                out=x1[s3][:rows, half * 512:(half + 1) * 512], in0=pb[:rows, :],
                                in1=xt[s4][:rows, half * 512:(half + 1) * 512], op=ALU.add),
                                reads=[Bpb, B_xtb[s4]], writes=[B_x1[s3]])
                        S.dma("pool", [(x1s[c0:c0 + rows, :], x1[s3][:rows])], B_x1[s3], reads=[B_x1[s3]])
                        S.op("act", lambda e: e.activation(out=junkb[:rows], in_=x1[s3][:rows], func=AF.Square,
                                                           accum_out=smb[:rows, s3 * 3:s3 * 3 + 1]),
                             reads=[B_x1[s3]], writes=[B_junkb, B_smb[s3][0]])

                    def b1_s2(i):
                        rows, c0 = tile_rows(i)
                        s2, s3 = i % 2, i % 3
                        rsqrt_small(rows, smb[:rows, s3 * 3 + 2:s3 * 3 + 3], smb[:rows, s3 * 3:s3 * 3 + 1], 1.0 / D, EPS,
                                    B_smb[s3][0], B_smb[s3][2], smb[:rows, s3 * 3 + 1:s3 * 3 + 2], B_smb[s3][1])
                        S.op("act", lambda e: e.activation(out=x1n[s2][:rows], in_=x1[s3][:rows], func=AF.Copy,
                                                           scale=smb[:rows, s3 * 3 + 2:s3 * 3 + 3]),
                             reads=[B_x1[s3], B_smb[s3][2]], writes=[B_x1n[s2]])

                    def b1_s3(i):
                        rows, c0 = tile_rows(i)
                        s2 = i % 2
                        pb, Bpb = bank()
                        pbb = pb[:].bitcast(BF16)

                        def trx(e, pbb=pbb):
                            ins = None
                            for kc in range(8):
                                ins = e.transpose(out=pbb[:, kc * 128:kc * 128 + rows],
                                                  in_=x1n[s2][:rows, kc * 128:(kc + 1) * 128], identity=identb[:rows, :rows])
                            return ins
                        S.op("pe", trx, reads=[B_x1n[s2], B_identb], writes=[Bpb])
                        S.op("dve", lambda e, pbb=pbb: e.tensor_copy(out=x1nT[:, :, c0:c0 + rows],
                                                                     in_=pbb.rearrange("p (k t) -> p k t", t=128)[:, :, :rows]),
                             reads=[Bpb], writes=[B_x1nT])
                    run_skewed1([b1_load, b1_s0, b1_s1, b1_s2, b1_s3], NTT)
                    S.barrier()

                with contextlib.ExitStack() as p2:
                    Wts = [Wt0, sbt(p2, "WtB1", [128, 8, QW], BF16)]
                    B_Wts = [B_Wt0, S.buf("WtB1")]
                    stg = stgB
                    B_stg = B_stgB
                    NSL = 4
                    raw = [sbt(p2, "rawB%d" % i, [128, QW]) for i in range(NSL)]
                    B_raw = [S.buf() for _ in range(NSL)]
                    sqt = [sbt(p2, "sqtB%d" % i, [128, QW]) for i in range(2)]
                    B_sqt = [S.buf() for _ in range(2)]
                    nb16 = [sbt(p2, "nb16_%d" % i, [128, QW], BF16) for i in range(NSL)]
                    B_nb16 = [S.buf() for _ in range(NSL)]
                    TTt = [sbt(p2, "TTt%d" % i, [128, 12, 128], BF16) for i in range(2)]
                    B_TTt = [S.buf("TTt%d" % i) for i in range(2)]
                    vb = [sbt(p2, "vb%d" % i, [128, NQH, HD + 1], BF16) for i in range(2)]
                    B_vb = [S.buf("vb%d" % i) for i in range(2)]
                    sm2 = sbt(p2, "sm2", [128, 2 * NSL * NQH])
                    B_sm2 = [[S.buf() for _ in range(2)] for _ in range(NSL)]
                    zsb = [sbt(p2, "zsb%d" % i, [128, 512], BF16) for i in range(2)]
                    sgz = [sbt(p2, "sgz%d" % i, [128, 512]) for i in range(2)]
                    B_zsb = [S.buf("zsb%d" % i) for i in range(2)]
                    B_sgz = [S.buf() for _ in range(2)]
                    for i in range(2):
                        S.op("pool", lambda e, i=i: e.memset(vb[i][:, :, HD:HD + 1], 1.0), writes=[B_vb[i]])

                    def run_skewed(stages, n):
                        ns = len(stages)
                        for step in range(n + ns - 1):
                            for st in range(ns - 1, -1, -1):
                                i = step - st
                                if 0 <= i < n:
                                    stages[st](i)

                    def load_w(wsl, wsrc, c0, ncols, gain):
                        Wt, B_Wt = Wts[wsl], B_Wts[wsl]
                        for kc in range(8):
                            s2 = kc % 2
                            S.dma("sp", [(stg[s2][:, 0:ncols], wsrc[kc * 128:(kc + 1) * 128, c0:c0 + ncols])], B_stg[s2],
                                  writes=[B_stg[s2]])
                            S.op("act", lambda e, kc=kc, s2=s2: e.activation(out=Wt[:, kc, 0:ncols], in_=stg[s2][:, 0:ncols],
                                                                             func=AF.Copy, scale=gain[:, kc:kc + 1]),
                                 reads=[B_stg[s2], B_gn], writes=[B_Wt])

                    def proj(i, nblk, evac, wsl):
                        Wt, B_Wt = Wts[wsl], B_Wts[wsl]
                        rows, c0 = tile_rows(i)
                        for nb in range(nblk):
                            pb, Bpb = bank()

                            def pm(e, nb=nb, pb=pb):
                                ins = None
                                for kc in range(8):
                                    ins = e.matmul(pb[:rows, :], lhsT=x1nT[:, kc, c0:c0 + rows],
                                                   rhs=Wt[:, kc, nb * 512:(nb + 1) * 512], start=(kc == 0), stop=(kc == 7))
                                return ins
                            S.op("pe", pm, reads=[B_x1nT, B_Wt], writes=[Bpb])
                            evac(nb, pb, Bpb)

                    def kv_out(i, which, src_tile, Bsrc):
                        rows, c0 = tile_rows(i)
                        pairs = []
                        for g, (win, dil) in enumerate(GROUPS):
                            if i < NT:
                                nrow = min(win, T)
                                r0 = c0 - (T - nrow)
                                if r0 < 0:
                                    continue
                                dst = kvp[g][r0:r0 + rows, which, :, :]
                            else:
                                dst = kvs[g][:, which, :, :]
                            pairs.append((dst, src_tile[:rows, g * 512:(g + 1) * 512].rearrange("p (h d) -> p h d", d=HD)))
                        if pairs:
                            S.dma("sp", pairs, Bsrc, reads=[Bsrc])

                    def qk_phase(wsl, g_bc, dstT, is_k, after_load=None):

                        def st0(i):
                            rows, c0 = tile_rows(i)
                            s4, s2 = i % NSL, i % 2

                            def ev(nb, pb, Bpb):
                                S.op("act", lambda e: e.copy(out=raw[s4][:rows, nb * 512:(nb + 1) * 512], in_=pb[:rows, :]),
                                     reads=[Bpb], writes=[B_raw[s4]])
                                S.op("act", lambda e: e.activation(out=sqt[s2][:rows, nb * 512:(nb + 1) * 512], in_=pb[:rows, :],
                                                                   func=AF.Square), reads=[Bpb], writes=[B_sqt[s2]])
                            proj(i, 3, ev, wsl)
                            if i == 0 and after_load is not None:
                                after_load()

                        def st1(i):
                            rows, c0 = tile_rows(i)
                            s4, s2 = i % NSL, i % 2
                            ssq = sm2[:rows, s4 * 2 * NQH:s4 * 2 * NQH + NQH]
                            rst = sm2[:rows, s4 * 2 * NQH + NQH:(s4 + 1) * 2 * NQH]
                            S.op("dve", lambda e: e.tensor_reduce(out=ssq, in_=sqt[s2][:rows].rearrange("p (h d) -> p h d", d=HD),
                                                                  axis=AX.X, op=ALU.add),
                                 reads=[B_sqt[s2]], writes=[B_sm2[s4][0]])
                            S.op("dve", lambda e: e.tensor_scalar(out=ssq, in0=ssq, scalar1=1.0 / HD, scalar2=EPS,
                                                                  op0=ALU.mult, op1=ALU.add),
                                 reads=[B_sm2[s4][0]], writes=[B_sm2[s4][0]])
                            S.op("act", lambda e: e.activation(out=ssq, in_=ssq, func=AF.Sqrt),
                                 reads=[B_sm2[s4][0]], writes=[B_sm2[s4][0]])
                            S.op("dve", lambda e: e.reciprocal(out=rst, in_=ssq),
                                 reads=[B_sm2[s4][0]], writes=[B_sm2[s4][1]])

                        def st2(i):
                            rows, c0 = tile_rows(i)
                            s4 = i % NSL
                            rst = sm2[:rows, s4 * 2 * NQH + NQH:(s4 + 1) * 2 * NQH]
                            if is_k:
                                S.op("dve", lambda e: e.tensor_tensor(
                                    out=raw[s4][:rows].rearrange("p (h d) -> p h d", d=HD),
                                    in0=raw[s4][:rows].rearrange("p (h d) -> p h d", d=HD),
                                    in1=rst.unsqueeze(2).to_broadcast([rows, NQH, HD]), op=ALU.mult),
                                    reads=[B_raw[s4], B_sm2[s4][1]], writes=[B_raw[s4]])
                                S.op("act", lambda e: e.copy(out=nb16[s4][:rows], in_=raw[s4][:rows]),
                                     reads=[B_raw[s4]], writes=[B_nb16[s4]])
                                need = []
                                for g, (win, dil) in enumerate(GROUPS):
                                    if i >= NT or c0 - (T - min(win, T)) >= 0:
                                        need.append(g)
                                for g in need:
                                    S.op("dve", lambda e, g=g: e.tensor_tensor(
                                        out=raw[s4][:rows, g * 512:(g + 1) * 512].rearrange("p (h d) -> p h d", d=HD),
                                        in0=raw[s4][:rows, g * 512:(g + 1) * 512].rearrange("p (h d) -> p h d", d=HD),
                                        in1=g_bc[:rows].unsqueeze(1).to_broadcast([rows, 8, HD]), op=ALU.mult),
                                        reads=[B_raw[s4], B_gqk], writes=[B_raw[s4]])
                                if need:
                                    kv_out(i, 0, raw[s4], B_raw[s4])
                            else:
                                S.op("dve", lambda e: e.tensor_tensor(
                                    out=nb16[s4][:rows].rearrange("p (h d) -> p h d", d=HD),
                                    in0=raw[s4][:rows].rearrange("p (h d) -> p h d", d=HD),
                                    in1=rst.unsqueeze(2).to_broadcast([rows, NQH, HD]), op=ALU.mult),
                                    reads=[B_raw[s4], B_sm2[s4][1]], writes=[B_nb16[s4]])

                        def st3(i):
                            rows, c0 = tile_rows(i)
                            s4, s2 = i % NSL, i % 2
                            for hb, (j0, j1) in enumerate(((0, 8), (8, 12))):
                                pb, Bpb = bank()
                                pbb = pb[:].bitcast(BF16)

                                def trq(e, j0=j0, j1=j1, pbb=pbb):
                                    ins = None
                                    for j in range(j0, j1):
                                        ins = e.transpose(out=pbb[:, (j - j0) * 128:(j - j0) * 128 + rows],
                                                          in_=nb16[s4][:rows, j * 128:(j + 1) * 128], identity=identb[:rows, :rows])
                                    return ins
                                S.op("pe", trq, reads=[B_nb16[s4], B_identb], writes=[Bpb])
                                gcol = gcolK if is_k else gcolQ
                                if hb == 0:
                                    S.op("dve", lambda e, j0=j0, j1=j1, pbb=pbb: e.tensor_scalar(
                                        out=TTt[s2][:, j0:j1, :rows],
                                        in0=pbb[:, 0:(j1 - j0) * 128].rearrange("p (k t) -> p k t", t=128)[:, :, :rows],
                                        scalar1=gcol[:, 0:1], scalar2=None, op0=ALU.mult),
                                        reads=[Bpb, B_gqk], writes=[B_TTt[s2]])
                                else:
                                    S.op("act", lambda e, j0=j0, j1=j1, pbb=pbb: e.activation(
                                        out=TTt[s2][:, j0:j1, :rows],
                                        in_=pbb[:, 0:(j1 - j0) * 128].rearrange("p (k t) -> p k t", t=128)[:, :, :rows],
                                        func=AF.Copy, scale=gcol[:, 0:1]),
                                        reads=[Bpb, B_gqk], writes=[B_TTt[s2]])
                            if is_k:
                                S.dma("sp", [(dstT.rearrange("(rg p) t -> p rg t", p=128)[:, :, c0:c0 + rows], TTt[s2][:, :, :rows])],
                                      B_TTt[s2], reads=[B_TTt[s2]])
                            else:
                                S.dma("sp", [(dstT.rearrange("(rg p) t -> p rg t", p=128)[:, :, c0:c0 + rows], TTt[s2][:, :, :rows]),
                                             (QTse.rearrange("(rg p) t -> p rg t", p=128)[0:64, :, c0:c0 + rows], TTt[s2][0:64, :, :rows]),
                                             (QTso.rearrange("(rg p) t -> p rg t", p=128)[64:128, :, c0:c0 + rows], TTt[s2][64:128, :, :rows])],
                                      B_TTt[s2], reads=[B_TTt[s2]])
                        run_skewed([st0, st1, st2, st3], NTT)

                    qk_phase(0, gk_bc, KTs, True, after_load=lambda: load_w(1, w_kv, QW, QW, gkv))

                    def v0(i):
                        rows, c0 = tile_rows(i)
                        s4 = i % NSL

                        def evv(nb, pb, Bpb):
                            S.op("act", lambda e: e.copy(out=raw[s4][:rows, nb * 512:(nb + 1) * 512], in_=pb[:rows, :]),
                                 reads=[Bpb], writes=[B_raw[s4]])
                        proj(i, 3, evv, 1)
                        if i == 0:
                            load_w(0, w_in_b, 0, QW, gnb)

                    def v1(i):
                        rows, c0 = tile_rows(i)
                        s4, s2 = i % NSL, i % 2
                        S.op("dve", lambda e: e.tensor_copy(out=vb[s2][:rows, :, 0:HD],
                                                            in_=raw[s4][:rows].rearrange("p (h d) -> p h d", d=HD)),
                             reads=[B_raw[s4]], writes=[B_vb[s2]])
                        S.dma("sp", [(Vs[c0:c0 + rows, :, :], vb[s2][:rows])], B_vb[s2], reads=[B_vb[s2]])
                        kv_out(i, 1, raw[s4], B_raw[s4])
                    run_skewed([v0, v1], NTT)
                    qk_phase(0, gq_bc, QTs, False, after_load=lambda: load_w(1, w_in_b, QW, 512, gnb))
                    zbank = {}

                    def z0(i):
                        rows, c0 = tile_rows(i)
                        s2 = i % 2

                        def evz(nb, pb, Bpb):
                            zbank[i] = (pb, Bpb)
                            S.op("act", lambda e: e.activation(out=sgz[s2][:rows], in_=pb[:rows, :], func=AF.Sigmoid),
                                 reads=[Bpb], writes=[B_sgz[s2]])
                        proj(i, 1, evz, 1)

                    def z1(i):
                        rows, c0 = tile_rows(i)
                        s2 = i % 2
                        pb, Bpb = zbank.pop(i)
                        S.op("dve", lambda e: e.tensor_tensor(out=zsb[s2][:rows], in0=pb[:rows, :], in1=sgz[s2][:rows], op=ALU.mult),
                             reads=[Bpb, B_sgz[s2]], writes=[B_zsb[s2]])
                        S.dma("sp", [(zss[c0:c0 + rows, :], zsb[s2][:rows])], B_zsb[s2], reads=[B_zsb[s2]])
                    run_skewed([z0, z1], NTT)
                    S.barrier()

        scr = {"x1s": (x1s, [TT, D], F32), "KTs": (KTs, [QW, TT], BF16), "QTs": (QTs, [QW, TT], BF16),
               "Vs": (Vs, [TT, NQH, HD + 1], BF16), "zss": (zss, [TT, 512], BF16), "osc": (osc, [3, TT, 8 * (HD + 1)], F32),
               "vecs": (vecs, [NQH, 384], F32)}
        for name in dbg:
            if name in scr:
                ap_, shp, dt_ = scr[name]
                o = dout("dbg_" + name, shp, dt_)
                dbg_out[name] = o
                Bd = S.buf("dbg" + name)
                S.dma("sp", [(o, ap_)], Bd)
        if dbg:
            S.barrier()


        OW = 8 * (HD + 1)
        ps_ = contextlib.ExitStack()
        ET = sbt(ps_, "ET", [NBUCK, NQH])
        EBs = sbt(ps_, "EBs", [128, 13, 8, 8])
        EBn = sbt(ps_, "EBn", [8, 3, 8, 8])
        if "C" in phases or "c" in phases:
            with contextlib.ExitStack() as pc:
                B_ET = S.buf("ET")
                S.dma("sp", [(ET[:], rel_bias[:, :])], B_ET, writes=[B_ET])
                S.op("act", lambda e: e.activation(out=ET[:], in_=ET[:], func=AF.Exp), reads=[B_ET], writes=[B_ET])
                Tb = sbt(pc, "Tb", [128, NQH, 256])
                B_Tb = S.buf("Tb")
                B_EBs = S.buf("EBs")
                B_EBn = S.buf("EBn")
                with contextlib.ExitStack() as pc0:
                    ohp = sbt(pc0, "ohp", [NBUCK, 3 * 384])
                    ohs = sbt(pc0, "ohs", [NBUCK, 13 * 8 * 128])
                    ohn = sbt(pc0, "ohn", [NBUCK, 3 * 8 * 8])
                    Jf = sbt(pc0, "Jf", [128, 128])
                    B_oh = S.buf("oh")
                    S.dma("sp", [(ohp[:], c_ohp[:, :]), (ohs[:], c_ohs[:, :]), (ohn[:], c_ohn[:, :]), (Jf[:], c_J[:, :])],
                          B_oh, writes=[B_oh])
                    vtmp = sbt(pc0, "vtmp", [8, 3 * 384])
                    B_vtmp = S.buf("vtmp")
                    for g in range(3):
                        pb, Bpb = bank()
                        S.op("pe", lambda e: e.matmul(pb[0:8, 0:384], lhsT=ET[:, g * 8:(g + 1) * 8], rhs=ohp[:, g * 384:(g + 1) * 384],
                                                      start=True, stop=True), reads=[B_ET, B_oh], writes=[Bpb])
                        S.op("dve", lambda e: e.tensor_copy(out=vtmp[:, g * 384:(g + 1) * 384], in_=pb[0:8, 0:384]),
                             reads=[Bpb], writes=[B_vtmp])
                    B_vecs = S.buf("vecs")
                    S.dma("sp", [(vecs[g * 8:(g + 1) * 8, :], vtmp[:, g * 384:(g + 1) * 384]) for g in range(3)], B_vtmp,
                          reads=[B_vtmp], writes=[B_vecs])
                    Hk = sbt(pc0, "Hk", [128, NQH, 256])
                    B_Hk = S.buf("Hk")
                    S.dma("sp", [(Hk[:], bass.AP(vecs.tensor, 0, [[1, 128], [384, NQH], [1, 256]]))], B_Hk,
                          reads=[B_vecs], writes=[B_Hk])
                    for gh2 in range(NQH // 2):
                        pb, Bpb = bank()
                        S.op("pe", lambda e: e.matmul(pb[:, :], lhsT=Jf[:, :],
                                                      rhs=Hk[:, 2 * gh2:2 * gh2 + 2, :].rearrange("p a b -> p (a b)"),
                                                      start=True, stop=True), reads=[B_oh, B_Hk], writes=[Bpb])
                        S.op("dve" if gh2 % 2 else "act", lambda e: (e.tensor_copy if gh2 % 2 else e.copy)(
                            out=Tb[:, 2 * gh2:2 * gh2 + 2, :].rearrange("p a b -> p (a b)"), in_=pb[:, :]),
                            reads=[Bpb], writes=[B_Tb])
                    S.op("pool", lambda e: e.memset(EBs[:].rearrange("p a b c -> p (a b c)"), 0.0), writes=[B_EBs])
                    for gr, (g, r) in enumerate(CLASSES):
                        pb, Bpb = bank()

                        def ebm(e):
                            ins = None
                            for s_ in range(8):
                                o0 = (gr * 8 + s_) * 128
                                ins = e.matmul(pb[:, s_ * 8:s_ * 8 + 8], lhsT=ohs[:, o0:o0 + 128], rhs=ET[:, g * 8:(g + 1) * 8],
                                               start=True, stop=True)
                            return ins
                        S.op("pe", ebm, reads=[B_oh, B_ET], writes=[Bpb])
                        S.op("dve", lambda e: e.tensor_copy(out=EBs[:, gr, :, :],
                                                            in_=pb[:, 0:64].rearrange("p (s h) -> p h s", h=8)),
                             reads=[Bpb], writes=[B_EBs])
                    for g in range(3):
                        pb, Bpb = bank()

                        def ebn(e):
                            ins = None
                            for s_ in range(8):
                                o0 = (g * 8 + s_) * 8
                                ins = e.matmul(pb[0:8, s_ * 8:s_ * 8 + 8], lhsT=ohn[:, o0:o0 + 8], rhs=ET[:, g * 8:(g + 1) * 8],
                                               start=True, stop=True)
                            return ins
                        S.op("pe", ebn, reads=[B_oh, B_ET], writes=[Bpb])
                        S.op("dve", lambda e: e.tensor_copy(out=EBn[:, g, :, :],
                                                            in_=pb[0:8, 0:64].rearrange("p (s h) -> p h s", h=8)),
                             reads=[Bpb], writes=[B_EBn])
                    if "Tb" in dbg:
                        dump("Tb", Tb[:], [128, NQH, 256], B_Tb)
                        dump("EBs", EBs[:], [128, 13, 8, 8], B_EBs)
                        dump("EBn", EBn[:], [8, 3, 8, 8], B_EBn)
                    S.barrier()

                QK = [[sbt(pc, "QK%d_%d" % (i, k), [128, 2, TT], BF16) for k in range(3)] for i in range(2)]
                B_QK = [S.buf("QK%d" % i) for i in range(2)]
                NV = 8
                Vt = [sbt(pc, "Vt%d" % i, [128, 4, HD + 1], BF16) for i in range(NV)]
                B_Vt = [S.buf("Vt%d" % i) for i in range(NV)]
                W4 = 4 * (HD + 1)
                ob = [sbt(pc, "ob%d" % i, [128, W4]) for i in range(2)]
                B_ob = [S.buf("ob%d" % i) for i in range(2)]
                if T % 2048 == 0 and "C" in phases:
                    Et2 = [sbt(pc, "Et2_%d" % i, [128, 4, 256], BF16) for i in range(3)]
                    Tbb = sbt(pc, "Tbb", [128, NQH, 256], BF16)
                    S.op("pool", lambda e: e.tensor_copy(out=Tbb[:].rearrange("p a b -> p (a b)"), in_=Tb[:].rearrange("p a b -> p (a b)")),
                         reads=[B_Tb], writes=[B_Tb])
                    Pt2 = [sbt(pc, "Pt2_%d" % i, [128, 4, 256], BF16) for i in range(3)]
                    B_Et2 = [S.buf() for _ in range(3)]
                    B_Pt2 = [S.buf() for _ in range(3)]
                    B_S2 = [pbanks[2][1], pbanks[4][1], pbanks[6][1]]
                    B_S2b = [pbanks[3][1], pbanks[5][1], pbanks[7][1]]
                    sets = [(g, b) for g in range(3) for b in range(2)]

                    def load_set(si):
                        g, b = sets[si]
                        sl = si % 2
                        r0 = (g * 4 + 2 * b) * 128
                        S.dma("sp", [(QK[sl][2][:], KTs[r0:r0 + 256, :].rearrange("(rg p) t -> p rg t", p=128)),
                                     (QK[sl][0][:], QTse[r0:r0 + 256, :].rearrange("(rg p) t -> p rg t", p=128)),
                                     (QK[sl][1][:], QTso[r0:r0 + 256, :].rearrange("(rg p) t -> p rg t", p=128))],
                              B_QK[sl], writes=[B_QK[sl]])
                    batches = []
                    vcount = 0
                    for si, (g, b) in enumerate(sets):
                        win, dil = GROUPS[g]
                        nblk = T // (128 * dil)
                        for r in range(dil):
                            prev = None
                            for n in range(nblk):
                                t0 = n * 128 * dil + r
                                cols = slice(t0, t0 + 127 * dil + 1, dil)
                                vi = vcount % NV
                                vcount += 1
                                batches.append(dict(si=si, g=g, b=b, dil=dil, t0=t0, cols=cols, vi=vi, prev=prev,
                                                    first=(r == 0 and n == 0)))
                                prev = (cols, vi)
                    load_set(0)

                    def emit_S(t, bt):
                        g, b, si = bt["g"], bt["b"], bt["si"]
                        sl = si % 2
                        if bt["first"] and si + 1 < len(sets):
                            load_set(si + 1)
                        sp_ = t % 3
                        psS = psall[:, (2 + 2 * sp_) * 512:(4 + 2 * sp_) * 512].rearrange("p (k m) -> p k m", m=256)
                        prev, cols = bt["prev"], bt["cols"]
                        wk = 256 if prev is not None else 128
                        Qe, Qo, Kt_ = QK[sl]

                        def sm_(e):
                            ins = None
                            for k in range(4):
                                rg = k // 2
                                Qx = Qe if k % 2 == 0 else Qo
                                ins = e.matmul(psS[:, k, 0:128], lhsT=Kt_[:, rg, cols], rhs=Qx[:, rg, cols], start=True, stop=True)
                                if prev is not None:
                                    ins = e.matmul(psS[:, k, 128:256], lhsT=Kt_[:, rg, prev[0]], rhs=Qx[:, rg, cols],
                                                   start=True, stop=True)
                            return ins
                        S.op("pe", sm_, reads=[B_QK[sl]], writes=[B_S2[sp_], B_S2b[sp_]])
                        S.op("act", lambda e: e.activation(out=Et2[sp_][:, :, 0:wk], in_=psS[:, :, 0:wk], func=AF.Exp, scale=0.125),
                             reads=[B_S2[sp_], B_S2b[sp_]], writes=[B_Et2[sp_]])
                        h0 = g * 8 + b * 4
                        S.op("dve", lambda e: e.tensor_tensor(out=Pt2[sp_][:, :, 0:wk], in0=Et2[sp_][:, :, 0:wk],
                                                              in1=Tbb[:, h0:h0 + 4, 0:wk], op=ALU.mult),
                             reads=[B_Et2[sp_], B_Tb], writes=[B_Pt2[sp_]])

                    def emit_V(bt):
                        g, b = bt["g"], bt["b"]
                        t0, dil, vi = bt["t0"], bt["dil"], bt["vi"]
                        S.dma("sp", [(Vt[vi][:], Vs[t0:t0 + 127 * dil + 1:dil, g * 8 + 4 * b:g * 8 + 4 * b + 4, :])], B_Vt[vi],
                              writes=[B_Vt[vi]])

                    def emit_PV(t, bt):
                        sp_ = t % 3
                        g, vi, prev, b = bt["g"], bt["vi"], bt["prev"], bt["b"]
                        o2 = t % 2
                        pO, BpO = pbanks[o2]

                        def pvm(e):
                            ins = None
                            for k in range(4):
                                oc = k * (HD + 1)
                                ins = e.matmul(pO[:, oc:oc + HD + 1], lhsT=Pt2[sp_][:, k, 0:128], rhs=Vt[vi][:, k, :],
                                               start=True, stop=(prev is None))
                                if prev is not None:
                                    ins = e.matmul(pO[:, oc:oc + HD + 1], lhsT=Pt2[sp_][:, k, 128:256], rhs=Vt[prev[1]][:, k, :],
                                                   start=False, stop=True)
                            return ins
                        rd = [B_Pt2[sp_], B_Vt[vi]] + ([B_Vt[prev[1]]] if prev is not None else [])
                        S.op("pe", pvm, reads=rd, writes=[BpO])
                        if o2 == 0:
                            S.op("act", lambda e: e.copy(out=ob[o2][:], in_=pO[:, 0:W4]), reads=[BpO], writes=[B_ob[o2]])
                        else:
                            S.op("dve", lambda e: e.tensor_copy(out=ob[o2][:], in_=pO[:, 0:W4]), reads=[BpO], writes=[B_ob[o2]])
                        t0, dil = bt["t0"], bt["dil"]
                        S.dma("sp", [(osc[g, t0:t0 + 127 * dil + 1:dil, b * W4:(b + 1) * W4], ob[o2][:])], B_ob[o2], reads=[B_ob[o2]])

                    for t in range(min(2, len(batches))):
                        emit_V(batches[t])
                    for t in range(len(batches) + 2):
                        if t + 2 < len(batches):
                            emit_V(batches[t + 2])
                        if t < len(batches):
                            emit_S(t, batches[t])
                        if t >= 2:
                            emit_PV(t - 2, batches[t - 2])

                S.barrier()

        if "D" in phases:
            with contextlib.ExitStack() as pd:
                Kc = [sbt(pd, "Kc%d" % i, [128, 512]) for i in range(2)]
                Vc = [sbt(pd, "Vc%d" % i, [128, 512]) for i in range(2)]
                B_Kc = [S.buf("Kc%d" % i) for i in range(2)]
                B_Vc = [S.buf("Vc%d" % i) for i in range(2)]
                Kcb = [sbt(pd, "Kcb%d" % i, [128, 512], BF16) for i in range(2)]
                B_Kcb = [S.buf() for _ in range(2)]
                KcT = [sbt(pd, "KcT%d" % i, [128, 4, 128], BF16) for i in range(2)]
                B_KcT = [S.buf() for _ in range(2)]
                Vca = sbt(pd, "Vca", [128, 13, 8, HD + 1], BF16)
                B_Vca = S.buf("Vca")
                S.op("pool", lambda e: e.memset(Vca[:].rearrange("p a b c -> p (a b c)"), 1.0), writes=[B_Vca])
                Pall = sbt(pd, "Pall", [128, 13, 8, 8], BF16)
                B_Pall = S.buf("Pall")
                Es = [sbt(pd, "Es%d" % i, [128, 64]) for i in range(2)]
                B_Es = [S.buf() for _ in range(2)]
                QTn = [sbt(pd, "QTn%d" % i, [128, 12, NST], BF16) for i in range(2)]
                KTn = sbt(pd, "KTn", [128, 12, NST], BF16)
                B_QKn = S.buf("QKn")
                S.dma("sp", [(QTn[0][:], QTse[:, T:TT].rearrange("(rg p) t -> p rg t", p=128)),
                             (QTn[1][:], QTso[:, T:TT].rearrange("(rg p) t -> p rg t", p=128)),
                             (KTn[:], KTs[:, T:TT].rearrange("(rg p) t -> p rg t", p=128))], B_QKn, writes=[B_QKn])
                Vn = sbt(pd, "Vn", [8, NQH, HD + 1], BF16)
                B_Vn = S.buf("Vn")
                Pn = sbt(pd, "Pn", [8, 3, 8, 8], BF16)
                En = sbt(pd, "En", [8, 3 * 64])
                B_Pn = S.buf("Pn")
                B_En = S.buf("En")
                obs = sbt(pd, "obs", [8, OW])
                B_obs = S.buf("obs")
                B_oscs = S.buf("oscs")
                kc_i = [0]

                def s_start(j):
                    S.dma("pool", [(Vn[:], Vs[T + j * DS:T + (j + 1) * DS, :, :])], B_Vn, writes=[B_Vn])

                def s_unit(j, gr):
                    g, r = CLASSES[gr]
                    qs_ = slice(j * DS, (j + 1) * DS)
                    dil = GROUPS[g][1]
                    s2 = kc_i[0] % 2
                    kc_i[0] += 1
                    S.dma("pool", [(Kc[s2][:], caches[g][j, r:r + 127 * dil + 1:dil, 0, :, :].rearrange("p h d -> p (h d)"))],
                          B_Kc[s2], writes=[B_Kc[s2]])
                    S.dma("pool", [(Vc[s2][:], caches[g][j, r:r + 127 * dil + 1:dil, 1, :, :].rearrange("p h d -> p (h d)"))],
                          B_Vc[s2], writes=[B_Vc[s2]])
                    S.op("act", lambda e: e.copy(out=Kcb[s2][:], in_=Kc[s2][:]), reads=[B_Kc[s2]], writes=[B_Kcb[s2]])
                    S.op("dve", lambda e: e.tensor_copy(out=Vca[:, gr, :, 0:HD], in_=Vc[s2][:].rearrange("p (h d) -> p h d", d=HD)),
                         reads=[B_Vc[s2]], writes=[B_Vca])
                    pb, Bpb = pbanks[4]
                    pbb = pb[:].bitcast(BF16)

                    def trk(e):
                        ins = None
                        for rg in range(4):
                            ins = e.transpose(out=pbb[:, rg * 128:(rg + 1) * 128], in_=Kcb[s2][:, rg * 128:(rg + 1) * 128],
                                              identity=identb[:, :])
                        return ins
                    S.op("pe", trk, reads=[B_Kcb[s2], B_identb], writes=[Bpb])
                    S.op("act", lambda e: e.copy(out=KcT[s2][:].rearrange("p a b -> p (a b)"), in_=pbb[:, 0:512]),
                         reads=[Bpb], writes=[B_KcT[s2]])
                    pS, BpS = pbanks[5]

                    def ssm(e):
                        ins = None
                        for hs in range(8):
                            rg = hs // 2
                            ins = e.matmul(pS[:, hs * 8:hs * 8 + 8], lhsT=KcT[s2][:, rg, :],
                                           rhs=QTn[hs % 2][:, g * 4 + rg, qs_], start=True, stop=True)
                        return ins
                    S.op("pe", ssm, reads=[B_KcT[s2], B_QKn], writes=[BpS])
                    S.op("act", lambda e: e.activation(out=Es[s2][:], in_=pS[:, 0:64], func=AF.Exp, scale=0.125),
                         reads=[BpS], writes=[B_Es[s2]])
                    S.op("dve", lambda e: e.tensor_tensor(out=Pall[:, gr, :, :].rearrange("p a b -> p (a b)"), in0=Es[s2][:],
                                                          in1=EBs[:, gr, :, :].rearrange("p a b -> p (a b)"), op=ALU.mult),
                         reads=[B_Es[s2], B_EBs], writes=[B_Pall])

                def s_finish(j):
                    qs_ = slice(j * DS, (j + 1) * DS)
                    pS, BpS = pbanks[5]

                    def snm(e):
                        ins = None
                        for g in range(3):
                            for hs in range(8):
                                rg = hs // 2
                                ins = e.matmul(pS[0:8, g * 64 + hs * 8:g * 64 + hs * 8 + 8], lhsT=KTn[:, g * 4 + rg, qs_],
                                               rhs=QTn[hs % 2][:, g * 4 + rg, qs_], start=True, stop=True)
                        return ins
                    S.op("pe", snm, reads=[B_QKn], writes=[BpS])
                    S.op("act", lambda e: e.activation(out=En[:], in_=pS[0:8, 0:192], func=AF.Exp, scale=0.125),
                         reads=[BpS], writes=[B_En])
                    S.op("dve", lambda e: e.tensor_tensor(out=Pn[:].rearrange("p a b c -> p (a b c)"), in0=En[:],
                                                          in1=EBn[:].rearrange("p a b c -> p (a b c)"), op=ALU.mult),
                         reads=[B_En, B_EBn], writes=[B_Pn])
                    pA, BpA = pbanks[6]
                    pB, BpB = pbanks[7]
                    for hs in range(8):
                        po_t, Bpo = (pA, BpA) if hs < 4 else (pB, BpB)
                        oc = (hs % 4) * (HD + 1)

                        def pvs(e):
                            ins = None
                            for gr, (g, r) in enumerate(CLASSES):
                                ins = e.matmul(po_t[0:8, oc:oc + HD + 1], lhsT=Pall[:, gr, hs, :], rhs=Vca[:, gr, hs, :],
                                               start=(gr == 0), stop=False)
                            for g in range(3):
                                ins = e.matmul(po_t[0:8, oc:oc + HD + 1], lhsT=Pn[:, g, hs, :], rhs=Vn[:, g * 8 + hs, :],
                                               start=False, stop=(g == 2))
                            return ins
                        S.op("pe", pvs, reads=[B_Pall, B_Vca, B_Pn, B_Vn], writes=[Bpo])
                    S.op("act", lambda e: e.copy(out=obs[:, 0:4 * (HD + 1)], in_=pA[0:8, 0:4 * (HD + 1)]), reads=[BpA], writes=[B_obs])
                    S.op("act", lambda e: e.copy(out=obs[:, 4 * (HD + 1):OW], in_=pB[0:8, 0:4 * (HD + 1)]), reads=[BpB], writes=[B_obs])
                    S.dma("sp", [(osc[0, T + j * DS:T + (j + 1) * DS, :], obs[:])], B_obs, reads=[B_obs], writes=[B_oscs])
                sitems = []
                for j in range(NS):
                    sitems.append((s_start, (j,)))
                    for gr in range(13):
                        sitems.append((s_unit, (j, gr)))
                    sitems.append((s_finish, (j,)))
                sitems.reverse()

                woB = sbt(pd, "woB", [128, 4, D], BF16)
                B_woB = S.buf("woB")
                S.dma("pool", [(woB[:], w_out_b.rearrange("(k p) c -> p k c", p=128))], B_woB, writes=[B_woB])
                o3 = [[sbt(pd, "o3_%d_%d" % (i, g), [128, OW]) for g in range(3)] for i in range(3)]
                B_o3 = [[S.buf("o3_%d_%d" % (i, g)) for g in range(3)] for i in range(3)]
                zt = [sbt(pd, "zt%d" % i, [128, 512], BF16) for i in range(3)]
                B_zt = [S.buf("zt%d" % i) for i in range(3)]
                x1t = [sbt(pd, "x1t%d" % i, [128, D]) for i in range(4)]
                B_x1t = [S.buf("x1t%d" % i) for i in range(4)]
                rden = [sbt(pd, "rden%d" % i, [128, 8]) for i in range(2)]
                B_rden = [S.buf() for _ in range(2)]
                om = [sbt(pd, "om%d" % i, [128, 512]) for i in range(2)]
                B_om = [S.buf() for _ in range(2)]
                og = [sbt(pd, "og%d" % i, [128, 512], BF16) for i in range(2)]
                B_og = [S.buf() for _ in range(2)]
                ogT = [sbt(pd, "ogT%d" % i, [128, 4, 128], BF16) for i in range(2)]
                B_ogT = [S.buf() for _ in range(2)]
                yt = [sbt(pd, "yt%d" % i, [128, D]) for i in range(2)]
                B_yt = [S.buf("yt%d" % i) for i in range(2)]

                def d_load(i):
                    rows, c0 = tile_rows(i)
                    s3, s4 = i % 3, i % 4
                    ng = 3 if i < NT else 1
                    for g in range(ng):
                        S.dma("sp", [(o3[s3][g][:rows], osc[g, c0:c0 + rows, :])], B_o3[s3][g], writes=[B_o3[s3][g]],
                              reads=([B_oscs] if i >= NT else []))
                    S.dma("sp", [(zt[s3][:rows], zss[c0:c0 + rows, :])], B_zt[s3], writes=[B_zt[s3]])
                    S.dma("sp", [(x1t[s4][:rows], x1s[c0:c0 + rows, :])], B_x1t[s4], writes=[B_x1t[s4]])

                def d_s0(i):
                    rows, c0 = tile_rows(i)
                    s2, s3 = i % 2, i % 3
                    if i < NT:
                        S.op("dve", lambda e: e.tensor_tensor(out=o3[s3][0][:rows], in0=o3[s3][0][:rows], in1=o3[s3][1][:rows], op=ALU.add),
                             reads=[B_o3[s3][0], B_o3[s3][1]], writes=[B_o3[s3][0]])
                        S.op("dve", lambda e: e.tensor_tensor(out=o3[s3][0][:rows], in0=o3[s3][0][:rows], in1=o3[s3][2][:rows], op=ALU.add),
                             reads=[B_o3[s3][0], B_o3[s3][2]], writes=[B_o3[s3][0]])
                    ov = o3[s3][0][:rows].rearrange("p (h d) -> p h d", d=HD + 1)
                    S.op("dve", lambda e: e.reciprocal(out=rden[s2][:rows].unsqueeze(2), in_=ov[:, :, HD:HD + 1]),
                         reads=[B_o3[s3][0]], writes=[B_rden[s2]])
                    S.op("dve", lambda e: e.tensor_tensor(out=om[s2][:rows].rearrange("p (h d) -> p h d", d=HD), in0=ov[:, :, 0:HD],
                                                          in1=rden[s2][:rows].unsqueeze(2).to_broadcast([rows, 8, HD]), op=ALU.mult),
                         reads=[B_o3[s3][0], B_rden[s2]], writes=[B_om[s2]])
                    S.op("dve", lambda e: e.tensor_tensor(out=og[s2][:rows], in0=om[s2][:rows], in1=zt[s3][:rows], op=ALU.mult),
                         reads=[B_om[s2], B_zt[s3]], writes=[B_og[s2]])

                drr = [0]

                def dbank():
                    t_, b_ = pbanks[drr[0] % 4]
                    drr[0] += 1
                    return t_, b_

                def d_s1(i):
                    rows, c0 = tile_rows(i)
                    s2 = i % 2
                    pb, Bpb = dbank()
                    pbb = pb[:].bitcast(BF16)

                    def tro(e):
                        ins = None
                        for kc in range(4):
                            ins = e.transpose(out=pbb[:, kc * 128:kc * 128 + rows], in_=og[s2][:rows, kc * 128:(kc + 1) * 128],
                                              identity=identb[:rows, :rows])
                        return ins
                    S.op("pe", tro, reads=[B_og[s2], B_identb], writes=[Bpb])
                    S.op("act", lambda e: e.copy(out=ogT[s2][:, :, :rows], in_=pbb[:, 0:512].rearrange("p (k t) -> p k t", t=128)[:, :, :rows]),
                         reads=[Bpb], writes=[B_ogT[s2]])

                def d_s2(i):
                    rows, c0 = tile_rows(i)
                    s2, s4 = i % 2, i % 4
                    for half in range(2):
                        pb, Bpb = dbank()

                        def ym2(e):
                            ins = None
                            for kc in range(4):
                                ins = e.matmul(pb[:rows, :], lhsT=ogT[s2][:, kc, :rows], rhs=woB[:, kc, half * 512:(half + 1) * 512],
                                               start=(kc == 0), stop=(kc == 3))
                            return ins
                        S.op("pe", ym2, reads=[B_ogT[s2], B_woB], writes=[Bpb])
                        S.op("dve" if half == 0 else "act", lambda e: (e.tensor_tensor(
                            out=yt[s2][:rows, half * 512:(half + 1) * 512], in0=pb[:rows, :],
                            in1=x1t[s4][:rows, half * 512:(half + 1) * 512], op=ALU.add)),
                            reads=[Bpb, B_x1t[s4]], writes=[B_yt[s2]]) if half == 0 else S.op("dve", lambda e: e.tensor_tensor(
                            out=yt[s2][:rows, half * 512:(half + 1) * 512], in0=pb[:rows, :],
                            in1=x1t[s4][:rows, half * 512:(half + 1) * 512], op=ALU.add),
                            reads=[Bpb, B_x1t[s4]], writes=[B_yt[s2]])
                    dst = y_p[c0:c0 + rows, :] if i < NT else y_s[:, :]
                    S.dma("sp", [(dst, yt[s2][:rows])], B_yt[s2], reads=[B_yt[s2]])
                stages = [d_load, d_s0, d_s1, d_s2]
                per_step = -(-len(sitems) // max(1, NTT - 4))
                for step in range(NTT + len(stages) - 1):
                    for _ in range(per_step):
                        if sitems:
                            fn_, args_ = sitems.pop()
                            fn_(*args_)
                    if step == NT:
                        while sitems:
                            fn_, args_ = sitems.pop()
                            fn_(*args_)
                    for st in range(len(stages) - 1, -1, -1):
                        i = step - st
                        if 0 <= i < NTT:
                            stages[st](i)
        ps_.close()
        S.finish()
        build.stats = dict(cnt=dict(S.cnt), nsem=len(S.dbufs_all) + 5, maxd=max([c for _, c in S.free_sems] + [b.dcnt for b in S.dbufs] + [0]))
    return nc, dbg_out


T_FULL = 4096
N_CORES = 8
_CACHE = {}


def _get_nc(T):
    if T not in _CACHE:
        _CACHE[T] = build(T)[0]
    return _CACHE[T]


def make_in_maps(T, x_prompt, x_sample, state_mlstm_C, state_mlstm_n, state_mlstm_m,
                 cache_kv_w128, cache_kv_w512, cache_kv_w2048,
                 norm_a, w_in_a, b_gates_a, hnorm_a, w_out_a, norm_kv, w_kv, k_norm,
                 norm_b, w_in_b, q_norm, rel_bias, w_out_b):
    f = lambda a: np.ascontiguousarray(np.asarray(a, dtype=np.float32))
    consts = make_consts(T)
    shared = dict(
        norm_a=f(norm_a).reshape(1, D), w_in_a=f(w_in_a)[0], b_gates=f(b_gates_a).reshape(1, 2 * H),
        hnorm=f(hnorm_a).reshape(DI), w_out_a=f(w_out_a)[0], norm_kv=f(norm_kv), w_kv=f(w_kv),
        k_norm=f(k_norm).reshape(1, HD), norm_b=f(norm_b).reshape(D), w_in_b=f(w_in_b)[0],
        q_norm=f(q_norm).reshape(1, HD), rel_bias=f(rel_bias), w_out_b=f(w_out_b)[0])
    shared.update(consts)
    xp = f(x_prompt)
    xs = f(x_sample)
    sC = f(state_mlstm_C)[0]
    sn_ = f(state_mlstm_n)[0]
    sm_ = f(state_mlstm_m)[0]
    c1, c5, c20 = f(cache_kv_w128), f(cache_kv_w512), f(cache_kv_w2048)
    maps = []
    for c in range(xp.shape[0]):
        sl = slice(c * NS, (c + 1) * NS)
        m = dict(shared)
        m.update(xp=xp[c], xs=xs[sl].reshape(NST, D), sC=sC[sl], sn=sn_[sl], sm=sm_[sl],
                 c128=c1[sl], c512=c5[sl], c2048=c20[sl])
        maps.append(m)
    return maps


def gather(results, T):
    n = len(results)
    cat = lambda k: np.stack([np.asarray(r[k], np.float32) for r in results], 0)
    y_p = cat("y_p")
    y_s = cat("y_s").reshape(n * NS, DS, D)
    C_p = cat("C_p")[None]
    n_p = cat("n_p")[None]
    m_p = cat("m_p").reshape(n, H)[None]
    C_s = cat("C_s").reshape(n * NS, H, DH, DH)[None]
    n_s = cat("n_s").reshape(n * NS, H, DH)[None]
    m_s = cat("m_s").reshape(n * NS, H)[None]
    kv = [cat(k) for k in ("kv128_p", "kv512_p", "kv2048_p")]
    kvs_ = [cat(k).reshape(n * NS, DS, 2, 8, HD) for k in ("kv128_s", "kv512_s", "kv2048_s")]
    return (y_p, y_s, C_p, n_p, m_p, C_s, n_s, m_s, kv[0], kv[1], kv[2], kvs_[0], kvs_[1], kvs_[2])


def kernel(**inputs):
    T = int(np.asarray(inputs["x_prompt"]).shape[1])
    n = int(np.asarray(inputs["x_prompt"]).shape[0])
    maps = make_in_maps(T, **inputs)
    nc = _get_nc(T)
    res = run_bass_kernel_spmd(nc, maps, core_ids=list(range(n)))
    return gather(res.results, T)
```

```python
import contextlib
import math
import numpy as np
import concourse.bass as bass
import concourse.mybir as mybir
from concourse.bass_utils import run_bass_kernel_spmd

F32 = mybir.dt.float32
BF16 = mybir.dt.bfloat16
AF = mybir.ActivationFunctionType
ALU = mybir.AluOpType
AX = mybir.AxisListType

D = 1024
DI = 2048
H = 4
DH = 512
NQH = 24
HD = 64
QW = 1536
NS = 4
DS = 8
NST = NS * DS
EPS = 1e-6
GROUPS = ((128, 1), (512, 4), (2048, 16))
NBUCK = 32
MAXDIST = 2048


class Buf:
    __slots__ = ("name", "w", "r", "dsem", "dcnt")

    def __init__(self, name):
        self.name = name
        self.w = []
        self.r = []
        self.dsem = None
        self.dcnt = 0


class Sched:
    ENG = ("pe", "act", "dve", "pool", "sp")

    def __init__(self, nc, stack):
        self.nc = nc
        self.stack = stack
        self.sem = {}
        self.cnt = {}
        self.waited = {}
        for e in self.ENG:
            self.sem[e] = stack.enter_context(nc.semaphore("sem_" + e))
            self.cnt[e] = 0
            self.waited[e] = {}
        self.semeng = {id(self.sem[e]): e for e in self.ENG}
        self.hw = {"pe": nc.tensor, "act": nc.scalar, "dve": nc.vector, "pool": nc.gpsimd, "sp": nc.sync}
        self.dbufs = []
        self.dbufs_all = []
        self.free_sems = []
        self.nb = 0

    def buf(self, name=None):
        self.nb += 1
        return Buf(name or ("b%d" % self.nb))

    def _emit(self, eng, waits, fn, inc):
        engine = self.hw[eng]
        for (s_, v) in waits:
            engine.wait_ge(s_, v)
        if fn is not None:
            ins = fn(engine)
            if inc is not None:
                ins.then_inc(inc[0], inc[1])

    def _waits(self, eng, deps):
        need = {}
        for (s, v) in deps:
            k = id(s)
            if self.semeng.get(k) == eng and eng in ("pe", "sp"):
                continue
            if self.waited[eng].get(k, 0) >= v:
                continue
            if k not in need or need[k][1] < v:
                need[k] = (s, v)
        out = []
        for k, (s, v) in need.items():
            self.waited[eng][k] = v
            out.append((s, v))
        return out

    def _deps(self, reads, writes):
        deps = []
        for b in reads:
            deps += b.w
        for b in writes:
            deps += b.w
            deps += b.r
        return deps

    def op(self, eng, fn, reads=(), writes=()):
        waits = self._waits(eng, self._deps(reads, writes))
        self.cnt[eng] += 1
        tok = (self.sem[eng], self.cnt[eng])
        self._emit(eng, waits, fn, (self.sem[eng], 1))
        for b in reads:
            b.r.append(tok)
        for b in writes:
            b.w = [tok]
            b.r = []
        return tok

    def dma(self, eng, pairs, owner, reads=(), writes=(), **kw):
        if owner.dsem is None:
            if self.free_sems:
                owner.dsem, owner.dcnt = self.free_sems.pop()
            else:
                owner.dsem = self.stack.enter_context(self.nc.semaphore("dsem_%d" % len(self.dbufs_all)))
                owner.dcnt = 0
            self.dbufs.append(owner)
            self.dbufs_all.append(owner)
        deps = self._deps(reads, writes)
        if owner.dcnt:
            deps.append((owner.dsem, owner.dcnt))
        waits = self._waits(eng, deps)
        for i, (o, i_) in enumerate(pairs):
            def fn(e, o=o, i_=i_):
                return e.dma_start(out=o, in_=i_, **kw)
            self._emit(eng, waits if i == 0 else [], fn, (owner.dsem, 16))
        owner.dcnt += 16 * len(pairs)
        tok = (owner.dsem, owner.dcnt)
        for b in reads:
            b.r.append(tok)
        for b in writes:
            b.w = [tok]
            b.r = []
        return tok

    def barrier(self):
        toks = [(self.sem[e], self.cnt[e]) for e in self.ENG if self.cnt[e]]
        for b in self.dbufs:
            if b.dcnt:
                toks.append((b.dsem, b.dcnt))
        for e in self.ENG:
            w = self._waits(e, [t for t in toks if self.semeng.get(id(t[0])) != e])
            if w:
                self._emit(e, w, None, None)
        for b in self.dbufs:
            self.free_sems.append((b.dsem, b.dcnt))
            b.dsem = None
            b.dcnt = 0
        self.dbufs = []

    def finish(self):
        toks = [(self.sem[e], self.cnt[e]) for e in self.ENG if self.cnt[e] and e != "sp"]
        for b in self.dbufs:
            if b.dcnt:
                toks.append((b.dsem, b.dcnt))
        w = self._waits("sp", toks)
        self._emit("sp", w, None, None)


def _t5_bucket_np(dist):
    exact = NBUCK // 2
    d = np.maximum(dist, 1).astype(np.float32)
    large = exact + (np.log(d / np.float32(exact)) / np.float32(math.log(MAXDIST / exact))
                     * np.float32(NBUCK - exact)).astype(np.int32)
    return np.where(dist < exact, dist, np.minimum(large, NBUCK - 1))


def make_consts(T):
    NC = T // 128
    TT = T + NST
    NCH = NC + NS
    c = {}
    c["c_ident"] = np.eye(128, dtype=np.float32)
    c["c_J"] = np.eye(128, dtype=np.float32)[::-1].copy()
    s_ = np.arange(128)[:, None]
    t_ = np.arange(128)[None, :]
    c["c_maskT"] = (s_ <= t_).astype(np.float32)
    rm = np.ones((4, TT), np.float32)
    rm[:, 0:T:128] = 0.0
    rm[:, T:TT:DS] = 0.0
    c["c_rm"] = rm
    dm = np.zeros((4, 4, NCH), np.float32)
    for h in range(4):
        dm[h, h, :] = 1.0
    c["c_dmask"] = dm.reshape(4, 4 * NCH)
    ohp = np.zeros((NBUCK, 3, 384), np.float32)
    for g, (win, dil) in enumerate(GROUPS):
        J = win // dil
        for j in range(384):
            delta = j - 127
            if 0 <= delta <= J:
                b = int(_t5_bucket_np(np.array([delta * dil]))[0])
                ohp[b, g, j] = 1.0
    c["c_ohp"] = ohp.reshape(NBUCK, 3 * 384)
    ohs = np.zeros((NBUCK, 13, 8, 128), np.float32)
    ohn = np.zeros((NBUCK, 3, 8, 8), np.float32)
    gr = 0
    for g, (win, dil) in enumerate(GROUPS):
        L = win
        J = win // dil
        for r in range(dil if g < 2 else 8):
            for s in range(8):
                if s % dil != r % dil:
                    continue
                if g == 2 and s != r:
                    continue
                for i in range(128):
                    row = r + dil * i
                    num = L + s - row
                    if num < 0 or num % dil:
                        continue
                    j = num // dil
                    if 0 <= j <= J:
                        b = int(_t5_bucket_np(np.array([dil * j]))[0])
                        ohs[b, gr, s, i] = 1.0
            gr += 1
        for s in range(8):
            for k in range(8):
                num = s - k
                if num < 0 or num % dil:
                    continue
                j = num // dil
                if j <= J:
                    b = int(_t5_bucket_np(np.array([dil * j]))[0])
                    ohn[b, g, s, k] = 1.0
    assert gr == 13
    c["c_ohs"] = ohs.reshape(NBUCK, 13 * 8 * 128)
    c["c_ohn"] = ohn.reshape(NBUCK, 3 * 8 * 8)
    return c


CLASSES = [(0, 0)] + [(1, r) for r in range(4)] + [(2, r) for r in range(8)]


def build(T, phases="ABCD", dbg=()):
    NT = T // 128
    NC = NT
    TT = T + NST
    NCH = NC + NS
    NTT = NT + 1
    nc = bass.Bass("TRN2", target_bir_lowering=False)

    def din(name, shape, dt=F32):
        return nc.dram_tensor(name, list(shape), dt, kind="ExternalInput").ap()

    def dout(name, shape, dt=F32):
        return nc.dram_tensor(name, list(shape), dt, kind="ExternalOutput").ap()

    def dscr(name, shape, dt):
        return nc.dram_tensor(name, list(shape), dt, kind="Internal").ap()

    xp = din("xp", [T, D])
    xs = din("xs", [NST, D])
    sC = din("sC", [NS, H, DH, DH])
    sn = din("sn", [NS, H, DH])
    sm = din("sm", [NS, H])
    caches = [din("c128", [NS, 128, 2, 8, HD]), din("c512", [NS, 512, 2, 8, HD]), din("c2048", [NS, 2048, 2, 8, HD])]
    norm_a = din("norm_a", [1, D])
    w_in_a = din("w_in_a", [D, 5 * DI + 2 * H])
    b_gates = din("b_gates", [1, 2 * H])
    hnorm = din("hnorm", [DI])
    w_out_a = din("w_out_a", [DI, D])
    norm_kv = din("norm_kv", [D])
    w_kv = din("w_kv", [D, 2 * QW])
    k_norm = din("k_norm", [1, HD])
    norm_b = din("norm_b", [D])
    w_in_b = din("w_in_b", [D, QW + 512])
    q_norm = din("q_norm", [1, HD])
    rel_bias = din("rel_bias", [NBUCK, NQH])
    w_out_b = din("w_out_b", [512, D])
    c_ident = din("c_ident", [128, 128])
    c_J = din("c_J", [128, 128])
    c_maskT = din("c_maskT", [128, 128])
    c_rm = din("c_rm", [4, TT])
    c_dmask = din("c_dmask", [4, 4 * NCH])
    c_ohp = din("c_ohp", [NBUCK, 3 * 384])
    c_ohs = din("c_ohs", [NBUCK, 13 * 8 * 128])
    c_ohn = din("c_ohn", [NBUCK, 3 * 8 * 8])

    y_p = dout("y_p", [T, D])
    y_s = dout("y_s", [NST, D])
    C_p = dout("C_p", [H, DH, DH])
    n_p = dout("n_p", [H, DH])
    m_p = dout("m_p", [H, 1])
    C_s = dout("C_s", [NS, H, DH, DH])
    n_s = dout("n_s", [NS, H, DH])
    m_s = dout("m_s", [NS, H])
    kvp = [dout("kv128_p", [min(128, T), 2, 8, HD]), dout("kv512_p", [min(512, T), 2, 8, HD]),
           dout("kv2048_p", [min(2048, T), 2, 8, HD])]
    kvs = [dout("kv128_s", [NST, 2, 8, HD]), dout("kv512_s", [NST, 2, 8, HD]), dout("kv2048_s", [NST, 2, 8, HD])]

    hf = dscr("hf_scr", [TT, DI], BF16)
    x1s = dscr("x1_scr", [TT, D], F32)
    KTs = dscr("KT_scr", [QW, TT], BF16)
    QTs = dscr("QT_scr", [QW, TT], BF16)
    QTse = dscr("QTe_scr", [QW, TT], BF16)
    QTso = dscr("QTo_scr", [QW, TT], BF16)
    Vs = dscr("V_scr", [TT, NQH, HD + 1], BF16)
    zss = dscr("zs_scr", [TT, 512], BF16)
    osc = dscr("o_scr", [3, TT, 8 * (HD + 1)], F32)
    vecs = dscr("vec_scr", [NQH, 384], F32)

    dbg_out = {}

    with contextlib.ExitStack() as top:
        S = Sched(nc, top)

        def sbt(stack, name, shape, dt=F32):
            return stack.enter_context(nc.sbuf_tensor(name, list(shape), dt))

        psall = top.enter_context(nc.psum_tensor("psall", [128, 8 * 512], F32))
        pbanks = []
        for i in range(8):
            pbanks.append((psall[:, i * 512:(i + 1) * 512], S.buf("pb%d" % i)))
        rr = [0]

        def bank():
            t, b = pbanks[rr[0] % 7]
            rr[0] += 1
            return t, b
        psm = pbanks[7][0]
        B_psS = B_psD = B_psn = pbanks[7][1]

        identf = sbt(top, "identf", [128, 128])
        identb = sbt(top, "identb", [128, 128], BF16)
        cm05 = sbt(top, "cm05", [128, 1])
        B_const = S.buf("const")
        S.dma("sp", [(identf[:], c_ident[:, :])], B_const, writes=[B_const])
        B_identb = S.buf("identb")
        S.op("dve", lambda e: e.tensor_copy(out=identb[:], in_=identf[:]), reads=[B_const], writes=[B_identb])
        B_cm05 = S.buf("cm05")
        S.op("pool", lambda e: e.memset(cm05[:], -0.5), writes=[B_cm05])
        ZW = 516
        assert TT % ZW == 0 or True
        zt_ = sbt(top, "zeroT", [64, ZW], BF16)
        B_zt_ = S.buf("zeroT")
        S.op("pool", lambda e: e.memset(zt_[:], 0.0), writes=[B_zt_])
        nz = TT // ZW
        rem = TT - nz * ZW
        zp = []
        for (dst_, lo) in ((QTse, 64), (QTso, 0)):
            v_ = dst_.rearrange("(rg p) t -> p rg t", p=128)[lo:lo + 64, :, :]
            for rg_ in range(12):
                if nz:
                    zp.append((v_[:, rg_, 0:nz * ZW].rearrange("p (a b) -> p a b", b=ZW),
                               zt_[:].unsqueeze(1).to_broadcast([64, nz, ZW])))
            if rem:
                zp.append((v_[:, :, nz * ZW:TT], zt_[:, 0:rem].unsqueeze(1).to_broadcast([64, 12, rem])))
        zero_fill = [(lambda pr=pr: S.dma("sp", [pr], B_zt_, reads=[B_zt_])) for pr in zp]
        if "A" not in phases:
            while zero_fill:
                zero_fill.pop()()

        def dump(name, ap_sb, shape, owner, dt=F32):
            o = dout("dbg_" + name, shape, dt)
            dbg_out[name] = o
            S.dma("sp", [(o, ap_sb)], owner, reads=[owner])

        def rsqrt_small(rows, out_ap, in_ap, scale, add, B_in, B_out, tmp_ap, B_tmp):
            S.op("dve", lambda e: e.tensor_scalar(out=tmp_ap, in0=in_ap, scalar1=scale, scalar2=add,
                                                  op0=ALU.mult, op1=ALU.add), reads=[B_in], writes=[B_tmp])
            S.op("pool", lambda e: e.tensor_tensor(out=out_ap, in0=tmp_ap, in1=cm05[:rows], op=ALU.pow),
                 reads=[B_tmp, B_cm05], writes=[B_out])

        if "A" in phases:
            with contextlib.ExitStack() as pa:
                xnT = sbt(pa, "xnT", [128, 8, TT], BF16)
                B_xnT = S.buf("xnT")
                uf_tok = sbt(pa, "uf_tok", [128, NTT * 8])
                ub_tok = sbt(pa, "ub_tok", [128, NTT * 8], BF16)
                ufs = sbt(pa, "ufs", [8, NS * 8])
                ubs = sbt(pa, "ubs", [8, NS * 8], BF16)
                a_bc = sbt(pa, "a_bc", [128, 4 * NCH])
                B_uf = S.buf("uf")
                B_abc = S.buf("abc")
                maskT = sbt(pa, "maskT", [128, 128])
                B_maskT = S.buf("maskT")
                S.dma("sp", [(maskT[:], c_maskT[:, :])], B_maskT, writes=[B_maskT])

                with contextlib.ExitStack() as p0:
                    g_bc = sbt(p0, "g_bc", [128, D])
                    B_gbc = S.buf("gbc")
                    S.dma("sp", [(g_bc[:], norm_a[0:1, :].to_broadcast([128, D]))], B_gbc, writes=[B_gbc])
                    xt = [sbt(p0, "xt%d" % i, [128, D]) for i in range(3)]
                    B_xt = [S.buf() for _ in range(3)]
                    xn = [sbt(p0, "xn%d" % i, [128, D], BF16) for i in range(2)]
                    B_xn = [S.buf() for _ in range(2)]
                    junk = sbt(p0, "junk0", [128, D], BF16)
                    B_junk = S.buf()
                    smt = sbt(p0, "smt0", [128, 9])
                    B_ss = [S.buf() for _ in range(3)]
                    B_tt = [S.buf() for _ in range(3)]
                    B_rs = [S.buf() for _ in range(3)]
                    a0bank = {}

                    def a0_ld(i):
                        rows = 128 if i < NT else NST
                        src = xp[i * 128:(i + 1) * 128, :] if i < NT else xs[:, :]
                        s3 = i % 3
                        S.dma("sp", [(xt[s3][:rows], src)], B_xt[s3], writes=[B_xt[s3]])

                    def a0_s0(i):
                        rows = 128 if i < NT else NST
                        s3 = i % 3
                        S.op("act", lambda e: e.activation(
                            out=junk[:rows], in_=xt[s3][:rows], func=AF.Square, accum_out=smt[:rows, s3:s3 + 1]),
                            reads=[B_xt[s3]], writes=[B_junk, B_ss[s3]])
                        rsqrt_small(rows, smt[:rows, 6 + s3:7 + s3], smt[:rows, s3:s3 + 1], 1.0 / D, EPS,
                                    B_ss[s3], B_rs[s3], smt[:rows, 3 + s3:4 + s3], B_tt[s3])

                    def a0_s1(i):
                        rows = 128 if i < NT else NST
                        s3, s2 = i % 3, i % 2
                        S.op("dve", lambda e: e.scalar_tensor_tensor(
                            out=xn[s2][:rows], in0=xt[s3][:rows], scalar=smt[:rows, 6 + s3:7 + s3], in1=g_bc[:rows],
                            op0=ALU.mult, op1=ALU.mult), reads=[B_xt[s3], B_rs[s3], B_gbc], writes=[B_xn[s2]])
                        pb, Bpb = bank()
                        pbb = pb[:].bitcast(BF16)
                        a0bank[i] = (pbb, Bpb)

                        def tr(e):
                            ins = None
                            for kc in range(8):
                                ins = e.transpose(out=pbb[:, kc * 128:kc * 128 + rows],
                                                  in_=xn[s2][:rows, kc * 128:(kc + 1) * 128],
                                                  identity=identb[:rows, :rows])
                            return ins
                        S.op("pe", tr, reads=[B_xn[s2], B_identb], writes=[Bpb])

                    def a0_s2(i):
                        rows = 128 if i < NT else NST
                        c0 = i * 128
                        pbb, Bpb = a0bank.pop(i)
                        S.op("act", lambda e: e.copy(
                            out=xnT[:, :, c0:c0 + rows],
                            in_=pbb.rearrange("p (k t) -> p k t", t=128)[:, :, :rows]),
                            reads=[Bpb], writes=[B_xnT])
                    a0st = [a0_ld, a0_s0, a0_s1, a0_s2]
                    for step in range(NTT + len(a0st) - 1):
                        for st in range(len(a0st) - 1, -1, -1):
                            i = step - st
                            if 0 <= i < NTT:
                                a0st[st](i)

                    Wg = sbt(p0, "Wg", [128, 8, 8], BF16)
                    B_Wg = S.buf("Wg")
                    S.dma("pool", [(Wg[:], w_in_a[:, 5 * DI:5 * DI + 8].rearrange("(k p) c -> p k c", p=128))],
                          B_Wg, writes=[B_Wg])
                    bgt = sbt(p0, "bgt", [4, 2])
                    B_bg = S.buf("bg")
                    S.dma("sp", [(bgt[:, 0:1], b_gates[0:1, 0:4].rearrange("o c -> c o")),
                                 (bgt[:, 1:2], b_gates[0:1, 4:8].rearrange("o c -> c o"))], B_bg, writes=[B_bg])
                    GA = sbt(p0, "GA", [4, TT])
                    GB = sbt(p0, "GB", [4, TT])
                    GC = sbt(p0, "GC", [4, TT])
                    rm = sbt(p0, "rm", [4, TT])
                    B_GA, B_GB, B_GC, B_rm = S.buf("GA"), S.buf("GB"), S.buf("GC"), S.buf("rm")
                    S.dma("sp", [(rm[:], c_rm[:, :])], B_rm, writes=[B_rm])
                    dmk = sbt(p0, "dmk", [4, 4 * NCH])
                    B_dmk = S.buf("dmk")
                    S.dma("sp", [(dmk[:], c_dmask[:, :])], B_dmk, writes=[B_dmk])
                    col = 0
                    while col < TT:
                        w = min(512, TT - col)
                        for which, dst, Bd in ((0, GA, B_GA), (1, GB, B_GB)):
                            pb, Bpb = bank()

                            def gm(e, which=which, col=col, w=w, pb=pb):
                                ins = None
                                for kc in range(8):
                                    ins = e.matmul(pb[0:4, 0:w], lhsT=Wg[:, kc, which * 4:which * 4 + 4],
                                                   rhs=xnT[:, kc, col:col + w], start=(kc == 0), stop=(kc == 7))
                                return ins
                            S.op("pe", gm, reads=[B_Wg, B_xnT], writes=[Bpb])
                            S.op("act", lambda e, which=which, col=col, w=w, pb=pb, dst=dst: e.activation(
                                out=dst[:, col:col + w], in_=pb[0:4, 0:w], func=AF.Identity,
                                bias=bgt[:, which:which + 1]), reads=[Bpb, B_bg], writes=[Bd])
                        col += w
                    S.op("act", lambda e: e.activation(out=GB[:], in_=GB[:], func=AF.Exp, scale=-1.0),
                         reads=[B_GB], writes=[B_GB])
                    S.op("act", lambda e: e.activation(out=GB[:], in_=GB[:], func=AF.Ln, bias=1.0),
                         reads=[B_GB], writes=[B_GB])
                    S.op("dve", lambda e: e.tensor_tensor_scan(out=GC[:], data0=rm[:], data1=GB[:], initial=0.0,
                                                               op0=ALU.mult, op1=ALU.add),
                         reads=[B_rm, B_GB], writes=[B_GC])
                    S.op("dve", lambda e: e.tensor_tensor(out=GA[:], in0=GA[:], in1=GC[:], op=ALU.add),
                         reads=[B_GA, B_GC], writes=[B_GA])
                    gsm = sbt(p0, "gsm", [4, 8 * NCH + 8])
                    B_gsm = S.buf("gsm")
                    Acol = gsm[:, 0:NCH]
                    bLc = gsm[:, NCH:2 * NCH]
                    mcol = gsm[:, 2 * NCH:3 * NCH]
                    Mcol = gsm[:, 3 * NCH:4 * NCH]
                    mpr = gsm[:, 4 * NCH:5 * NCH]
                    acol = gsm[:, 5 * NCH:6 * NCH]
                    m0T = gsm[:, 6 * NCH:6 * NCH + NS]
                    S.dma("sp", [(m0T, sm.rearrange("s h -> h s"))], B_gsm, writes=[B_gsm],
                          allow_slow_non_contiguous=True)
                    S.op("dve", lambda e: e.tensor_reduce(out=gsm[:, 0:NC], in_=GA[:, 0:T].rearrange("p (c k) -> p c k", k=128),
                                                          axis=AX.X, op=ALU.max), reads=[B_GA], writes=[B_gsm])
                    S.op("dve", lambda e: e.tensor_reduce(out=gsm[:, NC:NCH], in_=GA[:, T:TT].rearrange("p (c k) -> p c k", k=DS),
                                                          axis=AX.X, op=ALU.max), reads=[B_GA], writes=[B_gsm])
                    S.op("dve", lambda e: e.tensor_copy(out=gsm[:, NCH:NCH + NC],
                                                        in_=GC[:, 0:T].rearrange("p (c k) -> p c k", k=128)[:, :, 127]),
                         reads=[B_GC], writes=[B_gsm])
                    S.op("dve", lambda e: e.tensor_copy(out=gsm[:, NCH + NC:2 * NCH],
                                                        in_=GC[:, T:TT].rearrange("p (c k) -> p c k", k=DS)[:, :, DS - 1]),
                         reads=[B_GC], writes=[B_gsm])
                    S.op("dve", lambda e: e.tensor_tensor_scan(out=gsm[:, 2 * NCH:2 * NCH + NC], data0=gsm[:, 0:NC],
                                                               data1=gsm[:, NCH:NCH + NC], initial=0.0,
                                                               op0=ALU.max, op1=ALU.subtract),
                         reads=[B_gsm], writes=[B_gsm])
                    S.op("dve", lambda e: e.memset(gsm[:, 4 * NCH:4 * NCH + 1], 0.0), reads=[B_gsm], writes=[B_gsm])
                    if NC > 1:
                        S.op("dve", lambda e: e.tensor_copy(out=gsm[:, 4 * NCH + 1:4 * NCH + NC],
                                                            in_=gsm[:, 2 * NCH:2 * NCH + NC - 1]),
                             reads=[B_gsm], writes=[B_gsm])
                    S.op("dve", lambda e: e.tensor_copy(out=gsm[:, 4 * NCH + NC:5 * NCH], in_=m0T),
                         reads=[B_gsm], writes=[B_gsm])
                    S.op("dve", lambda e: e.tensor_tensor(out=Mcol, in0=mpr, in1=Acol, op=ALU.max),
                         reads=[B_gsm], writes=[B_gsm])
                    S.op("dve", lambda e: e.tensor_tensor(out=gsm[:, 2 * NCH + NC:3 * NCH], in0=gsm[:, 3 * NCH + NC:4 * NCH],
                                                          in1=gsm[:, NCH + NC:2 * NCH], op=ALU.subtract),
                         reads=[B_gsm], writes=[B_gsm])
                    S.op("dve", lambda e: e.tensor_tensor(out=acol, in0=mpr, in1=Mcol, op=ALU.subtract),
                         reads=[B_gsm], writes=[B_gsm])
                    S.op("act", lambda e: e.activation(out=acol, in_=acol, func=AF.Exp), reads=[B_gsm], writes=[B_gsm])
                    S.dma("sp", [(m_p[:, :], gsm[:, 2 * NCH + NC - 1:2 * NCH + NC])], B_gsm, reads=[B_gsm])
                    S.dma("sp", [(m_s.rearrange("s h -> h s"), gsm[:, 2 * NCH + NC:3 * NCH])], B_gsm, reads=[B_gsm],
                          allow_slow_non_contiguous=True)
                    for (dst, Bd, srcg, Bs) in ((GB, B_GB, GA, B_GA), (GC, B_GC, GC, B_GC)):
                        S.op("dve", lambda e, dst=dst, srcg=srcg: e.tensor_tensor(
                            out=dst[:, 0:T].rearrange("p (c k) -> p c k", k=128),
                            in0=srcg[:, 0:T].rearrange("p (c k) -> p c k", k=128),
                            in1=gsm[:, 3 * NCH:3 * NCH + NC].unsqueeze(2).to_broadcast([4, NC, 128]), op=ALU.subtract),
                            reads=[Bs, B_gsm], writes=[Bd])
                        S.op("dve", lambda e, dst=dst, srcg=srcg: e.tensor_tensor(
                            out=dst[:, T:TT].rearrange("p (c k) -> p c k", k=DS),
                            in0=srcg[:, T:TT].rearrange("p (c k) -> p c k", k=DS),
                            in1=gsm[:, 3 * NCH + NC:4 * NCH].unsqueeze(2).to_broadcast([4, NS, DS]), op=ALU.subtract),
                            reads=[Bs, B_gsm], writes=[Bd])
                        S.op("act", lambda e, dst=dst: e.activation(out=dst[:], in_=dst[:], func=AF.Exp),
                             reads=[Bd], writes=[Bd])
                    pb, Bpb = bank()

                    def trg(e, pb=pb):
                        ins = None
                        for i in range(NTT):
                            rows = 128 if i < NT else NST
                            for k, srcg in enumerate((GB, GC)):
                                ins = e.matmul(pb[:rows, i * 8 + 4 * k:i * 8 + 4 * k + 4],
                                               lhsT=srcg[:, i * 128:i * 128 + rows], rhs=identf[0:4, 0:4],
                                               start=True, stop=True)
                        return ins
                    S.op("pe", trg, reads=[B_GB, B_GC, B_const], writes=[Bpb])
                    S.op("dve", lambda e, pb=pb: e.tensor_copy(out=uf_tok[:], in_=pb[:, 0:NTT * 8]), reads=[Bpb], writes=[B_uf])
                    S.op("dve", lambda e: e.tensor_copy(out=ub_tok[:], in_=uf_tok[:]), reads=[B_uf], writes=[B_uf])
                    pb, Bpb = bank()

                    def trs(e, pb=pb):
                        ins = None
                        for j in range(NS):
                            for k, srcg in enumerate((GB, GC)):
                                ins = e.matmul(pb[0:DS, j * 8 + 4 * k:j * 8 + 4 * k + 4],
                                               lhsT=srcg[:, T + j * DS:T + (j + 1) * DS], rhs=identf[0:4, 0:4],
                                               start=True, stop=True)
                        return ins
                    S.op("pe", trs, reads=[B_GB, B_GC, B_const], writes=[Bpb])
                    S.op("dve", lambda e, pb=pb: e.tensor_copy(out=ufs[:], in_=pb[0:DS, 0:NS * 8]), reads=[Bpb], writes=[B_uf])
                    S.op("dve", lambda e: e.tensor_copy(out=ubs[:], in_=ufs[:]), reads=[B_uf], writes=[B_uf])
                    adg = sbt(p0, "adg", [4, 4 * NCH])
                    ones4 = sbt(p0, "ones4", [4, 128])
                    B_adg = S.buf("adg")
                    S.op("dve", lambda e: e.memset(ones4[:], 1.0), writes=[B_adg])
                    S.op("dve", lambda e: e.tensor_tensor(out=adg[:].rearrange("p (a c) -> p a c", a=4),
                                                          in0=acol.unsqueeze(1).to_broadcast([4, 4, NCH]),
                                                          in1=dmk[:].rearrange("p (a c) -> p a c", a=4), op=ALU.mult),
                         reads=[B_gsm, B_dmk, B_adg], writes=[B_adg])
                    pb, Bpb = bank()
                    S.op("pe", lambda e, pb=pb: e.matmul(pb[:, 0:4 * NCH], lhsT=ones4[:, :], rhs=adg[:, :], start=True, stop=True),
                         reads=[B_adg], writes=[Bpb])
                    S.op("dve", lambda e, pb=pb: e.tensor_copy(out=a_bc[:], in_=pb[:, 0:4 * NCH]), reads=[Bpb], writes=[B_abc])
                    if "uf" in dbg:
                        dump("uf", uf_tok[:], [128, NTT * 8], B_uf)
                        dump("abc", a_bc[:], [128, 4 * NCH], B_abc)
                        dump("ufs", ufs[:], [8, NS * 8], B_uf)
                    S.barrier()

                with contextlib.ExitStack() as p1:
                    W5 = [sbt(p1, "W5_%d" % i, [128, 8, 5, 512], BF16) for i in range(2)]
                    B_W = [[S.buf("W5_%d_%d" % (i, j)) for j in range(5)] for i in range(2)]
                    qT = [sbt(p1, "qT%d" % i, [128, 4, 512], BF16) for i in range(2)]
                    kT = [sbt(p1, "kT%d" % i, [128, 4, 512], BF16) for i in range(2)]
                    B_qT = [S.buf() for _ in range(2)]
                    B_kT = [S.buf() for _ in range(2)]
                    Cst = sbt(p1, "Cst", [128, 4, 512])
                    Css = sbt(p1, "Css", [128, 4, 512])
                    B_C = [S.buf() for _ in range(4)]
                    B_Cs = [S.buf() for _ in range(4)]
                    nst = sbt(p1, "nst", [128, 4])
                    nss = sbt(p1, "nss", [128, 4])
                    B_n = S.buf("n")
                    B_ns = S.buf("ns")
                    Cbf = sbt(p1, "Cbf", [128, 4, 512], BF16)
                    nbf = [sbt(p1, "nbf%d" % i, [128, 4], BF16) for i in range(2)]
                    B_Cbf = S.buf("Cbf")
                    B_nbf = [S.buf() for _ in range(2)]
                    vaug = [sbt(p1, "vaug%d" % i, [128, 512], BF16) for i in range(2)]
                    so = [sbt(p1, "so%d" % i, [128, 512], BF16) for i in range(2)]
                    sz = [sbt(p1, "sz%d" % i, [128, 512], BF16) for i in range(2)]
                    G1 = [sbt(p1, "G1%d" % i, [128, 512], BF16) for i in range(2)]
                    G = [sbt(p1, "G%d" % i, [128, 512], BF16) for i in range(2)]
                    ktok = [sbt(p1, "ktok%d" % i, [128, 512], BF16) for i in range(2)]
                    SpT = [sbt(p1, "SpT%d" % i, [128, 128], BF16) for i in range(2)]
                    hfin = [sbt(p1, "hfin%d" % i, [128, 512], BF16) for i in range(2)]
                    sml = [sbt(p1, "sml%d" % i, [128, 8]) for i in range(2)]
                    junk1 = sbt(p1, "junk1", [128, 512], BF16)
                    B_junk1 = S.buf()
                    B_vaug = [S.buf() for _ in range(2)]
                    B_so = [S.buf() for _ in range(2)]
                    B_sz = [S.buf() for _ in range(2)]
                    B_G1 = [S.buf() for _ in range(2)]
                    B_G = [S.buf() for _ in range(2)]
                    B_ktok = [S.buf() for _ in range(2)]
                    B_SpT = [S.buf() for _ in range(2)]
                    B_hfin = [S.buf("hfin%d" % i) for i in range(2)]
                    B_sml = [[S.buf() for _ in range(8)] for _ in range(2)]
                    cnt = [0]

                    def chunk(h, L, xcols, qt, kt, Bq, Bk, qcols, ucol, flcol, ubcol, acolp, Ct, BCs, nt, Bn, hf_rows, W5c, B_Wc):
                        s = cnt[0] % 2
                        cnt[0] += 1
                        pb, Bpb = bank()
                        pbb = pb[:].bitcast(BF16)

                        def ktr(e, pbb=pbb):
                            ins = None
                            for dc in range(4):
                                ins = e.transpose(out=pbb[:L, dc * 128:(dc + 1) * 128], in_=kt[:, dc, qcols],
                                                  identity=identb[:, :])
                            return ins
                        S.op("pe", ktr, reads=[Bk, B_identb], writes=[Bpb])
                        S.op("act", lambda e, pbb=pbb: e.copy(out=ktok[s][:L], in_=pbb[:L, 0:512]), reads=[Bpb], writes=[B_ktok[s]])

                        def smm(e):
                            ins = None
                            for dc in range(4):
                                ins = e.matmul(psm[:L, 0:L], lhsT=kt[:, dc, qcols], rhs=qt[:, dc, qcols],
                                               start=(dc == 0), stop=(dc == 3))
                            return ins
                        S.op("pe", smm, reads=[Bq, Bk], writes=[B_psS])
                        S.op("dve", lambda e: e.tensor_tensor(out=SpT[s][:L, :L], in0=psm[:L, 0:L], in1=maskT[:L, :L], op=ALU.mult),
                             reads=[B_psS, B_maskT], writes=[B_SpT[s]])
                        S.op("act", lambda e: e.activation(out=Cbf[:].rearrange("p a b -> p (a b)"),
                                                           in_=Ct[:].rearrange("p a b -> p (a b)"), func=AF.Copy, scale=acolp),
                             reads=list(BCs) + [B_abc], writes=[B_Cbf])
                        S.op("act", lambda e: e.activation(out=nbf[s][:], in_=nt[:], func=AF.Copy, scale=acolp),
                             reads=[Bn, B_abc], writes=[B_nbf[s]])
                        pv = []
                        for j in (2, 3, 4):
                            pb, Bpb = bank()

                            def pj(e, j=j, pb=pb):
                                ins = None
                                for kc in range(8):
                                    ins = e.matmul(pb[:L, :], lhsT=xnT[:, kc, xcols], rhs=W5c[:, kc, j, :],
                                                   start=(kc == 0), stop=(kc == 7))
                                return ins
                            S.op("pe", pj, reads=[B_xnT, B_Wc[j]], writes=[Bpb])
                            pv.append((pb, Bpb))
                            if j == 2:
                                pvv, Bpv = pb, Bpb
                                S.op("dve", lambda e: e.tensor_scalar(out=vaug[s][:L], in0=pvv[:L, :], scalar1=ucol, scalar2=None,
                                                                      op0=ALU.mult), reads=[Bpv, B_uf], writes=[B_vaug[s]])
                            elif j == 3:
                                po, Bpo = pb, Bpb
                                S.op("act", lambda e: e.activation(out=so[s][:L], in_=po[:L, :], func=AF.Sigmoid),
                                     reads=[Bpo], writes=[B_so[s]])
                            else:
                                pz, Bpz = pb, Bpb
                                S.op("act", lambda e: e.activation(out=sz[s][:L], in_=pz[:L, :], func=AF.Sigmoid),
                                     reads=[Bpz], writes=[B_sz[s]])
                                S.op("dve", lambda e: e.tensor_tensor(out=G1[s][:L], in0=pz[:L, :], in1=sz[s][:L], op=ALU.mult),
                                     reads=[Bpz, B_sz[s]], writes=[B_G1[s]])
                                S.op("pool", lambda e: e.tensor_tensor(out=G[s][:L], in0=G1[s][:L], in1=so[s][:L], op=ALU.mult),
                                     reads=[B_G1[s], B_so[s]], writes=[B_G[s]])
                        for dc in range(4):
                            pC, BpC = bank()
                            S.op("pe", lambda e, dc=dc, pC=pC: e.matmul(pC[:, :], lhsT=ktok[s][:L, dc * 128:(dc + 1) * 128],
                                                                        rhs=vaug[s][:L], start=True, stop=True),
                                 reads=[B_ktok[s], B_vaug[s]], writes=[BpC])
                            S.op("dve", lambda e, dc=dc, pC=pC: e.scalar_tensor_tensor(
                                out=Ct[:, dc, :], in0=Ct[:, dc, :], scalar=acolp, in1=pC[:, :], op0=ALU.mult, op1=ALU.add),
                                reads=[BCs[dc], BpC, B_abc], writes=[BCs[dc]])
                        pN, BpN = bank()

                        def nmm(e, pN=pN):
                            e.matmul(pN[:L, :], lhsT=SpT[s][:L, :L], rhs=vaug[s][:L], start=True, stop=False)
                            ins = None
                            for dc in range(4):
                                ins = e.matmul(pN[:L, :], lhsT=qt[:, dc, qcols], rhs=Cbf[:, dc, :],
                                               start=False, stop=(dc == 3))
                            return ins
                        S.op("pe", nmm, reads=[B_SpT[s], B_vaug[s], Bq, B_Cbf], writes=[BpN])

                        def dmm(e):
                            e.matmul(psm[:L, 128:129], lhsT=SpT[s][:L, :L], rhs=ubcol, start=True, stop=False)
                            ins = None
                            for dc in range(4):
                                ins = e.matmul(psm[:L, 128:129], lhsT=qt[:, dc, qcols], rhs=nbf[s][:, dc:dc + 1],
                                               start=False, stop=(dc == 3))
                            for dc in range(4):
                                ins = e.matmul(psm[:, 132 + dc:133 + dc], lhsT=ktok[s][:L, dc * 128:(dc + 1) * 128],
                                               rhs=ubcol, start=True, stop=True)
                            return ins
                        S.op("pe", dmm, reads=[B_SpT[s], B_uf, Bq, B_nbf[s], B_ktok[s]], writes=[B_psD])
                        S.op("dve", lambda e: e.scalar_tensor_tensor(out=nt[:], in0=nt[:], scalar=acolp, in1=psm[:, 132:136],
                                                                     op0=ALU.mult, op1=ALU.add),
                             reads=[Bn, B_psn, B_abc], writes=[Bn])
                        bs = B_sml[s]
                        sm_ = sml[s]
                        S.op("act", lambda e, pN=pN: e.activation(out=junk1[:L], in_=pN[:L, :], func=AF.Square,
                                                                   accum_out=sm_[:L, 0:1]),
                             reads=[BpN], writes=[B_junk1, bs[0]])
                        S.op("dve", lambda e: e.tensor_scalar(out=sm_[:L, 1:2], in0=psm[:L, 128:129], scalar1=-1.0, scalar2=flcol,
                                                              op0=ALU.mult, op1=ALU.max),
                             reads=[B_psD, B_uf], writes=[bs[1]])
                        S.op("dve", lambda e: e.tensor_tensor(out=sm_[:L, 2:3], in0=psm[:L, 128:129], in1=sm_[:L, 1:2], op=ALU.max),
                             reads=[B_psD, bs[1]], writes=[bs[2]])
                        S.op("dve", lambda e: e.scalar_tensor_tensor(out=sm_[:L, 3:4], in0=sm_[:L, 2:3], scalar=EPS,
                                                                     in1=sm_[:L, 2:3], op0=ALU.mult, op1=ALU.mult),
                             reads=[bs[2]], writes=[bs[3]])
                        S.op("dve", lambda e: e.scalar_tensor_tensor(out=sm_[:L, 4:5], in0=sm_[:L, 0:1], scalar=1.0 / DH,
                                                                     in1=sm_[:L, 3:4], op0=ALU.mult, op1=ALU.add),
                             reads=[bs[0], bs[3]], writes=[bs[4]])
                        S.op("pool", lambda e: e.tensor_tensor(out=sm_[:L, 5:6], in0=sm_[:L, 4:5], in1=cm05[:L], op=ALU.pow),
                             reads=[bs[4], B_cm05], writes=[bs[5]])
                        S.op("dve", lambda e, pN=pN: e.scalar_tensor_tensor(out=hfin[s][:L], in0=pN[:L, :], scalar=sm_[:L, 5:6],
                                                                            in1=G[s][:L], op0=ALU.mult, op1=ALU.mult),
                             reads=[BpN, bs[5], B_G[s]], writes=[B_hfin[s]])
                        S.dma("sp", [(hf[hf_rows, h * 512:(h + 1) * 512], hfin[s][:L])], B_hfin[s], reads=[B_hfin[s]])

                    def qkproj(h, slot, col, w, W5c, B_Wc, dsts=None):
                        if dsts is None:
                            dsts = ((0, qT[slot], B_qT[slot]), (1, kT[slot], B_kT[slot]))
                        for which, dst, Bd in dsts:
                            for dc in range(4):
                                pb, Bpb = bank()

                                def pm(e, which=which, dc=dc, pb=pb):
                                    ins = None
                                    for kc in range(8):
                                        ins = e.matmul(pb[:, 0:w], lhsT=W5c[:, kc, which, dc * 128:(dc + 1) * 128],
                                                       rhs=xnT[:, kc, col:col + w], start=(kc == 0), stop=(kc == 7))
                                    return ins
                                S.op("pe", pm, reads=[B_Wc[which], B_xnT], writes=[Bpb])
                                if which == 0:
                                    S.op("act", lambda e, dc=dc, pb=pb, dst=dst: e.copy(out=dst[:, dc, 0:w], in_=pb[:, 0:w]),
                                         reads=[Bpb], writes=[Bd])
                                else:
                                    S.op("act", lambda e, dc=dc, pb=pb, dst=dst: e.activation(
                                        out=dst[:, dc, 0:w], in_=pb[:, 0:w], func=AF.Copy, scale=float(DH) ** -0.5),
                                        reads=[Bpb], writes=[Bd])

                    gcount = 0

                    def load_w5(h):
                        sl = h % 2
                        for j in range(5):
                            c0 = j * DI + h * DH
                            S.dma("pool", [(W5[sl][:, :, j, :], w_in_a[:, c0:c0 + DH].rearrange("(k p) c -> p k c", p=128))],
                                  B_W[sl][j], writes=[B_W[sl][j]])
                    load_w5(0)
                    NG = (T + 511) // 512
                    qTs = sbt(p1, "qTs", [128, 4, NST], BF16)
                    kTs = sbt(p1, "kTs", [128, 4, NST], BF16)
                    B_qTs, B_kTs = S.buf("qTs"), S.buf("kTs")
                    spos = [(j + 1) * NC // (NS + 1) for j in range(NS)]

                    def sample_load(h, j):
                        S.dma("sp", [(Css[:], sC[j, h].rearrange("(dc p) e -> p dc e", p=128))], B_Cs[0], writes=B_Cs)
                        S.dma("sp", [(nss[:], sn[j, h].rearrange("(dc p) -> p dc", p=128))], B_ns, writes=[B_ns],
                              allow_slow_non_contiguous=True)

                    def sample_seq(h, j, W5c, B_Wc):
                        chunk(h, DS, slice(T + j * DS, T + (j + 1) * DS), qTs, kTs, B_qTs, B_kTs,
                              slice(j * DS, (j + 1) * DS),
                              ufs[:, j * 8 + h:j * 8 + h + 1], ufs[:, j * 8 + 4 + h:j * 8 + 5 + h],
                              ubs[:, j * 8 + h:j * 8 + h + 1], a_bc[:, h * NCH + NC + j:h * NCH + NC + j + 1],
                              Css, B_Cs, nss, B_ns, slice(T + j * DS, T + (j + 1) * DS), W5c, B_Wc)
                        S.dma("sp", [(C_s[j, h].rearrange("(dc p) e -> p dc e", p=128), Css[:])], B_Cs[0], reads=B_Cs)
                        S.dma("sp", [(n_s[j, h].rearrange("(dc p) -> p dc", p=128), nss[:])], B_ns, reads=[B_ns],
                              allow_slow_non_contiguous=True)

                    for h in range(H):
                        W5c, B_Wc = W5[h % 2], B_W[h % 2]
                        if h + 1 < H:
                            load_w5(h + 1)
                        for dc in range(4):
                            S.op("pool", lambda e, dc=dc: e.memset(Cst[:, dc, :], 0.0), writes=[B_C[dc]])
                        S.op("pool", lambda e: e.memset(nst[:], 0.0), writes=[B_n])
                        slot = gcount % 2
                        gcount += 1
                        qkproj(h, slot, 0, min(512, T), W5c, B_Wc)
                        qkproj(h, None, T, NST, W5c, B_Wc, dsts=((0, qTs, B_qTs), (1, kTs, B_kTs)))
                        sample_load(h, 0)
                        for grp in range(NG):
                            col = grp * 512
                            w = min(512, T - col)
                            nch = w // 128
                            nslot = slot
                            for cc in range(nch):
                                c = grp * 4 + cc
                                if cc == nch - 1 and grp + 1 < NG:
                                    nslot = gcount % 2
                                    gcount += 1
                                    qkproj(h, nslot, col + 512, min(512, T - col - 512), W5c, B_Wc)
                                if zero_fill:
                                    zero_fill.pop()()
                                chunk(h, 128, slice(c * 128, (c + 1) * 128), qT[slot], kT[slot], B_qT[slot], B_kT[slot],
                                      slice(cc * 128, (cc + 1) * 128),
                                      uf_tok[:, c * 8 + h:c * 8 + h + 1], uf_tok[:, c * 8 + 4 + h:c * 8 + 5 + h],
                                      ub_tok[:, c * 8 + h:c * 8 + h + 1], a_bc[:, h * NCH + c:h * NCH + c + 1],
                                      Cst, B_C, nst, B_n, slice(c * 128, (c + 1) * 128), W5c, B_Wc)
                                for j in range(NS):
                                    if spos[j] == c:
                                        sample_seq(h, j, W5c, B_Wc)
                                        if j + 1 < NS:
                                            sample_load(h, j + 1)
                            slot = nslot
                        S.dma("sp", [(C_p[h].rearrange("(dc p) e -> p dc e", p=128), Cst[:])], B_C[0], reads=B_C)
                        S.dma("sp", [(n_p[h].rearrange("(dc p) -> p dc", p=128), nst[:])], B_n, reads=[B_n],
                              allow_slow_non_contiguous=True)
                    while zero_fill:
                        zero_fill.pop()()
                    S.barrier()
        if dbg and "hf" in dbg:
            o = dout("dbg_hf", [TT, DI], BF16)
            dbg_out["hf"] = o
            Bd = S.buf("dbghf")
            S.dma("sp", [(o[:, :], hf[:, :])], Bd)
            S.barrier()


        gk_bc = sbt(top, "gk_bc", [128, HD])
        gq_bc = sbt(top, "gq_bc", [128, HD])
        B_gqk = S.buf("gqk")
        S.dma("sp", [(gk_bc[:], k_norm[0:1, :].to_broadcast([128, HD])),
                     (gq_bc[:], q_norm[0:1, :].to_broadcast([128, HD]))], B_gqk, writes=[B_gqk])

        gcolK = sbt(top, "gcolK", [128, 1])
        gcolQ = sbt(top, "gcolQ", [128, 1])
        S.dma("sp", [(gcolK[0:64, :], k_norm[0:1, :].rearrange("o d -> d o")), (gcolK[64:128, :], k_norm[0:1, :].rearrange("o d -> d o")),
                     (gcolQ[0:64, :], q_norm[0:1, :].rearrange("o d -> d o")), (gcolQ[64:128, :], q_norm[0:1, :].rearrange("o d -> d o"))],
              B_gqk, writes=[B_gqk])
        def tile_rows(i):
            return (128 if i < NT else NST), i * 128

        if "B" in phases:
            with contextlib.ExitStack() as pb_:
                x1nT = sbt(pb_, "x1nT", [128, 8, TT], BF16)
                B_x1nT = S.buf("x1nT")
                gkv = sbt(pb_, "gkv", [128, 8])
                gnb = sbt(pb_, "gnb", [128, 8])
                B_gn = S.buf("gn")
                S.dma("sp", [(gkv[:], norm_kv.rearrange("(k p) -> p k", p=128)),
                             (gnb[:], norm_b.rearrange("(k p) -> p k", p=128))], B_gn, writes=[B_gn],
                      allow_slow_non_contiguous=True)
                Wt0 = sbt(pb_, "WtB0", [128, 8, QW], BF16)
                stgB = [sbt(pb_, "stgB%d" % i, [128, QW]) for i in range(2)]
                B_stgB = [S.buf() for _ in range(2)]
                B_Wt0 = S.buf("WtB0")
                for kc in range(8):
                    s2_ = kc % 2
                    S.dma("sp", [(stgB[s2_][:, 0:QW], w_kv[kc * 128:(kc + 1) * 128, 0:QW])], B_stgB[s2_], writes=[B_stgB[s2_]])
                    S.op("act", lambda e, kc=kc, s2_=s2_: e.activation(out=Wt0[:, kc, 0:QW], in_=stgB[s2_][:, 0:QW],
                                                                       func=AF.Copy, scale=gkv[:, kc:kc + 1]),
                         reads=[B_stgB[s2_], B_gn], writes=[B_Wt0])
                with contextlib.ExitStack() as p1:
                    woA = sbt(p1, "woA", [128, 16, D], BF16)
                    B_woA = S.buf("woA")
                    stg = [sbt(p1, "stgA%d" % i, [128, D]) for i in range(2)]
                    B_stg = [S.buf() for _ in range(2)]
                    hng = sbt(p1, "hng", [128, 16])
                    B_hng = S.buf("hng")
                    S.dma("sp", [(hng[:], hnorm.rearrange("(k p) -> p k", p=128))], B_hng, writes=[B_hng],
                          allow_slow_non_contiguous=True)
                    for ec in range(16):
                        s2 = ec % 2
                        S.dma("sp", [(stg[s2][:], w_out_a[ec * 128:(ec + 1) * 128, :])], B_stg[s2], writes=[B_stg[s2]])
                        S.op("act", lambda e, ec=ec, s2=s2: e.activation(out=woA[:, ec, :], in_=stg[s2][:], func=AF.Copy,
                                                                         scale=hng[:, ec:ec + 1]),
                             reads=[B_stg[s2], B_hng], writes=[B_woA])
                    hft = [sbt(p1, "hft%d" % i, [128, DI], BF16) for i in range(3)]
                    hfT = [sbt(p1, "hfT%d" % i, [128, 16, 128], BF16) for i in range(2)]
                    xt = [sbt(p1, "xtB%d" % i, [128, D]) for i in range(4)]
                    x1 = [sbt(p1, "x1B%d" % i, [128, D]) for i in range(3)]
                    x1n = [sbt(p1, "x1n%d" % i, [128, D], BF16) for i in range(2)]
                    junkb = sbt(p1, "junkB", [128, D], BF16)
                    smb = sbt(p1, "smB", [128, 9])
                    B_hft = [S.buf("hft%d" % i) for i in range(3)]
                    B_hfT = [[S.buf(), S.buf()] for _ in range(2)]
                    B_xtb = [S.buf("xtB%d" % i) for i in range(4)]
                    B_x1 = [S.buf("x1B%d" % i) for i in range(3)]
                    B_x1n = [S.buf() for _ in range(2)]
                    B_junkb = S.buf()
                    B_smb = [[S.buf() for _ in range(3)] for _ in range(3)]

                    def run_skewed1(stages, n):
                        ns = len(stages)
                        for step in range(n + ns - 1):
                            for st in range(ns - 1, -1, -1):
                                i = step - st
                                if 0 <= i < n:
                                    stages[st](i)

                    def b1_load(i):
                        rows, c0 = tile_rows(i)
                        S.dma("sp", [(hft[i % 3][:rows], hf[c0:c0 + rows, :])], B_hft[i % 3], writes=[B_hft[i % 3]])
                        src = xp[c0:c0 + rows, :] if i < NT else xs[:, :]
                        S.dma("sp", [(xt[i % 4][:rows], src)], B_xtb[i % 4], writes=[B_xtb[i % 4]])

                    def b1_s0(i):
                        rows, c0 = tile_rows(i)
                        s2, s3 = i % 2, i % 3
                        for hb in range(2):
                            pb, Bpb = bank()
                            pbb = pb[:].bitcast(BF16)

                            def trh(e, hb=hb, pbb=pbb):
                                ins = None
                                for j in range(8):
                                    ec = hb * 8 + j
                                    ins = e.transpose(out=pbb[:, j * 128:j * 128 + rows],
                                                      in_=hft[s3][:rows, ec * 128:(ec + 1) * 128], identity=identb[:rows, :rows])
                                return ins
                            S.op("pe", trh, reads=[B_hft[s3], B_identb], writes=[Bpb])
                            S.op("act" if hb == 0 else "dve", lambda e, hb=hb, pbb=pbb: (e.copy if hb == 0 else e.tensor_copy)(
                                out=hfT[s2][:, hb * 8:(hb + 1) * 8, :rows],
                                in_=pbb.rearrange("p (k t) -> p k t", t=128)[:, :, :rows]),
                                reads=[Bpb], writes=[B_hfT[s2][hb]])

                    def b1_s1(i):
                        rows, c0 = tile_rows(i)
                        s2, s3, s4 = i % 2, i % 3, i % 4
                        for half in range(2):
                            pb, Bpb = bank()

                            def ym(e, half=half, pb=pb):
                                ins = None
                                for ec in range(16):
                                    ins = e.matmul(pb[:rows, :], lhsT=hfT[s2][:, ec, :rows],
                                                   rhs=woA[:, ec, half * 512:(half + 1) * 512], start=(ec == 0), stop=(ec == 15))
                                return ins
                            S.op("pe", ym, reads=[B_hfT[s2][0], B_hfT[s2][1], B_woA], writes=[Bpb])
                            S.op("dve", lambda e, half=half, pb=pb: e.tensor_tensor(
                                out=x1[s3][:rows, half * 512:(half + 1) * 512], in0=pb[:rows, :],
                                in1=xt[s4][:rows, half * 512:(half + 1) * 512], op=ALU.add),
                                reads=[Bpb, B_xtb[s4]], writes=[B_x1[s3]])
                        S.dma("pool", [(x1s[c0:c0 + rows, :], x1[s3][:rows])], B_x1[s3], reads=[B_x1[s3]])
                        S.op("act", lambda e: e.activation(out=junkb[:rows], in_=x1[s3][:rows], func=AF.Square,
                                                           accum_out=smb[:rows, s3 * 3:s3 * 3 + 1]),
                             reads=[B_x1[s3]], writes=[B_junkb, B_smb[s3][0]])

                    def b1_s2(i):
                        rows, c0 = tile_rows(i)
                        s2, s3 = i % 2, i % 3
                        rsqrt_small(rows, smb[:rows, s3 * 3 + 2:s3 * 3 + 3], smb[:rows, s3 * 3:s3 * 3 + 1], 1.0 / D, EPS,
                                    B_smb[s3][0], B_smb[s3][2], smb[:rows, s3 * 3 + 1:s3 * 3 + 2], B_smb[s3][1])
                        S.op("act", lambda e: e.activation(out=x1n[s2][:rows], in_=x1[s3][:rows], func=AF.Copy,
                                                           scale=smb[:rows, s3 * 3 + 2:s3 * 3 + 3]),
                             reads=[B_x1[s3], B_smb[s3][2]], writes=[B_x1n[s2]])

                    def b1_s3(i):
                        rows, c0 = tile_rows(i)
                        s2 = i % 2
                        pb, Bpb = bank()
                        pbb = pb[:].bitcast(BF16)

                        def trx(e, pbb=pbb):
                            ins = None
                            for kc in range(8):
                                ins = e.transpose(out=pbb[:, kc * 128:kc * 128 + rows],
                                                  in_=x1n[s2][:rows, kc * 128:(kc + 1) * 128], identity=identb[:rows, :rows])
                            return ins
                        S.op("pe", trx, reads=[B_x1n[s2], B_identb], writes=[Bpb])
                        S.op("dve", lambda e, pbb=pbb: e.tensor_copy(out=x1nT[:, :, c0:c0 + rows],
                                                                     in_=pbb.rearrange("p (k t) -> p k t", t=128)[:, :, :rows]),
                             reads=[Bpb], writes=[B_x1nT])
                    run_skewed1([b1_load, b1_s0, b1_s1, b1_s2, b1_s3], NTT)
                    S.barrier()

                with contextlib.ExitStack() as p2:
                    Wts = [Wt0, sbt(p2, "WtB1", [128, 8, QW], BF16)]
                    B_Wts = [B_Wt0, S.buf("WtB1")]
                    stg = stgB
                    B_stg = B_stgB
                    NSL = 4
                    raw = [sbt(p2, "rawB%d" % i, [128, QW]) for i in range(NSL)]
                    B_raw = [S.buf() for _ in range(NSL)]
                    sqt = [sbt(p2, "sqtB%d" % i, [128, QW]) for i in range(2)]
                    B_sqt = [S.buf() for _ in range(2)]
                    nb16 = [sbt(p2, "nb16_%d" % i, [128, QW], BF16) for i in range(NSL)]
                    B_nb16 = [S.buf() for _ in range(NSL)]
                    TTt = [sbt(p2, "TTt%d" % i, [128, 12, 128], BF16) for i in range(2)]
                    B_TTt = [S.buf("TTt%d" % i) for i in range(2)]
                    vb = [sbt(p2, "vb%d" % i, [128, NQH, HD + 1], BF16) for i in range(2)]
                    B_vb = [S.buf("vb%d" % i) for i in range(2)]
                    sm2 = sbt(p2, "sm2", [128, 2 * NSL * NQH])
                    B_sm2 = [[S.buf() for _ in range(2)] for _ in range(NSL)]
                    zsb = [sbt(p2, "zsb%d" % i, [128, 512], BF16) for i in range(2)]
                    sgz = [sbt(p2, "sgz%d" % i, [128, 512]) for i in range(2)]
                    B_zsb = [S.buf("zsb%d" % i) for i in range(2)]
                    B_sgz = [S.buf() for _ in range(2)]
                    for i in range(2):
                        S.op("pool", lambda e, i=i: e.memset(vb[i][:, :, HD:HD + 1], 1.0), writes=[B_vb[i]])

                    def run_skewed(stages, n):
                        ns = len(stages)
                        for step in range(n + ns - 1):
                            for st in range(ns - 1, -1, -1):
                                i = step - st
                                if 0 <= i < n:
                                    stages[st](i)

                    def load_w(wsl, wsrc, c0, ncols, gain):
                        Wt, B_Wt = Wts[wsl], B_Wts[wsl]
                        for kc in range(8):
                            s2 = kc % 2
                            S.dma("sp", [(stg[s2][:, 0:ncols], wsrc[kc * 128:(kc + 1) * 128, c0:c0 + ncols])], B_stg[s2],
                                  writes=[B_stg[s2]])
                            S.op("act", lambda e, kc=kc, s2=s2: e.activation(out=Wt[:, kc, 0:ncols], in_=stg[s2][:, 0:ncols],
                                                                             func=AF.Copy, scale=gain[:, kc:kc + 1]),
                                 reads=[B_stg[s2], B_gn], writes=[B_Wt])

                    def proj(i, nblk, evac, wsl):
                        Wt, B_Wt = Wts[wsl], B_Wts[wsl]
                        rows, c0 = tile_rows(i)
                        for nb in range(nblk):
                            pb, Bpb = bank()

                            def pm(e, nb=nb, pb=pb):
                                ins = None
                                for kc in range(8):
                                    ins = e.matmul(pb[:rows, :], lhsT=x1nT[:, kc, c0:c0 + rows],
                                                   rhs=Wt[:, kc, nb * 512:(nb + 1) * 512], start=(kc == 0), stop=(kc == 7))
                                return ins
                            S.op("pe", pm, reads=[B_x1nT, B_Wt], writes=[Bpb])
                            evac(nb, pb, Bpb)

                    def kv_out(i, which, src_tile, Bsrc):
                        rows, c0 = tile_rows(i)
                        pairs = []
                        for g, (win, dil) in enumerate(GROUPS):
                            if i < NT:
                                nrow = min(win, T)
                                r0 = c0 - (T - nrow)
                                if r0 < 0:
                                    continue
                                dst = kvp[g][r0:r0 + rows, which, :, :]
                            else:
                                dst = kvs[g][:, which, :, :]
                            pairs.append((dst, src_tile[:rows, g * 512:(g + 1) * 512].rearrange("p (h d) -> p h d", d=HD)))
                        if pairs:
                            S.dma("sp", pairs, Bsrc, reads=[Bsrc])

                    def qk_phase(wsl, g_bc, dstT, is_k, after_load=None):

                        def st0(i):
                            rows, c0 = tile_rows(i)
                            s4, s2 = i % NSL, i % 2

                            def ev(nb, pb, Bpb):
                                S.op("act", lambda e: e.copy(out=raw[s4][:rows, nb * 512:(nb + 1) * 512], in_=pb[:rows, :]),
                                     reads=[Bpb], writes=[B_raw[s4]])
                                S.op("act", lambda e: e.activation(out=sqt[s2][:rows, nb * 512:(nb + 1) * 512], in_=pb[:rows, :],
                                                                   func=AF.Square), reads=[Bpb], writes=[B_sqt[s2]])
                            proj(i, 3, ev, wsl)
                            if i == 0 and after_load is not None:
                                after_load()

                        def st1(i):
                            rows, c0 = tile_rows(i)
                            s4, s2 = i % NSL, i % 2
                            ssq = sm2[:rows, s4 * 2 * NQH:s4 * 2 * NQH + NQH]
                            rst = sm2[:rows, s4 * 2 * NQH + NQH:(s4 + 1) * 2 * NQH]
                            S.op("dve", lambda e: e.tensor_reduce(out=ssq, in_=sqt[s2][:rows].rearrange("p (h d) -> p h d", d=HD),
                                                                  axis=AX.X, op=ALU.add),
                                 reads=[B_sqt[s2]], writes=[B_sm2[s4][0]])
                            S.op("dve", lambda e: e.tensor_scalar(out=ssq, in0=ssq, scalar1=1.0 / HD, scalar2=EPS,
                                                                  op0=ALU.mult, op1=ALU.add),
                                 reads=[B_sm2[s4][0]], writes=[B_sm2[s4][0]])
                            S.op("act", lambda e: e.activation(out=ssq, in_=ssq, func=AF.Sqrt),
                                 reads=[B_sm2[s4][0]], writes=[B_sm2[s4][0]])
                            S.op("dve", lambda e: e.reciprocal(out=rst, in_=ssq),
                                 reads=[B_sm2[s4][0]], writes=[B_sm2[s4][1]])

                        def st2(i):
                            rows, c0 = tile_rows(i)
                            s4 = i % NSL
                            rst = sm2[:rows, s4 * 2 * NQH + NQH:(s4 + 1) * 2 * NQH]
                            if is_k:
                                S.op("dve", lambda e: e.tensor_tensor(
                                    out=raw[s4][:rows].rearrange("p (h d) -> p h d", d=HD),
                                    in0=raw[s4][:rows].rearrange("p (h d) -> p h d", d=HD),
                                    in1=rst.unsqueeze(2).to_broadcast([rows, NQH, HD]), op=ALU.mult),
                                    reads=[B_raw[s4], B_sm2[s4][1]], writes=[B_raw[s4]])
                                S.op("act", lambda e: e.copy(out=nb16[s4][:rows], in_=raw[s4][:rows]),
                                     reads=[B_raw[s4]], writes=[B_nb16[s4]])
                                need = []
                                for g, (win, dil) in enumerate(GROUPS):
                                    if i >= NT or c0 - (T - min(win, T)) >= 0:
                                        need.append(g)
                                for g in need:
                                    S.op("dve", lambda e, g=g: e.tensor_tensor(
                                        out=raw[s4][:rows, g * 512:(g + 1) * 512].rearrange("p (h d) -> p h d", d=HD),
                                        in0=raw[s4][:rows, g * 512:(g + 1) * 512].rearrange("p (h d) -> p h d", d=HD),
                                        in1=g_bc[:rows].unsqueeze(1).to_broadcast([rows, 8, HD]), op=ALU.mult),
                                        reads=[B_raw[s4], B_gqk], writes=[B_raw[s4]])
                                if need:
                                    kv_out(i, 0, raw[s4], B_raw[s4])
                            else:
                                S.op("dve", lambda e: e.tensor_tensor(
                                    out=nb16[s4][:rows].rearrange("p (h d) -> p h d", d=HD),
                                    in0=raw[s4][:rows].rearrange("p (h d) -> p h d", d=HD),
                                    in1=rst.unsqueeze(2).to_broadcast([rows, NQH, HD]), op=ALU.mult),
                                    reads=[B_raw[s4], B_sm2[s4][1]], writes=[B_nb16[s4]])

                        def st3(i):
                            rows, c0 = tile_rows(i)
                            s4, s2 = i % NSL, i % 2
                            for hb, (j0, j1) in enumerate(((0, 8), (8, 12))):
                                pb, Bpb = bank()
                                pbb = pb[:].bitcast(BF16)

                                def trq(e, j0=j0, j1=j1, pbb=pbb):
                                    ins = None
                                    for j in range(j0, j1):
                                        ins = e.transpose(out=pbb[:, (j - j0) * 128:(j - j0) * 128 + rows],
                                                          in_=nb16[s4][:rows, j * 128:(j + 1) * 128], identity=identb[:rows, :rows])
                                    return ins
                                S.op("pe", trq, reads=[B_nb16[s4], B_identb], writes=[Bpb])
                                gcol = gcolK if is_k else gcolQ
                                if hb == 0:
                                    S.op("dve", lambda e, j0=j0, j1=j1, pbb=pbb: e.tensor_scalar(
                                        out=TTt[s2][:, j0:j1, :rows],
                                        in0=pbb[:, 0:(j1 - j0) * 128].rearrange("p (k t) -> p k t", t=128)[:, :, :rows],
                                        scalar1=gcol[:, 0:1], scalar2=None, op0=ALU.mult),
                                        reads=[Bpb, B_gqk], writes=[B_TTt[s2]])
                                else:
                                    S.op("act", lambda e, j0=j0, j1=j1, pbb=pbb: e.activation(
                                        out=TTt[s2][:, j0:j1, :rows],
                                        in_=pbb[:, 0:(j1 - j0) * 128].rearrange("p (k t) -> p k t", t=128)[:, :, :rows],
                                        func=AF.Copy, scale=gcol[:, 0:1]),
                                        reads=[Bpb, B_gqk], writes=[B_TTt[s2]])
                            if is_k:
                                S.dma("sp", [(dstT.rearrange("(rg p) t -> p rg t", p=128)[:, :, c0:c0 + rows], TTt[s2][:, :, :rows])],
                                      B_TTt[s2], reads=[B_TTt[s2]])
                            else:
                                S.dma("sp", [(dstT.rearrange("(rg p) t -> p rg t", p=128)[:, :, c0:c0 + rows], TTt[s2][:, :, :rows]),
                                             (QTse.rearrange("(rg p) t -> p rg t", p=128)[0:64, :, c0:c0 + rows], TTt[s2][0:64, :, :rows]),
                                             (QTso.rearrange("(rg p) t -> p rg t", p=128)[64:128, :, c0:c0 + rows], TTt[s2][64:128, :, :rows])],
                                      B_TTt[s2], reads=[B_TTt[s2]])
                        run_skewed([st0, st1, st2, st3], NTT)

                    qk_phase(0, gk_bc, KTs, True, after_load=lambda: load_w(1, w_kv, QW, QW, gkv))

                    def v0(i):
                        rows, c0 = tile_rows(i)
                        s4 = i % NSL

                        def evv(nb, pb, Bpb):
                            S.op("act", lambda e: e.copy(out=raw[s4][:rows, nb * 512:(nb + 1) * 512], in_=pb[:rows, :]),
                                 reads=[Bpb], writes=[B_raw[s4]])
                        proj(i, 3, evv, 1)
                        if i == 0:
                            load_w(0, w_in_b, 0, QW, gnb)

                    def v1(i):
                        rows, c0 = tile_rows(i)
                        s4, s2 = i % NSL, i % 2
                        S.op("dve", lambda e: e.tensor_copy(out=vb[s2][:rows, :, 0:HD],
                                                            in_=raw[s4][:rows].rearrange("p (h d) -> p h d", d=HD)),
                             reads=[B_raw[s4]], writes=[B_vb[s2]])
                        S.dma("sp", [(Vs[c0:c0 + rows, :, :], vb[s2][:rows])], B_vb[s2], reads=[B_vb[s2]])
                        kv_out(i, 1, raw[s4], B_raw[s4])
                    run_skewed([v0, v1], NTT)
                    qk_phase(0, gq_bc, QTs, False, after_load=lambda: load_w(1, w_in_b, QW, 512, gnb))
                    zbank = {}

                    def z0(i):
                        rows, c0 = tile_rows(i)
                        s2 = i % 2

                        def evz(nb, pb, Bpb):
                            zbank[i] = (pb, Bpb)
                            S.op("act", lambda e: e.activation(out=sgz[s2][:rows], in_=pb[:rows, :], func=AF.Sigmoid),
                                 reads=[Bpb], writes=[B_sgz[s2]])
                        proj(i, 1, evz, 1)

                    def z1(i):
                        rows, c0 = tile_rows(i)
                        s2 = i % 2
                        pb, Bpb = zbank.pop(i)
                        S.op("dve", lambda e: e.tensor_tensor(out=zsb[s2][:rows], in0=pb[:rows, :], in1=sgz[s2][:rows], op=ALU.mult),
                             reads=[Bpb, B_sgz[s2]], writes=[B_zsb[s2]])
                        S.dma("sp", [(zss[c0:c0 + rows, :], zsb[s2][:rows])], B_zsb[s2], reads=[B_zsb[s2]])
                    run_skewed([z0, z1], NTT)
                    S.barrier()

        scr = {"x1s": (x1s, [TT, D], F32), "KTs": (KTs, [QW, TT], BF16), "QTs": (QTs, [QW, TT], BF16),
               "Vs": (Vs, [TT, NQH, HD + 1], BF16), "zss": (zss, [TT, 512], BF16), "osc": (osc, [3, TT, 8 * (HD + 1)], F32),
               "vecs": (vecs, [NQH, 384], F32)}
        for name in dbg:
            if name in scr:
                ap_, shp, dt_ = scr[name]
                o = dout("dbg_" + name, shp, dt_)
                dbg_out[name] = o
                Bd = S.buf("dbg" + name)
                S.dma("sp", [(o, ap_)], Bd)
        if dbg:
            S.barrier()


        OW = 8 * (HD + 1)
        ps_ = contextlib.ExitStack()
        ET = sbt(ps_, "ET", [NBUCK, NQH])
        EBs = sbt(ps_, "EBs", [128, 13, 8, 8])
        EBn = sbt(ps_, "EBn", [8, 3, 8, 8])
        if "C" in phases or "c" in phases:
            with contextlib.ExitStack() as pc:
                B_ET = S.buf("ET")
                S.dma("sp", [(ET[:], rel_bias[:, :])], B_ET, writes=[B_ET])
                S.op("act", lambda e: e.activation(out=ET[:], in_=ET[:], func=AF.Exp), reads=[B_ET], writes=[B_ET])
                Tb = sbt(pc, "Tb", [128, NQH, 256])
                B_Tb = S.buf("Tb")
                B_EBs = S.buf("EBs")
                B_EBn = S.buf("EBn")
                with contextlib.ExitStack() as pc0:
                    ohp = sbt(pc0, "ohp", [NBUCK, 3 * 384])
                    ohs = sbt(pc0, "ohs", [NBUCK, 13 * 8 * 128])
                    ohn = sbt(pc0, "ohn", [NBUCK, 3 * 8 * 8])
                    Jf = sbt(pc0, "Jf", [128, 128])
                    B_oh = S.buf("oh")
                    S.dma("sp", [(ohp[:], c_ohp[:, :]), (ohs[:], c_ohs[:, :]), (ohn[:], c_ohn[:, :]), (Jf[:], c_J[:, :])],
                          B_oh, writes=[B_oh])
                    vtmp = sbt(pc0, "vtmp", [8, 3 * 384])
                    B_vtmp = S.buf("vtmp")
                    for g in range(3):
                        pb, Bpb = bank()
                        S.op("pe", lambda e: e.matmul(pb[0:8, 0:384], lhsT=ET[:, g * 8:(g + 1) * 8], rhs=ohp[:, g * 384:(g + 1) * 384],
                                                      start=True, stop=True), reads=[B_ET, B_oh], writes=[Bpb])
                        S.op("dve", lambda e: e.tensor_copy(out=vtmp[:, g * 384:(g + 1) * 384], in_=pb[0:8, 0:384]),
                             reads=[Bpb], writes=[B_vtmp])
                    B_vecs = S.buf("vecs")
                    S.dma("sp", [(vecs[g * 8:(g + 1) * 8, :], vtmp[:, g * 384:(g + 1) * 384]) for g in range(3)], B_vtmp,
                          reads=[B_vtmp], writes=[B_vecs])
                    Hk = sbt(pc0, "Hk", [128, NQH, 256])
                    B_Hk = S.buf("Hk")
                    S.dma("sp", [(Hk[:], bass.AP(vecs.tensor, 0, [[1, 128], [384, NQH], [1, 256]]))], B_Hk,
                          reads=[B_vecs], writes=[B_Hk])
                    for gh2 in range(NQH // 2):
                        pb, Bpb = bank()
                        S.op("pe", lambda e: e.matmul(pb[:, :], lhsT=Jf[:, :],
                                                      rhs=Hk[:, 2 * gh2:2 * gh2 + 2, :].rearrange("p a b -> p (a b)"),
                                                      start=True, stop=True), reads=[B_oh, B_Hk], writes=[Bpb])
                        S.op("dve" if gh2 % 2 else "act", lambda e: (e.tensor_copy if gh2 % 2 else e.copy)(
                            out=Tb[:, 2 * gh2:2 * gh2 + 2, :].rearrange("p a b -> p (a b)"), in_=pb[:, :]),
                            reads=[Bpb], writes=[B_Tb])
                    S.op("pool", lambda e: e.memset(EBs[:].rearrange("p a b c -> p (a b c)"), 0.0), writes=[B_EBs])
                    for gr, (g, r) in enumerate(CLASSES):
                        pb, Bpb = bank()

                        def ebm(e):
                            ins = None
                            for s_ in range(8):
                                o0 = (gr * 8 + s_) * 128
                                ins = e.matmul(pb[:, s_ * 8:s_ * 8 + 8], lhsT=ohs[:, o0:o0 + 128], rhs=ET[:, g * 8:(g + 1) * 8],
                                               start=True, stop=True)
                            return ins
                        S.op("pe", ebm, reads=[B_oh, B_ET], writes=[Bpb])
                        S.op("dve", lambda e: e.tensor_copy(out=EBs[:, gr, :, :],
                                                            in_=pb[:, 0:64].rearrange("p (s h) -> p h s", h=8)),
                             reads=[Bpb], writes=[B_EBs])
                    for g in range(3):
                        pb, Bpb = bank()

                        def ebn(e):
                            ins = None
                            for s_ in range(8):
                                o0 = (g * 8 + s_) * 8
                                ins = e.matmul(pb[0:8, s_ * 8:s_ * 8 + 8], lhsT=ohn[:, o0:o0 + 8], rhs=ET[:, g * 8:(g + 1) * 8],
                                               start=True, stop=True)
                            return ins
                        S.op("pe", ebn, reads=[B_oh, B_ET], writes=[Bpb])
                        S.op("dve", lambda e: e.tensor_copy(out=EBn[:, g, :, :],
                                                            in_=pb[0:8, 0:64].rearrange("p (s h) -> p h s", h=8)),
                             reads=[Bpb], writes=[B_EBn])
                    if "Tb" in dbg:
                        dump("Tb", Tb[:], [128, NQH, 256], B_Tb)
                        dump("EBs", EBs[:], [128, 13, 8, 8], B_EBs)
                        dump("EBn", EBn[:], [8, 3, 8, 8], B_EBn)
                    S.barrier()

                QK = [[sbt(pc, "QK%d_%d" % (i, k), [128, 2, TT], BF16) for k in range(3)] for i in range(2)]
                B_QK = [S.buf("QK%d" % i) for i in range(2)]
                NV = 8
                Vt = [sbt(pc, "Vt%d" % i, [128, 4, HD + 1], BF16) for i in range(NV)]
                B_Vt = [S.buf("Vt%d" % i) for i in range(NV)]
                W4 = 4 * (HD + 1)
                ob = [sbt(pc, "ob%d" % i, [128, W4]) for i in range(2)]
                B_ob = [S.buf("ob%d" % i) for i in range(2)]
                if T % 2048 == 0 and "C" in phases:
                    Et2 = [sbt(pc, "Et2_%d" % i, [128, 4, 256], BF16) for i in range(3)]
                    Tbb = sbt(pc, "Tbb", [128, NQH, 256], BF16)
                    S.op("pool", lambda e: e.tensor_copy(out=Tbb[:].rearrange("p a b -> p (a b)"), in_=Tb[:].rearrange("p a b -> p (a b)")),
                         reads=[B_Tb], writes=[B_Tb])
                    Pt2 = [sbt(pc, "Pt2_%d" % i, [128, 4, 256], BF16) for i in range(3)]
                    B_Et2 = [S.buf() for _ in range(3)]
                    B_Pt2 = [S.buf() for _ in range(3)]
                    B_S2 = [pbanks[2][1], pbanks[4][1], pbanks[6][1]]
                    B_S2b = [pbanks[3][1], pbanks[5][1], pbanks[7][1]]
                    sets = [(g, b) for g in range(3) for b in range(2)]

                    def load_set(si):
                        g, b = sets[si]
                        sl = si % 2
                        r0 = (g * 4 + 2 * b) * 128
                        S.dma("sp", [(QK[sl][2][:], KTs[r0:r0 + 256, :].rearrange("(rg p) t -> p rg t", p=128)),
                                     (QK[sl][0][:], QTse[r0:r0 + 256, :].rearrange("(rg p) t -> p rg t", p=128)),
                                     (QK[sl][1][:], QTso[r0:r0 + 256, :].rearrange("(rg p) t -> p rg t", p=128))],
                              B_QK[sl], writes=[B_QK[sl]])
                    batches = []
                    vcount = 0
                    for si, (g, b) in enumerate(sets):
                        win, dil = GROUPS[g]
                        nblk = T // (128 * dil)
                        for r in range(dil):
                            prev = None
                            for n in range(nblk):
                                t0 = n * 128 * dil + r
                                cols = slice(t0, t0 + 127 * dil + 1, dil)
                                vi = vcount % NV
                                vcount += 1
                                batches.append(dict(si=si, g=g, b=b, dil=dil, t0=t0, cols=cols, vi=vi, prev=prev,
                                                    first=(r == 0 and n == 0)))
                                prev = (cols, vi)
                    load_set(0)

                    def emit_S(t, bt):
                        g, b, si = bt["g"], bt["b"], bt["si"]
                        sl = si % 2
                        if bt["first"] and si + 1 < len(sets):
                            load_set(si + 1)
                        sp_ = t % 3
                        psS = psall[:, (2 + 2 * sp_) * 512:(4 + 2 * sp_) * 512].rearrange("p (k m) -> p k m", m=256)
                        prev, cols = bt["prev"], bt["cols"]
                        wk = 256 if prev is not None else 128
                        Qe, Qo, Kt_ = QK[sl]

                        def sm_(e):
                            ins = None
                            for k in range(4):
                                rg = k // 2
                                Qx = Qe if k % 2 == 0 else Qo
                                ins = e.matmul(psS[:, k, 0:128], lhsT=Kt_[:, rg, cols], rhs=Qx[:, rg, cols], start=True, stop=True)
                                if prev is not None:
                                    ins = e.matmul(psS[:, k, 128:256], lhsT=Kt_[:, rg, prev[0]], rhs=Qx[:, rg, cols],
                                                   start=True, stop=True)
                            return ins
                        S.op("pe", sm_, reads=[B_QK[sl]], writes=[B_S2[sp_], B_S2b[sp_]])
                        S.op("act", lambda e: e.activation(out=Et2[sp_][:, :, 0:wk], in_=psS[:, :, 0:wk], func=AF.Exp, scale=0.125),
                             reads=[B_S2[sp_], B_S2b[sp_]], writes=[B_Et2[sp_]])
                        h0 = g * 8 + b * 4
                        S.op("dve", lambda e: e.tensor_tensor(out=Pt2[sp_][:, :, 0:wk], in0=Et2[sp_][:, :, 0:wk],
                                                              in1=Tbb[:, h0:h0 + 4, 0:wk], op=ALU.mult),
                             reads=[B_Et2[sp_], B_Tb], writes=[B_Pt2[sp_]])

                    def emit_V(bt):
                        g, b = bt["g"], bt["b"]
                        t0, dil, vi = bt["t0"], bt["dil"], bt["vi"]
                        S.dma("sp", [(Vt[vi][:], Vs[t0:t0 + 127 * dil + 1:dil, g * 8 + 4 * b:g * 8 + 4 * b + 4, :])], B_Vt[vi],
                              writes=[B_Vt[vi]])

                    def emit_PV(t, bt):
                        sp_ = t % 3
                        g, vi, prev, b = bt["g"], bt["vi"], bt["prev"], bt["b"]
                        o2 = t % 2
                        pO, BpO = pbanks[o2]

                        def pvm(e):
                            ins = None
                            for k in range(4):
                                oc = k * (HD + 1)
                                ins = e.matmul(pO[:, oc:oc + HD + 1], lhsT=Pt2[sp_][:, k, 0:128], rhs=Vt[vi][:, k, :],
                                               start=True, stop=(prev is None))
                                if prev is not None:
                                    ins = e.matmul(pO[:, oc:oc + HD + 1], lhsT=Pt2[sp_][:, k, 128:256], rhs=Vt[prev[1]][:, k, :],
                                                   start=False, stop=True)
                            return ins
                        rd = [B_Pt2[sp_], B_Vt[vi]] + ([B_Vt[prev[1]]] if prev is not None else [])
                        S.op("pe", pvm, reads=rd, writes=[BpO])
                        if o2 == 0:
                            S.op("act", lambda e: e.copy(out=ob[o2][:], in_=pO[:, 0:W4]), reads=[BpO], writes=[B_ob[o2]])
                        else:
                            S.op("dve", lambda e: e.tensor_copy(out=ob[o2][:], in_=pO[:, 0:W4]), reads=[BpO], writes=[B_ob[o2]])
                        t0, dil = bt["t0"], bt["dil"]
                        S.dma("sp", [(osc[g, t0:t0 + 127 * dil + 1:dil, b * W4:(b + 1) * W4], ob[o2][:])], B_ob[o2], reads=[B_ob[o2]])

                    for t in range(min(2, len(batches))):
                        emit_V(batches[t])
                    for t in range(len(batches) + 2):
                        if t + 2 < len(batches):
                            emit_V(batches[t + 2])
                        if t < len(batches):
                            emit_S(t, batches[t])
                        if t >= 2:
                            emit_PV(t - 2, batches[t - 2])

                S.barrier()

        if "D" in phases:
            with contextlib.ExitStack() as pd:
                Kc = [sbt(pd, "Kc%d" % i, [128, 512]) for i in range(2)]
                Vc = [sbt(pd, "Vc%d" % i, [128, 512]) for i in range(2)]
                B_Kc = [S.buf("Kc%d" % i) for i in range(2)]
                B_Vc = [S.buf("Vc%d" % i) for i in range(2)]
                Kcb = [sbt(pd, "Kcb%d" % i, [128, 512], BF16) for i in range(2)]
                B_Kcb = [S.buf() for _ in range(2)]
                KcT = [sbt(pd, "KcT%d" % i, [128, 4, 128], BF16) for i in range(2)]
                B_KcT = [S.buf() for _ in range(2)]
                Vca = sbt(pd, "Vca", [128, 13, 8, HD + 1], BF16)
                B_Vca = S.buf("Vca")
                S.op("pool", lambda e: e.memset(Vca[:].rearrange("p a b c -> p (a b c)"), 1.0), writes=[B_Vca])
                Pall = sbt(pd, "Pall", [128, 13, 8, 8], BF16)
                B_Pall = S.buf("Pall")
                Es = [sbt(pd, "Es%d" % i, [128, 64]) for i in range(2)]
                B_Es = [S.buf() for _ in range(2)]
                QTn = [sbt(pd, "QTn%d" % i, [128, 12, NST], BF16) for i in range(2)]
                KTn = sbt(pd, "KTn", [128, 12, NST], BF16)
                B_QKn = S.buf("QKn")
                S.dma("sp", [(QTn[0][:], QTse[:, T:TT].rearrange("(rg p) t -> p rg t", p=128)),
                             (QTn[1][:], QTso[:, T:TT].rearrange("(rg p) t -> p rg t", p=128)),
                             (KTn[:], KTs[:, T:TT].rearrange("(rg p) t -> p rg t", p=128))], B_QKn, writes=[B_QKn])
                Vn = sbt(pd, "Vn", [8, NQH, HD + 1], BF16)
                B_Vn = S.buf("Vn")
                Pn = sbt(pd, "Pn", [8, 3, 8, 8], BF16)
                En = sbt(pd, "En", [8, 3 * 64])
                B_Pn = S.buf("Pn")
                B_En = S.buf("En")
                obs = sbt(pd, "obs", [8, OW])
                B_obs = S.buf("obs")
                B_oscs = S.buf("oscs")
                kc_i = [0]

                def s_start(j):
                    S.dma("pool", [(Vn[:], Vs[T + j * DS:T + (j + 1) * DS, :, :])], B_Vn, writes=[B_Vn])

                def s_unit(j, gr):
                    g, r = CLASSES[gr]
                    qs_ = slice(j * DS, (j + 1) * DS)
                    dil = GROUPS[g][1]
                    s2 = kc_i[0] % 2
                    kc_i[0] += 1
                    S.dma("pool", [(Kc[s2][:], caches[g][j, r:r + 127 * dil + 1:dil, 0, :, :].rearrange("p h d -> p (h d)"))],
                          B_Kc[s2], writes=[B_Kc[s2]])
                    S.dma("pool", [(Vc[s2][:], caches[g][j, r:r + 127 * dil + 1:dil, 1, :, :].rearrange("p h d -> p (h d)"))],
                          B_Vc[s2], writes=[B_Vc[s2]])
                    S.op("act", lambda e: e.copy(out=Kcb[s2][:], in_=Kc[s2][:]), reads=[B_Kc[s2]], writes=[B_Kcb[s2]])
                    S.op("dve", lambda e: e.tensor_copy(out=Vca[:, gr, :, 0:HD], in_=Vc[s2][:].rearrange("p (h d) -> p h d", d=HD)),
                         reads=[B_Vc[s2]], writes=[B_Vca])
                    pb, Bpb = pbanks[4]
                    pbb = pb[:].bitcast(BF16)

                    def trk(e):
                        ins = None
                        for rg in range(4):
                            ins = e.transpose(out=pbb[:, rg * 128:(rg + 1) * 128], in_=Kcb[s2][:, rg * 128:(rg + 1) * 128],
                                              identity=identb[:, :])
                        return ins
                    S.op("pe", trk, reads=[B_Kcb[s2], B_identb], writes=[Bpb])
                    S.op("act", lambda e: e.copy(out=KcT[s2][:].rearrange("p a b -> p (a b)"), in_=pbb[:, 0:512]),
                         reads=[Bpb], writes=[B_KcT[s2]])
                    pS, BpS = pbanks[5]

                    def ssm(e):
                        ins = None
                        for hs in range(8):
                            rg = hs // 2
                            ins = e.matmul(pS[:, hs * 8:hs * 8 + 8], lhsT=KcT[s2][:, rg, :],
                                           rhs=QTn[hs % 2][:, g * 4 + rg, qs_], start=True, stop=True)
                        return ins
                    S.op("pe", ssm, reads=[B_KcT[s2], B_QKn], writes=[BpS])
                    S.op("act", lambda e: e.activation(out=Es[s2][:], in_=pS[:, 0:64], func=AF.Exp, scale=0.125),
                         reads=[BpS], writes=[B_Es[s2]])
                    S.op("dve", lambda e: e.tensor_tensor(out=Pall[:, gr, :, :].rearrange("p a b -> p (a b)"), in0=Es[s2][:],
                                                          in1=EBs[:, gr, :, :].rearrange("p a b -> p (a b)"), op=ALU.mult),
                         reads=[B_Es[s2], B_EBs], writes=[B_Pall])

                def s_finish(j):
                    qs_ = slice(j * DS, (j + 1) * DS)
                    pS, BpS = pbanks[5]

                    def snm(e):
                        ins = None
                        for g in range(3):
                            for hs in range(8):
                                rg = hs // 2
                                ins = e.matmul(pS[0:8, g * 64 + hs * 8:g * 64 + hs * 8 + 8], lhsT=KTn[:, g * 4 + rg, qs_],
                                               rhs=QTn[hs % 2][:, g * 4 + rg, qs_], start=True, stop=True)
                        return ins
                    S.op("pe", snm, reads=[B_QKn], writes=[BpS])
                    S.op("act", lambda e: e.activation(out=En[:], in_=pS[0:8, 0:192], func=AF.Exp, scale=0.125),
                         reads=[BpS], writes=[B_En])
                    S.op("dve", lambda e: e.tensor_tensor(out=Pn[:].rearrange("p a b c -> p (a b c)"), in0=En[:],
                                                          in1=EBn[:].rearrange("p a b c -> p (a b c)"), op=ALU.mult),
                         reads=[B_En, B_EBn], writes=[B_Pn])
                    pA, BpA = pbanks[6]
                    pB, BpB = pbanks[7]
                    for hs in range(8):
                        po_t, Bpo = (pA, BpA) if hs < 4 else (pB, BpB)
                        oc = (hs % 4) * (HD + 1)

                        def pvs(e):
                            ins = None
                            for gr, (g, r) in enumerate(CLASSES):
                                ins = e.matmul(po_t[0:8, oc:oc + HD + 1], lhsT=Pall[:, gr, hs, :], rhs=Vca[:, gr, hs, :],
                                               start=(gr == 0), stop=False)
                            for g in range(3):
                                ins = e.matmul(po_t[0:8, oc:oc + HD + 1], lhsT=Pn[:, g, hs, :], rhs=Vn[:, g * 8 + hs, :],
                                               start=False, stop=(g == 2))
                            return ins
                        S.op("pe", pvs, reads=[B_Pall, B_Vca, B_Pn, B_Vn], writes=[Bpo])
                    S.op("act", lambda e: e.copy(out=obs[:, 0:4 * (HD + 1)], in_=pA[0:8, 0:4 * (HD + 1)]), reads=[BpA], writes=[B_obs])
                    S.op("act", lambda e: e.copy(out=obs[:, 4 * (HD + 1):OW], in_=pB[0:8, 0:4 * (HD + 1)]), reads=[BpB], writes=[B_obs])
                    S.dma("sp", [(osc[0, T + j * DS:T + (j + 1) * DS, :], obs[:])], B_obs, reads=[B_obs], writes=[B_oscs])
                sitems = []
                for j in range(NS):
                    sitems.append((s_start, (j,)))
                    for gr in range(13):
                        sitems.append((s_unit, (j, gr)))
                    sitems.append((s_finish, (j,)))
                sitems.reverse()

                woB = sbt(pd, "woB", [128, 4, D], BF16)
                B_woB = S.buf("woB")
                S.dma("pool", [(woB[:], w_out_b.rearrange("(k p) c -> p k c", p=128))], B_woB, writes=[B_woB])
                o3 = [[sbt(pd, "o3_%d_%d" % (i, g), [128, OW]) for g in range(3)] for i in range(3)]
                B_o3 = [[S.buf("o3_%d_%d" % (i, g)) for g in range(3)] for i in range(3)]
                zt = [sbt(pd, "zt%d" % i, [128, 512], BF16) for i in range(3)]
                B_zt = [S.buf("zt%d" % i) for i in range(3)]
                x1t = [sbt(pd, "x1t%d" % i, [128, D]) for i in range(4)]
                B_x1t = [S.buf("x1t%d" % i) for i in range(4)]
                rden = [sbt(pd, "rden%d" % i, [128, 8]) for i in range(2)]
                B_rden = [S.buf() for _ in range(2)]
                om = [sbt(pd, "om%d" % i, [128, 512]) for i in range(2)]
                B_om = [S.buf() for _ in range(2)]
                og = [sbt(pd, "og%d" % i, [128, 512], BF16) for i in range(2)]
                B_og = [S.buf() for _ in range(2)]
                ogT = [sbt(pd, "ogT%d" % i, [128, 4, 128], BF16) for i in range(2)]
                B_ogT = [S.buf() for _ in range(2)]
                yt = [sbt(pd, "yt%d" % i, [128, D]) for i in range(2)]
                B_yt = [S.buf("yt%d" % i) for i in range(2)]

                def d_load(i):
                    rows, c0 = tile_rows(i)
                    s3, s4 = i % 3, i % 4
                    ng = 3 if i < NT else 1
                    for g in range(ng):
                        S.dma("sp", [(o3[s3][g][:rows], osc[g, c0:c0 + rows, :])], B_o3[s3][g], writes=[B_o3[s3][g]],
                              reads=([B_oscs] if i >= NT else []))
                    S.dma("sp", [(zt[s3][:rows], zss[c0:c0 + rows, :])], B_zt[s3], writes=[B_zt[s3]])
                    S.dma("sp", [(x1t[s4][:rows], x1s[c0:c0 + rows, :])], B_x1t[s4], writes=[B_x1t[s4]])

                def d_s0(i):
                    rows, c0 = tile_rows(i)
                    s2, s3 = i % 2, i % 3
                    if i < NT:
                        S.op("dve", lambda e: e.tensor_tensor(out=o3[s3][0][:rows], in0=o3[s3][0][:rows], in1=o3[s3][1][:rows], op=ALU.add),
                             reads=[B_o3[s3][0], B_o3[s3][1]], writes=[B_o3[s3][0]])
                        S.op("dve", lambda e: e.tensor_tensor(out=o3[s3][0][:rows], in0=o3[s3][0][:rows], in1=o3[s3][2][:rows], op=ALU.add),
                             reads=[B_o3[s3][0], B_o3[s3][2]], writes=[B_o3[s3][0]])
                    ov = o3[s3][0][:rows].rearrange("p (h d) -> p h d", d=HD + 1)
                    S.op("dve", lambda e: e.reciprocal(out=rden[s2][:rows].unsqueeze(2), in_=ov[:, :, HD:HD + 1]),
                         reads=[B_o3[s3][0]], writes=[B_rden[s2]])
                    S.op("dve", lambda e: e.tensor_tensor(out=om[s2][:rows].rearrange("p (h d) -> p h d", d=HD), in0=ov[:, :, 0:HD],
                                                          in1=rden[s2][:rows].unsqueeze(2).to_broadcast([rows, 8, HD]), op=ALU.mult),
                         reads=[B_o3[s3][0], B_rden[s2]], writes=[B_om[s2]])
                    S.op("dve", lambda e: e.tensor_tensor(out=og[s2][:rows], in0=om[s2][:rows], in1=zt[s3][:rows], op=ALU.mult),
                         reads=[B_om[s2], B_zt[s3]], writes=[B_og[s2]])

                drr = [0]

                def dbank():
                    t_, b_ = pbanks[drr[0] % 4]
                    drr[0] += 1
                    return t_, b_

                def d_s1(i):
                    rows, c0 = tile_rows(i)
                    s2 = i % 2
                    pb, Bpb = dbank()
                    pbb = pb[:].bitcast(BF16)

                    def tro(e):
                        ins = None
                        for kc in range(4):
                            ins = e.transpose(out=pbb[:, kc * 128:kc * 128 + rows], in_=og[s2][:rows, kc * 128:(kc + 1) * 128],
                                              identity=identb[:rows, :rows])
                        return ins
                    S.op("pe", tro, reads=[B_og[s2], B_identb], writes=[Bpb])
                    S.op("act", lambda e: e.copy(out=ogT[s2][:, :, :rows], in_=pbb[:, 0:512].rearrange("p (k t) -> p k t", t=128)[:, :, :rows]),
                         reads=[Bpb], writes=[B_ogT[s2]])

                def d_s2(i):
                    rows, c0 = tile_rows(i)
                    s2, s4 = i % 2, i % 4
                    for half in range(2):
                        pb, Bpb = dbank()

                        def ym2(e):
                            ins = None
                            for kc in range(4):
                                ins = e.matmul(pb[:rows, :], lhsT=ogT[s2][:, kc, :rows], rhs=woB[:, kc, half * 512:(half + 1) * 512],
                                               start=(kc == 0), stop=(kc == 3))
                            return ins
                        S.op("pe", ym2, reads=[B_ogT[s2], B_woB], writes=[Bpb])
                        S.op("dve" if half == 0 else "act", lambda e: (e.tensor_tensor(
                            out=yt[s2][:rows, half * 512:(half + 1) * 512], in0=pb[:rows, :],
                            in1=x1t[s4][:rows, half * 512:(half + 1) * 512], op=ALU.add)),
                            reads=[Bpb, B_x1t[s4]], writes=[B_yt[s2]]) if half == 0 else S.op("dve", lambda e: e.tensor_tensor(
                            out=yt[s2][:rows, half * 512:(half + 1) * 512], in0=pb[:rows, :],
                            in1=x1t[s4][:rows, half * 512:(half + 1) * 512], op=ALU.add),
                            reads=[Bpb, B_x1t[s4]], writes=[B_yt[s2]])
                    dst = y_p[c0:c0 + rows, :] if i < NT else y_s[:, :]
                    S.dma("sp", [(dst, yt[s2][:rows])], B_yt[s2], reads=[B_yt[s2]])
                stages = [d_load, d_s0, d_s1, d_s2]
                per_step = -(-len(sitems) // max(1, NTT - 4))
                for step in range(NTT + len(stages) - 1):
                    for _ in range(per_step):
                        if sitems:
                            fn_, args_ = sitems.pop()
                            fn_(*args_)
                    if step == NT:
                        while sitems:
                            fn_, args_ = sitems.pop()
                            fn_(*args_)
                    for st in range(len(stages) - 1, -1, -1):
                        i = step - st
                        if 0 <= i < NTT:
                            stages[st](i)
        ps_.close()
        S.finish()
        build.stats = dict(cnt=dict(S.cnt), nsem=len(S.dbufs_all) + 5, maxd=max([c for _, c in S.free_sems] + [b.dcnt for b in S.dbufs] + [0]))
    return nc, dbg_out


T_FULL = 4096
N_CORES = 8
_CACHE = {}


def _get_nc(T):
    if T not in _CACHE:
        _CACHE[T] = build(T)[0]
    return _CACHE[T]


def make_in_maps(T, x_prompt, x_sample, state_mlstm_C, state_mlstm_n, state_mlstm_m,
                 cache_kv_w128, cache_kv_w512, cache_kv_w2048,
                 norm_a, w_in_a, b_gates_a, hnorm_a, w_out_a, norm_kv, w_kv, k_norm,
                 norm_b, w_in_b, q_norm, rel_bias, w_out_b):
    f = lambda a: np.ascontiguousarray(np.asarray(a, dtype=np.float32))
    consts = make_consts(T)
    shared = dict(
        norm_a=f(norm_a).reshape(1, D), w_in_a=f(w_in_a)[0], b_gates=f(b_gates_a).reshape(1, 2 * H),
        hnorm=f(hnorm_a).reshape(DI), w_out_a=f(w_out_a)[0], norm_kv=f(norm_kv), w_kv=f(w_kv),
        k_norm=f(k_norm).reshape(1, HD), norm_b=f(norm_b).reshape(D), w_in_b=f(w_in_b)[0],
        q_norm=f(q_norm).reshape(1, HD), rel_bias=f(rel_bias), w_out_b=f(w_out_b)[0])
    shared.update(consts)
    xp = f(x_prompt)
    xs = f(x_sample)
    sC = f(state_mlstm_C)[0]
    sn_ = f(state_mlstm_n)[0]
    sm_ = f(state_mlstm_m)[0]
    c1, c5, c20 = f(cache_kv_w128), f(cache_kv_w512), f(cache_kv_w2048)
    maps = []
    for c in range(xp.shape[0]):
        sl = slice(c * NS, (c + 1) * NS)
        m = dict(shared)
        m.update(xp=xp[c], xs=xs[sl].reshape(NST, D), sC=sC[sl], sn=sn_[sl], sm=sm_[sl],
                 c128=c1[sl], c512=c5[sl], c2048=c20[sl])
        maps.append(m)
    return maps


def gather(results, T):
    n = len(results)
    cat = lambda k: np.stack([np.asarray(r[k], np.float32) for r in results], 0)
    y_p = cat("y_p")
    y_s = cat("y_s").reshape(n * NS, DS, D)
    C_p = cat("C_p")[None]
    n_p = cat("n_p")[None]
    m_p = cat("m_p").reshape(n, H)[None]
    C_s = cat("C_s").reshape(n * NS, H, DH, DH)[None]
    n_s = cat("n_s").reshape(n * NS, H, DH)[None]
    m_s = cat("m_s").reshape(n * NS, H)[None]
    kv = [cat(k) for k in ("kv128_p", "kv512_p", "kv2048_p")]
    kvs_ = [cat(k).reshape(n * NS, DS, 2, 8, HD) for k in ("kv128_s", "kv512_s", "kv2048_s")]
    return (y_p, y_s, C_p, n_p, m_p, C_s, n_s, m_s, kv[0], kv[1], kv[2], kvs_[0], kvs_[1], kvs_[2])


def kernel(**inputs):
    T = int(np.asarray(inputs["x_prompt"]).shape[1])
    n = int(np.asarray(inputs["x_prompt"]).shape[0])
    maps = make_in_maps(T, **inputs)
    nc = _get_nc(T)
    res = run_bass_kernel_spmd(nc, maps, core_ids=list(range(n)))
    return gather(res.results, T)
```

```python
import contextlib
import math
import numpy as np
import concourse.bass as bass
import concourse.mybir as mybir
from concourse.bass_utils import run_bass_kernel_spmd

F32 = mybir.dt.float32
BF16 = mybir.dt.bfloat16
AF = mybir.ActivationFunctionType
ALU = mybir.AluOpType
AX = mybir.AxisListType

D = 1024
DI = 2048
H = 4
DH = 512
NQH = 24
HD = 64
QW = 1536
NS = 4
DS = 8
NST = NS * DS
EPS = 1e-6
GROUPS = ((128, 1), (512, 4), (2048, 16))
NBUCK = 32
MAXDIST = 2048


class Buf:
    __slots__ = ("name", "w", "r", "dsem", "dcnt")

    def __init__(self, name):
        self.name = name
        self.w = []
        self.r = []
        self.dsem = None
        self.dcnt = 0


class Sched:
    ENG = ("pe", "act", "dve", "pool", "sp")

    def __init__(self, nc, stack):
        self.nc = nc
        self.stack = stack
        self.sem = {}
        self.cnt = {}
        self.waited = {}
        for e in self.ENG:
            self.sem[e] = stack.enter_context(nc.semaphore("sem_" + e))
            self.cnt[e] = 0
            self.waited[e] = {}
        self.semeng = {id(self.sem[e]): e for e in self.ENG}
        self.hw = {"pe": nc.tensor, "act": nc.scalar, "dve": nc.vector, "pool": nc.gpsimd, "sp": nc.sync}
        self.dbufs = []
        self.dbufs_all = []
        self.free_sems = []
        self.nb = 0

    def buf(self, name=None):
        self.nb += 1
        return Buf(name or ("b%d" % self.nb))

    def _emit(self, eng, waits, fn, inc):
        engine = self.hw[eng]
        for (s_, v) in waits:
            engine.wait_ge(s_, v)
        if fn is not None:
            ins = fn(engine)
            if inc is not None:
                ins.then_inc(inc[0], inc[1])

    def _waits(self, eng, deps):
        need = {}
        for (s, v) in deps:
            k = id(s)
            if self.semeng.get(k) == eng and eng in ("pe", "sp"):
                continue
            if self.waited[eng].get(k, 0) >= v:
                continue
            if k not in need or need[k][1] < v:
                need[k] = (s, v)
        out = []
        for k, (s, v) in need.items():
            self.waited[eng][k] = v
            out.append((s, v))
        return out

    def _deps(self, reads, writes):
        deps = []
        for b in reads:
            deps += b.w
        for b in writes:
            deps += b.w
            deps += b.r
        return deps

    def op(self, eng, fn, reads=(), writes=()):
        waits = self._waits(eng, self._deps(reads, writes))
        self.cnt[eng] += 1
        tok = (self.sem[eng], self.cnt[eng])
        self._emit(eng, waits, fn, (self.sem[eng], 1))
        for b in reads:
            b.r.append(tok)
        for b in writes:
            b.w = [tok]
            b.r = []
        return tok

    def dma(self, eng, pairs, owner, reads=(), writes=(), **kw):
        if owner.dsem is None:
            if self.free_sems:
                owner.dsem, owner.dcnt = self.free_sems.pop()
            else:
                owner.dsem = self.stack.enter_context(self.nc.semaphore("dsem_%d" % len(self.dbufs_all)))
                owner.dcnt = 0
            self.dbufs.append(owner)
            self.dbufs_all.append(owner)
        deps = self._deps(reads, writes)
        if owner.dcnt:
            deps.append((owner.dsem, owner.dcnt))
        waits = self._waits(eng, deps)
        for i, (o, i_) in enumerate(pairs):
            def fn(e, o=o, i_=i_):
                return e.dma_start(out=o, in_=i_, **kw)
            self._emit(eng, waits if i == 0 else [], fn, (owner.dsem, 16))
        owner.dcnt += 16 * len(pairs)
        tok = (owner.dsem, owner.dcnt)
        for b in reads:
            b.r.append(tok)
        for b in writes:
            b.w = [tok]
            b.r = []
        return tok

    def barrier(self):
        toks = [(self.sem[e], self.cnt[e]) for e in self.ENG if self.cnt[e]]
        for b in self.dbufs:
            if b.dcnt:
                toks.append((b.dsem, b.dcnt))
        for e in self.ENG:
            w = self._waits(e, [t for t in toks if self.semeng.get(id(t[0])) != e])
            if w:
                self._emit(e, w, None, None)
        for b in self.dbufs:
            self.free_sems.append((b.dsem, b.dcnt))
            b.dsem = None
            b.dcnt = 0
        self.dbufs = []

    def finish(self):
        toks = [(self.sem[e], self.cnt[e]) for e in self.ENG if self.cnt[e] and e != "sp"]
        for b in self.dbufs:
            if b.dcnt:
                toks.append((b.dsem, b.dcnt))
        w = self._waits("sp", toks)
        self._emit("sp", w, None, None)


def _t5_bucket_np(dist):
    exact = NBUCK // 2
    d = np.maximum(dist, 1).astype(np.float32)
    large = exact + (np.log(d / np.float32(exact)) / np.float32(math.log(MAXDIST / exact))
                     * np.float32(NBUCK - exact)).astype(np.int32)
    return np.where(dist < exact, dist, np.minimum(large, NBUCK - 1))


def make_consts(T):
    NC = T // 128
    TT = T + NST
    NCH = NC + NS
    c = {}
    c["c_ident"] = np.eye(128, dtype=np.float32)
    c["c_J"] = np.eye(128, dtype=np.float32)[::-1].copy()
    s_ = np.arange(128)[:, None]
    t_ = np.arange(128)[None, :]
    c["c_maskT"] = (s_ <= t_).astype(np.float32)
    rm = np.ones((4, TT), np.float32)
    rm[:, 0:T:128] = 0.0
    rm[:, T:TT:DS] = 0.0
    c["c_rm"] = rm
    dm = np.zeros((4, 4, NCH), np.float32)
    for h in range(4):
        dm[h, h, :] = 1.0
    c["c_dmask"] = dm.reshape(4, 4 * NCH)
    ohp = np.zeros((NBUCK, 3, 384), np.float32)
    for g, (win, dil) in enumerate(GROUPS):
        J = win // dil
        for j in range(384):
            delta = j - 127
            if 0 <= delta <= J:
                b = int(_t5_bucket_np(np.array([delta * dil]))[0])
                ohp[b, g, j] = 1.0
    c["c_ohp"] = ohp.reshape(NBUCK, 3 * 384)
    ohs = np.zeros((NBUCK, 13, 8, 128), np.float32)
    ohn = np.zeros((NBUCK, 3, 8, 8), np.float32)
    gr = 0
    for g, (win, dil) in enumerate(GROUPS):
        L = win
        J = win // dil
        for r in range(dil if g < 2 else 8):
            for s in range(8):
                if s % dil != r % dil:
                    continue
                if g == 2 and s != r:
                    continue
                for i in range(128):
                    row = r + dil * i
                    num = L + s - row
                    if num < 0 or num % dil:
                        continue
                    j = num // dil
                    if 0 <= j <= J:
                        b = int(_t5_bucket_np(np.array([dil * j]))[0])
                        ohs[b, gr, s, i] = 1.0
            gr += 1
        for s in range(8):
            for k in range(8):
                num = s - k
                if num < 0 or num % dil:
                    continue
                j = num // dil
                if j <= J:
                    b = int(_t5_bucket_np(np.array([dil * j]))[0])
                    ohn[b, g, s, k] = 1.0
    assert gr == 13
    c["c_ohs"] = ohs.reshape(NBUCK, 13 * 8 * 128)
    c["c_ohn"] = ohn.reshape(NBUCK, 3 * 8 * 8)
    return c


CLASSES = [(0, 0)] + [(1, r) for r in range(4)] + [(2, r) for r in range(8)]


def build(T, phases="ABCD", dbg=()):
    NT = T // 128
    NC = NT
    TT = T + NST
    NCH = NC + NS
    NTT = NT + 1
    nc = bass.Bass("TRN2", target_bir_lowering=False)

    def din(name, shape, dt=F32):
        return nc.dram_tensor(name, list(shape), dt, kind="ExternalInput").ap()

    def dout(name, shape, dt=F32):
        return nc.dram_tensor(name, list(shape), dt, kind="ExternalOutput").ap()

    def dscr(name, shape, dt):
        return nc.dram_tensor(name, list(shape), dt, kind="Internal").ap()

    xp = din("xp", [T, D])
    xs = din("xs", [NST, D])
    sC = din("sC", [NS, H, DH, DH])
    sn = din("sn", [NS, H, DH])
    sm = din("sm", [NS, H])
    caches = [din("c128", [NS, 128, 2, 8, HD]), din("c512", [NS, 512, 2, 8, HD]), din("c2048", [NS, 2048, 2, 8, HD])]
    norm_a = din("norm_a", [1, D])
    w_in_a = din("w_in_a", [D, 5 * DI + 2 * H])
    b_gates = din("b_gates", [1, 2 * H])
    hnorm = din("hnorm", [DI])
    w_out_a = din("w_out_a", [DI, D])
    norm_kv = din("norm_kv", [D])
    w_kv = din("w_kv", [D, 2 * QW])
    k_norm = din("k_norm", [1, HD])
    norm_b = din("norm_b", [D])
    w_in_b = din("w_in_b", [D, QW + 512])
    q_norm = din("q_norm", [1, HD])
    rel_bias = din("rel_bias", [NBUCK, NQH])
    w_out_b = din("w_out_b", [512, D])
    c_ident = din("c_ident", [128, 128])
    c_J = din("c_J", [128, 128])
    c_maskT = din("c_maskT", [128, 128])
    c_rm = din("c_rm", [4, TT])
    c_dmask = din("c_dmask", [4, 4 * NCH])
    c_ohp = din("c_ohp", [NBUCK, 3 * 384])
    c_ohs = din("c_ohs", [NBUCK, 13 * 8 * 128])
    c_ohn = din("c_ohn", [NBUCK, 3 * 8 * 8])

    y_p = dout("y_p", [T, D])
    y_s = dout("y_s", [NST, D])
    C_p = dout("C_p", [H, DH, DH])
    n_p = dout("n_p", [H, DH])
    m_p = dout("m_p", [H, 1])
    C_s = dout("C_s", [NS, H, DH, DH])
    n_s = dout("n_s", [NS, H, DH])
    m_s = dout("m_s", [NS, H])
    kvp = [dout("kv128_p", [min(128, T), 2, 8, HD]), dout("kv512_p", [min(512, T), 2, 8, HD]),
           dout("kv2048_p", [min(2048, T), 2, 8, HD])]
    kvs = [dout("kv128_s", [NST, 2, 8, HD]), dout("kv512_s", [NST, 2, 8, HD]), dout("kv2048_s", [NST, 2, 8, HD])]

    hf = dscr("hf_scr", [TT, DI], BF16)
    x1s = dscr("x1_scr", [TT, D], F32)
    KTs = dscr("KT_scr", [QW, TT], BF16)
    QTs = dscr("QT_scr", [QW, TT], BF16)
    QTse = dscr("QTe_scr", [QW, TT], BF16)
    QTso = dscr("QTo_scr", [QW, TT], BF16)
    Vs = dscr("V_scr", [TT, NQH, HD + 1], BF16)
    zss = dscr("zs_scr", [TT, 512], BF16)
    osc = dscr("o_scr", [3, TT, 8 * (HD + 1)], F32)
    vecs = dscr("vec_scr", [NQH, 384], F32)

    dbg_out = {}

    with contextlib.ExitStack() as top:
        S = Sched(nc, top)

        def sbt(stack, name, shape, dt=F32):
            return stack.enter_context(nc.sbuf_tensor(name, list(shape), dt))

        psall = top.enter_context(nc.psum_tensor("psall", [128, 8 * 512], F32))
        pbanks = []
        for i in range(8):
            pbanks.append((psall[:, i * 512:(i + 1) * 512], S.buf("pb%d" % i)))
        rr = [0]

        def bank():
            t, b = pbanks[rr[0] % 7]
            rr[0] += 1
            return t, b
        psm = pbanks[7][0]
        B_psS = B_psD = B_psn = pbanks[7][1]

        identf = sbt(top, "identf", [128, 128])
        identb = sbt(top, "identb", [128, 128], BF16)
        cm05 = sbt(top, "cm05", [128, 1])
        B_const = S.buf("const")
        S.dma("sp", [(identf[:], c_ident[:, :])], B_const, writes=[B_const])
        B_identb = S.buf("identb")
        S.op("dve", lambda e: e.tensor_copy(out=identb[:], in_=identf[:]), reads=[B_const], writes=[B_identb])
        B_cm05 = S.buf("cm05")
        S.op("pool", lambda e: e.memset(cm05[:], -0.5), writes=[B_cm05])
        ZW = 516
        assert TT % ZW == 0 or True
        zt_ = sbt(top, "zeroT", [64, ZW], BF16)
        B_zt_ = S.buf("zeroT")
        S.op("pool", lambda e: e.memset(zt_[:], 0.0), writes=[B_zt_])
        nz = TT // ZW
        rem = TT - nz * ZW
        zp = []
        for (dst_, lo) in ((QTse, 64), (QTso, 0)):
            v_ = dst_.rearrange("(rg p) t -> p rg t", p=128)[lo:lo + 64, :, :]
            for rg_ in range(12):
                if nz:
                    zp.append((v_[:, rg_, 0:nz * ZW].rearrange("p (a b) -> p a b", b=ZW),
                               zt_[:].unsqueeze(1).to_broadcast([64, nz, ZW])))
            if rem:
                zp.append((v_[:, :, nz * ZW:TT], zt_[:, 0:rem].unsqueeze(1).to_broadcast([64, 12, rem])))
        zero_fill = [(lambda pr=pr: S.dma("sp", [pr], B_zt_, reads=[B_zt_])) for pr in zp]
        if "A" not in phases:
            while zero_fill:
                zero_fill.pop()()

        def dump(name, ap_sb, shape, owner, dt=F32):
            o = dout("dbg_" + name, shape, dt)
            dbg_out[name] = o
            S.dma("sp", [(o, ap_sb)], owner, reads=[owner])

        def rsqrt_small(rows, out_ap, in_ap, scale, add, B_in, B_out, tmp_ap, B_tmp):
            S.op("dve", lambda e: e.tensor_scalar(out=tmp_ap, in0=in_ap, scalar1=scale, scalar2=add,
                                                  op0=ALU.mult, op1=ALU.add), reads=[B_in], writes=[B_tmp])
            S.op("pool", lambda e: e.tensor_tensor(out=out_ap, in0=tmp_ap, in1=cm05[:rows], op=ALU.pow),
                 reads=[B_tmp, B_cm05], writes=[B_out])

        if "A" in phases:
            with contextlib.ExitStack() as pa:
                xnT = sbt(pa, "xnT", [128, 8, TT], BF16)
                B_xnT = S.buf("xnT")
                uf_tok = sbt(pa, "uf_tok", [128, NTT * 8])
                ub_tok = sbt(pa, "ub_tok", [128, NTT * 8], BF16)
                ufs = sbt(pa, "ufs", [8, NS * 8])
                ubs = sbt(pa, "ubs", [8, NS * 8], BF16)
                a_bc = sbt(pa, "a_bc", [128, 4 * NCH])
                B_uf = S.buf("uf")
                B_abc = S.buf("abc")
                maskT = sbt(pa, "maskT", [128, 128])
                B_maskT = S.buf("maskT")
                S.dma("sp", [(maskT[:], c_maskT[:, :])], B_maskT, writes=[B_maskT])

                with contextlib.ExitStack() as p0:
                    g_bc = sbt(p0, "g_bc", [128, D])
                    B_gbc = S.buf("gbc")
                    S.dma("sp", [(g_bc[:], norm_a[0:1, :].to_broadcast([128, D]))], B_gbc, writes=[B_gbc])
                    xt = [sbt(p0, "xt%d" % i, [128, D]) for i in range(3)]
                    B_xt = [S.buf() for _ in range(3)]
                    xn = [sbt(p0, "xn%d" % i, [128, D], BF16) for i in range(2)]
                    B_xn = [S.buf() for _ in range(2)]
                    junk = sbt(p0, "junk0", [128, D], BF16)
                    B_junk = S.buf()
                    smt = sbt(p0, "smt0", [128, 9])
                    B_ss = [S.buf() for _ in range(3)]
                    B_tt = [S.buf() for _ in range(3)]
                    B_rs = [S.buf() for _ in range(3)]
                    a0bank = {}

                    def a0_ld(i):
                        rows = 128 if i < NT else NST
                        src = xp[i * 128:(i + 1) * 128, :] if i < NT else xs[:, :]
                        s3 = i % 3
                        S.dma("sp", [(xt[s3][:rows], src)], B_xt[s3], writes=[B_xt[s3]])

                    def a0_s0(i):
                        rows = 128 if i < NT else NST
                        s3 = i % 3
                        S.op("act", lambda e: e.activation(
                            out=junk[:rows], in_=xt[s3][:rows], func=AF.Square, accum_out=smt[:rows, s3:s3 + 1]),
                            reads=[B_xt[s3]], writes=[B_junk, B_ss[s3]])
                        rsqrt_small(rows, smt[:rows, 6 + s3:7 + s3], smt[:rows, s3:s3 + 1], 1.0 / D, EPS,
                                    B_ss[s3], B_rs[s3], smt[:rows, 3 + s3:4 + s3], B_tt[s3])

                    def a0_s1(i):
                        rows = 128 if i < NT else NST
                        s3, s2 = i % 3, i % 2
                        S.op("dve", lambda e: e.scalar_tensor_tensor(
                            out=xn[s2][:rows], in0=xt[s3][:rows], scalar=smt[:rows, 6 + s3:7 + s3], in1=g_bc[:rows],
                            op0=ALU.mult, op1=ALU.mult), reads=[B_xt[s3], B_rs[s3], B_gbc], writes=[B_xn[s2]])
                        pb, Bpb = bank()
                        pbb = pb[:].bitcast(BF16)
                        a0bank[i] = (pbb, Bpb)

                        def tr(e):
                            ins = None
                            for kc in range(8):
                                ins = e.transpose(out=pbb[:, kc * 128:kc * 128 + rows],
                                                  in_=xn[s2][:rows, kc * 128:(kc + 1) * 128],
                                                  identity=identb[:rows, :rows])
                            return ins
                        S.op("pe", tr, reads=[B_xn[s2], B_identb], writes=[Bpb])

                    def a0_s2(i):
                        rows = 128 if i < NT else NST
                        c0 = i * 128
                        pbb, Bpb = a0bank.pop(i)
                        S.op("act", lambda e: e.copy(
                            out=xnT[:, :, c0:c0 + rows],
                            in_=pbb.rearrange("p (k t) -> p k t", t=128)[:, :, :rows]),
                            reads=[Bpb], writes=[B_xnT])
                    a0st = [a0_ld, a0_s0, a0_s1, a0_s2]
                    for step in range(NTT + len(a0st) - 1):
                        for st in range(len(a0st) - 1, -1, -1):
                            i = step - st
                            if 0 <= i < NTT:
                                a0st[st](i)

                    Wg = sbt(p0, "Wg", [128, 8, 8], BF16)
                    B_Wg = S.buf("Wg")
                    S.dma("pool", [(Wg[:], w_in_a[:, 5 * DI:5 * DI + 8].rearrange("(k p) c -> p k c", p=128))],
                          B_Wg, writes=[B_Wg])
                    bgt = sbt(p0, "bgt", [4, 2])
                    B_bg = S.buf("bg")
                    S.dma("sp", [(bgt[:, 0:1], b_gates[0:1, 0:4].rearrange("o c -> c o")),
                                 (bgt[:, 1:2], b_gates[0:1, 4:8].rearrange("o c -> c o"))], B_bg, writes=[B_bg])
                    GA = sbt(p0, "GA", [4, TT])
                    GB = sbt(p0, "GB", [4, TT])
                    GC = sbt(p0, "GC", [4, TT])
                    rm = sbt(p0, "rm", [4, TT])
                    B_GA, B_GB, B_GC, B_rm = S.buf("GA"), S.buf("GB"), S.buf("GC"), S.buf("rm")
                    S.dma("sp", [(rm[:], c_rm[:, :])], B_rm, writes=[B_rm])
                    dmk = sbt(p0, "dmk", [4, 4 * NCH])
                    B_dmk = S.buf("dmk")
                    S.dma("sp", [(dmk[:], c_dmask[:, :])], B_dmk, writes=[B_dmk])
                    col = 0
                    while col < TT:
                        w = min(512, TT - col)
                        for which, dst, Bd in ((0, GA, B_GA), (1, GB, B_GB)):
                            pb, Bpb = bank()

                            def gm(e, which=which, col=col, w=w, pb=pb):
                                ins = None
                                for kc in range(8):
                                    ins = e.matmul(pb[0:4, 0:w], lhsT=Wg[:, kc, which * 4:which * 4 + 4],
                                                   rhs=xnT[:, kc, col:col + w], start=(kc == 0), stop=(kc == 7))
                                return ins
                            S.op("pe", gm, reads=[B_Wg, B_xnT], writes=[Bpb])
                            S.op("act", lambda e, which=which, col=col, w=w, pb=pb, dst=dst: e.activation(
                                out=dst[:, col:col + w], in_=pb[0:4, 0:w], func=AF.Identity,
                                bias=bgt[:, which:which + 1]), reads=[Bpb, B_bg], writes=[Bd])
                        col += w
                    S.op("act", lambda e: e.activation(out=GB[:], in_=GB[:], func=AF.Exp, scale=-1.0),
                         reads=[B_GB], writes=[B_GB])
                    S.op("act", lambda e: e.activation(out=GB[:], in_=GB[:], func=AF.Ln, bias=1.0),
                         reads=[B_GB], writes=[B_GB])
                    S.op("dve", lambda e: e.tensor_tensor_scan(out=GC[:], data0=rm[:], data1=GB[:], initial=0.0,
                                                               op0=ALU.mult, op1=ALU.add),
                         reads=[B_rm, B_GB], writes=[B_GC])
                    S.op("dve", lambda e: e.tensor_tensor(out=GA[:], in0=GA[:], in1=GC[:], op=ALU.add),
                         reads=[B_GA, B_GC], writes=[B_GA])
                    gsm = sbt(p0, "gsm", [4, 8 * NCH + 8])
                    B_gsm = S.buf("gsm")
                    Acol = gsm[:, 0:NCH]
                    bLc = gsm[:, NCH:2 * NCH]
                    mcol = gsm[:, 2 * NCH:3 * NCH]
                    Mcol = gsm[:, 3 * NCH:4 * NCH]
                    mpr = gsm[:, 4 * NCH:5 * NCH]
                    acol = gsm[:, 5 * NCH:6 * NCH]
                    m0T = gsm[:, 6 * NCH:6 * NCH + NS]
                    S.dma("sp", [(m0T, sm.rearrange("s h -> h s"))], B_gsm, writes=[B_gsm],
                          allow_slow_non_contiguous=True)
                    S.op("dve", lambda e: e.tensor_reduce(out=gsm[:, 0:NC], in_=GA[:, 0:T].rearrange("p (c k) -> p c k", k=128),
                                                          axis=AX.X, op=ALU.max), reads=[B_GA], writes=[B_gsm])
                    S.op("dve", lambda e: e.tensor_reduce(out=gsm[:, NC:NCH], in_=GA[:, T:TT].rearrange("p (c k) -> p c k", k=DS),
                                                          axis=AX.X, op=ALU.max), reads=[B_GA], writes=[B_gsm])
                    S.op("dve", lambda e: e.tensor_copy(out=gsm[:, NCH:NCH + NC],
                                                        in_=GC[:, 0:T].rearrange("p (c k) -> p c k", k=128)[:, :, 127]),
                         reads=[B_GC], writes=[B_gsm])
                    S.op("dve", lambda e: e.tensor_copy(out=gsm[:, NCH + NC:2 * NCH],
                                                        in_=GC[:, T:TT].rearrange("p (c k) -> p c k", k=DS)[:, :, DS - 1]),
                         reads=[B_GC], writes=[B_gsm])
                    S.op("dve", lambda e: e.tensor_tensor_scan(out=gsm[:, 2 * NCH:2 * NCH + NC], data0=gsm[:, 0:NC],
                                                               data1=gsm[:, NCH:NCH + NC], initial=0.0,
                                                               op0=ALU.max, op1=ALU.subtract),
                         reads=[B_gsm], writes=[B_gsm])
                    S.op("dve", lambda e: e.memset(gsm[:, 4 * NCH:4 * NCH + 1], 0.0), reads=[B_gsm], writes=[B_gsm])
                    if NC > 1:
                        S.op("dve", lambda e: e.tensor_copy(out=gsm[:, 4 * NCH + 1:4 * NCH + NC],
                                                            in_=gsm[:, 2 * NCH:2 * NCH + NC - 1]),
                             reads=[B_gsm], writes=[B_gsm])
                    S.op("dve", lambda e: e.tensor_copy(out=gsm[:, 4 * NCH + NC:5 * NCH], in_=m0T),
                         reads=[B_gsm], writes=[B_gsm])
                    S.op("dve", lambda e: e.tensor_tensor(out=Mcol, in0=mpr, in1=Acol, op=ALU.max),
                         reads=[B_gsm], writes=[B_gsm])
                    S.op("dve", lambda e: e.tensor_tensor(out=gsm[:, 2 * NCH + NC:3 * NCH], in0=gsm[:, 3 * NCH + NC:4 * NCH],
                                                          in1=gsm[:, NCH + NC:2 * NCH], op=ALU.subtract),
                         reads=[B_gsm], writes=[B_gsm])
                    S.op("dve", lambda e: e.tensor_tensor(out=acol, in0=mpr, in1=Mcol, op=ALU.subtract),
                         reads=[B_gsm], writes=[B_gsm])
                    S.op("act", lambda e: e.activation(out=acol, in_=acol, func=AF.Exp), reads=[B_gsm], writes=[B_gsm])
                    S.dma("sp", [(m_p[:, :], gsm[:, 2 * NCH + NC - 1:2 * NCH + NC])], B_gsm, reads=[B_gsm])
                    S.dma("sp", [(m_s.rearrange("s h -> h s"), gsm[:, 2 * NCH + NC:3 * NCH])], B_gsm, reads=[B_gsm],
                          allow_slow_non_contiguous=True)
                    for (dst, Bd, srcg, Bs) in ((GB, B_GB, GA, B_GA), (GC, B_GC, GC, B_GC)):
                        S.op("dve", lambda e, dst=dst, srcg=srcg: e.tensor_tensor(
                            out=dst[:, 0:T].rearrange("p (c k) -> p c k", k=128),
                            in0=srcg[:, 0:T].rearrange("p (c k) -> p c k", k=128),
                            in1=gsm[:, 3 * NCH:3 * NCH + NC].unsqueeze(2).to_broadcast([4, NC, 128]), op=ALU.subtract),
                            reads=[Bs, B_gsm], writes=[Bd])
                        S.op("dve", lambda e, dst=dst, srcg=srcg: e.tensor_tensor(
                            out=dst[:, T:TT].rearrange("p (c k) -> p c k", k=DS),
                            in0=srcg[:, T:TT].rearrange("p (c k) -> p c k", k=DS),
                            in1=gsm[:, 3 * NCH + NC:4 * NCH].unsqueeze(2).to_broadcast([4, NS, DS]), op=ALU.subtract),
                            reads=[Bs, B_gsm], writes=[Bd])
                        S.op("act", lambda e, dst=dst: e.activation(out=dst[:], in_=dst[:], func=AF.Exp),
                             reads=[Bd], writes=[Bd])
                    pb, Bpb = bank()

                    def trg(e, pb=pb):
                        ins = None
                        for i in range(NTT):
                            rows = 128 if i < NT else NST
                            for k, srcg in enumerate((GB, GC)):
                                ins = e.matmul(pb[:rows, i * 8 + 4 * k:i * 8 + 4 * k + 4],
                                               lhsT=srcg[:, i * 128:i * 128 + rows], rhs=identf[0:4, 0:4],
                                               start=True, stop=True)
                        return ins
                    S.op("pe", trg, reads=[B_GB, B_GC, B_const], writes=[Bpb])
                    S.op("dve", lambda e, pb=pb: e.tensor_copy(out=uf_tok[:], in_=pb[:, 0:NTT * 8]), reads=[Bpb], writes=[B_uf])
                    S.op("dve", lambda e: e.tensor_copy(out=ub_tok[:], in_=uf_tok[:]), reads=[B_uf], writes=[B_uf])
                    pb, Bpb = bank()

                    def trs(e, pb=pb):
                        ins = None
                        for j in range(NS):
                            for k, srcg in enumerate((GB, GC)):
                                ins = e.matmul(pb[0:DS, j * 8 + 4 * k:j * 8 + 4 * k + 4],
                                               lhsT=srcg[:, T + j * DS:T + (j + 1) * DS], rhs=identf[0:4, 0:4],
                                               start=True, stop=True)
                        return ins
                    S.op("pe", trs, reads=[B_GB, B_GC, B_const], writes=[Bpb])
                    S.op("dve", lambda e, pb=pb: e.tensor_copy(out=ufs[:], in_=pb[0:DS, 0:NS * 8]), reads=[Bpb], writes=[B_uf])
                    S.op("dve", lambda e: e.tensor_copy(out=ubs[:], in_=ufs[:]), reads=[B_uf], writes=[B_uf])
                    adg = sbt(p0, "adg", [4, 4 * NCH])
                    ones4 = sbt(p0, "ones4", [4, 128])
                    B_adg = S.buf("adg")
                    S.op("dve", lambda e: e.memset(ones4[:], 1.0), writes=[B_adg])
                    S.op("dve", lambda e: e.tensor_tensor(out=adg[:].rearrange("p (a c) -> p a c", a=4),
                                                          in0=acol.unsqueeze(1).to_broadcast([4, 4, NCH]),
                                                          in1=dmk[:].rearrange("p (a c) -> p a c", a=4), op=ALU.mult),
                         reads=[B_gsm, B_dmk, B_adg], writes=[B_adg])
                    pb, Bpb = bank()
                    S.op("pe", lambda e, pb=pb: e.matmul(pb[:, 0:4 * NCH], lhsT=ones4[:, :], rhs=adg[:, :], start=True, stop=True),
                         reads=[B_adg], writes=[Bpb])
                    S.op("dve", lambda e, pb=pb: e.tensor_copy(out=a_bc[:], in_=pb[:, 0:4 * NCH]), reads=[Bpb], writes=[B_abc])
                    if "uf" in dbg:
                        dump("uf", uf_tok[:], [128, NTT * 8], B_uf)
                        dump("abc", a_bc[:], [128, 4 * NCH], B_abc)
                        dump("ufs", ufs[:], [8, NS * 8], B_uf)
                    S.barrier()

                with contextlib.ExitStack() as p1:
                    W5 = [sbt(p1, "W5_%d" % i, [128, 8, 5, 512], BF16) for i in range(2)]
                    B_W = [[S.buf("W5_%d_%d" % (i, j)) for j in range(5)] for i in range(2)]
                    qT = [sbt(p1, "qT%d" % i, [128, 4, 512], BF16) for i in range(2)]
                    kT = [sbt(p1, "kT%d" % i, [128, 4, 512], BF16) for i in range(2)]
                    B_qT = [S.buf() for _ in range(2)]
                    B_kT = [S.buf() for _ in range(2)]
                    Cst = sbt(p1, "Cst", [128, 4, 512])
                    Css = sbt(p1, "Css", [128, 4, 512])
                    B_C = [S.buf() for _ in range(4)]
                    B_Cs = [S.buf() for _ in range(4)]
                    nst = sbt(p1, "nst", [128, 4])
                    nss = sbt(p1, "nss", [128, 4])
                    B_n = S.buf("n")
                    B_ns = S.buf("ns")
                    Cbf = sbt(p1, "Cbf", [128, 4, 512], BF16)
                    nbf = [sbt(p1, "nbf%d" % i, [128, 4], BF16) for i in range(2)]
                    B_Cbf = S.buf("Cbf")
                    B_nbf = [S.buf() for _ in range(2)]
                    vaug = [sbt(p1, "vaug%d" % i, [128, 512], BF16) for i in range(2)]
                    so = [sbt(p1, "so%d" % i, [128, 512], BF16) for i in range(2)]
                    sz = [sbt(p1, "sz%d" % i, [128, 512], BF16) for i in range(2)]
                    G1 = [sbt(p1, "G1%d" % i, [128, 512], BF16) for i in range(2)]
                    G = [sbt(p1, "G%d" % i, [128, 512], BF16) for i in range(2)]
                    ktok = [sbt(p1, "ktok%d" % i, [128, 512], BF16) for i in range(2)]
                    SpT = [sbt(p1, "SpT%d" % i, [128, 128], BF16) for i in range(2)]
                    hfin = [sbt(p1, "hfin%d" % i, [128, 512], BF16) for i in range(2)]
                    sml = [sbt(p1, "sml%d" % i, [128, 8]) for i in range(2)]
                    junk1 = sbt(p1, "junk1", [128, 512], BF16)
                    B_junk1 = S.buf()
                    B_vaug = [S.buf() for _ in range(2)]
                    B_so = [S.buf() for _ in range(2)]
                    B_sz = [S.buf() for _ in range(2)]
                    B_G1 = [S.buf() for _ in range(2)]
                    B_G = [S.buf() for _ in range(2)]
                    B_ktok = [S.buf() for _ in range(2)]
                    B_SpT = [S.buf() for _ in range(2)]
                    B_hfin = [S.buf("hfin%d" % i) for i in range(2)]
                    B_sml = [[S.buf() for _ in range(8)] for _ in range(2)]
                    cnt = [0]

                    def chunk(h, L, xcols, qt, kt, Bq, Bk, qcols, ucol, flcol, ubcol, acolp, Ct, BCs, nt, Bn, hf_rows, W5c, B_Wc):
                        s = cnt[0] % 2
                        cnt[0] += 1
                        pb, Bpb = bank()
                        pbb = pb[:].bitcast(BF16)

                        def ktr(e, pbb=pbb):
                            ins = None
                            for dc in range(4):
                                ins = e.transpose(out=pbb[:L, dc * 128:(dc + 1) * 128], in_=kt[:, dc, qcols],
                                                  identity=identb[:, :])
                            return ins
                        S.op("pe", ktr, reads=[Bk, B_identb], writes=[Bpb])
                        S.op("act", lambda e, pbb=pbb: e.copy(out=ktok[s][:L], in_=pbb[:L, 0:512]), reads=[Bpb], writes=[B_ktok[s]])

                        def smm(e):
                            ins = None
                            for dc in range(4):
                                ins = e.matmul(psm[:L, 0:L], lhsT=kt[:, dc, qcols], rhs=qt[:, dc, qcols],
                                               start=(dc == 0), stop=(dc == 3))
                            return ins
                        S.op("pe", smm, reads=[Bq, Bk], writes=[B_psS])
                        S.op("dve", lambda e: e.tensor_tensor(out=SpT[s][:L, :L], in0=psm[:L, 0:L], in1=maskT[:L, :L], op=ALU.mult),
                             reads=[B_psS, B_maskT], writes=[B_SpT[s]])
                        S.op("act", lambda e: e.activation(out=Cbf[:].rearrange("p a b -> p (a b)"),
                                                           in_=Ct[:].rearrange("p a b -> p (a b)"), func=AF.Copy, scale=acolp),
                             reads=list(BCs) + [B_abc], writes=[B_Cbf])
                        S.op("act", lambda e: e.activation(out=nbf[s][:], in_=nt[:], func=AF.Copy, scale=acolp),
                             reads=[Bn, B_abc], writes=[B_nbf[s]])
                        pv = []
                        for j in (2, 3, 4):
                            pb, Bpb = bank()

                            def pj(e, j=j, pb=pb):
                                ins = None
                                for kc in range(8):
                                    ins = e.matmul(pb[:L, :], lhsT=xnT[:, kc, xcols], rhs=W5c[:, kc, j, :],
                                                   start=(kc == 0), stop=(kc == 7))
                                return ins
                            S.op("pe", pj, reads=[B_xnT, B_Wc[j]], writes=[Bpb])
                            pv.append((pb, Bpb))
                            if j == 2:
                                pvv, Bpv = pb, Bpb
                                S.op("dve", lambda e: e.tensor_scalar(out=vaug[s][:L], in0=pvv[:L, :], scalar1=ucol, scalar2=None,
                                                                      op0=ALU.mult), reads=[Bpv, B_uf], writes=[B_vaug[s]])
                            elif j == 3:
                                po, Bpo = pb, Bpb
                                S.op("act", lambda e: e.activation(out=so[s][:L], in_=po[:L, :], func=AF.Sigmoid),
                                     reads=[Bpo], writes=[B_so[s]])
                            else:
                                pz, Bpz = pb, Bpb
                                S.op("act", lambda e: e.activation(out=sz[s][:L], in_=pz[:L, :], func=AF.Sigmoid),
                                     reads=[Bpz], writes=[B_sz[s]])
                                S.op("dve", lambda e: e.tensor_tensor(out=G1[s][:L], in0=pz[:L, :], in1=sz[s][:L], op=ALU.mult),
                                     reads=[Bpz, B_sz[s]], writes=[B_G1[s]])
                                S.op("pool", lambda e: e.tensor_tensor(out=G[s][:L], in0=G1[s][:L], in1=so[s][:L], op=ALU.mult),
                                     reads=[B_G1[s], B_so[s]], writes=[B_G[s]])
                        for dc in range(4):
                            pC, BpC = bank()
                            S.op("pe", lambda e, dc=dc, pC=pC: e.matmul(pC[:, :], lhsT=ktok[s][:L, dc * 128:(dc + 1) * 128],
                                                                        rhs=vaug[s][:L], start=True, stop=True),
                                 reads=[B_ktok[s], B_vaug[s]], writes=[BpC])
                            S.op("dve", lambda e, dc=dc, pC=pC: e.scalar_tensor_tensor(
                                out=Ct[:, dc, :], in0=Ct[:, dc, :], scalar=acolp, in1=pC[:, :], op0=ALU.mult, op1=ALU.add),
                                reads=[BCs[dc], BpC, B_abc], writes=[BCs[dc]])
                        pN, BpN = bank()

                        def nmm(e, pN=pN):
                            e.matmul(pN[:L, :], lhsT=SpT[s][:L, :L], rhs=vaug[s][:L], start=True, stop=False)
                            ins = None
                            for dc in range(4):
                                ins = e.matmul(pN[:L, :], lhsT=qt[:, dc, qcols], rhs=Cbf[:, dc, :],
                                               start=False, stop=(dc == 3))
                            return ins
                        S.op("pe", nmm, reads=[B_SpT[s], B_vaug[s], Bq, B_Cbf], writes=[BpN])

                        def dmm(e):
                            e.matmul(psm[:L, 128:129], lhsT=SpT[s][:L, :L], rhs=ubcol, start=True, stop=False)
                            ins = None
                            for dc in range(4):
                                ins = e.matmul(psm[:L, 128:129], lhsT=qt[:, dc, qcols], rhs=nbf[s][:, dc:dc + 1],
                                               start=False, stop=(dc == 3))
                            for dc in range(4):
                                ins = e.matmul(psm[:, 132 + dc:133 + dc], lhsT=ktok[s][:L, dc * 128:(dc + 1) * 128],
                                               rhs=ubcol, start=True, stop=True)
                            return ins
                        S.op("pe", dmm, reads=[B_SpT[s], B_uf, Bq, B_nbf[s], B_ktok[s]], writes=[B_psD])
                        S.op("dve", lambda e: e.scalar_tensor_tensor(out=nt[:], in0=nt[:], scalar=acolp, in1=psm[:, 132:136],
                                                                     op0=ALU.mult, op1=ALU.add),
                             reads=[Bn, B_psn, B_abc], writes=[Bn])
                        bs = B_sml[s]
                        sm_ = sml[s]
                        S.op("act", lambda e, pN=pN: e.activation(out=junk1[:L], in_=pN[:L, :], func=AF.Square,
                                                                   accum_out=sm_[:L, 0:1]),
                             reads=[BpN], writes=[B_junk1, bs[0]])
                        S.op("dve", lambda e: e.tensor_scalar(out=sm_[:L, 1:2], in0=psm[:L, 128:129], scalar1=-1.0, scalar2=flcol,
                                                              op0=ALU.mult, op1=ALU.max),
                             reads=[B_psD, B_uf], writes=[bs[1]])
                        S.op("dve", lambda e: e.tensor_tensor(out=sm_[:L, 2:3], in0=psm[:L, 128:129], in1=sm_[:L, 1:2], op=ALU.max),
                             reads=[B_psD, bs[1]], writes=[bs[2]])
                        S.op("dve", lambda e: e.scalar_tensor_tensor(out=sm_[:L, 3:4], in0=sm_[:L, 2:3], scalar=EPS,
                                                                     in1=sm_[:L, 2:3], op0=ALU.mult, op1=ALU.mult),
                             reads=[bs[2]], writes=[bs[3]])
                        S.op("dve", lambda e: e.scalar_tensor_tensor(out=sm_[:L, 4:5], in0=sm_[:L, 0:1], scalar=1.0 / DH,
                                                                     in1=sm_[:L, 3:4], op0=ALU.mult, op1=ALU.add),
                             reads=[bs[0], bs[3]], writes=[bs[4]])
                        S.op("pool", lambda e: e.tensor_tensor(out=sm_[:L, 5:6], in0=sm_[:L, 4:5], in1=cm05[:L], op=ALU.pow),
                             reads=[bs[4], B_cm05], writes=[bs[5]])
                        S.op("dve", lambda e, pN=pN: e.scalar_tensor_tensor(out=hfin[s][:L], in0=pN[:L, :], scalar=sm_[:L, 5:6],
                                                                            in1=G[s][:L], op0=ALU.mult, op1=ALU.mult),
                             reads=[BpN, bs[5], B_G[s]], writes=[B_hfin[s]])
                        S.dma("sp", [(hf[hf_rows, h * 512:(h + 1) * 512], hfin[s][:L])], B_hfin[s], reads=[B_hfin[s]])

                    def qkproj(h, slot, col, w, W5c, B_Wc, dsts=None):
                        if dsts is None:
                            dsts = ((0, qT[slot], B_qT[slot]), (1, kT[slot], B_kT[slot]))
                        for which, dst, Bd in dsts:
                            for dc in range(4):
                                pb, Bpb = bank()

                                def pm(e, which=which, dc=dc, pb=pb):
                                    ins = None
                                    for kc in range(8):
                                        ins = e.matmul(pb[:, 0:w], lhsT=W5c[:, kc, which, dc * 128:(dc + 1) * 128],
                                                       rhs=xnT[:, kc, col:col + w], start=(kc == 0), stop=(kc == 7))
                                    return ins
                                S.op("pe", pm, reads=[B_Wc[which], B_xnT], writes=[Bpb])
                                if which == 0:
                                    S.op("act", lambda e, dc=dc, pb=pb, dst=dst: e.copy(out=dst[:, dc, 0:w], in_=pb[:, 0:w]),
                                         reads=[Bpb], writes=[Bd])
                                else:
                                    S.op("act", lambda e, dc=dc, pb=pb, dst=dst: e.activation(
                                        out=dst[:, dc, 0:w], in_=pb[:, 0:w], func=AF.Copy, scale=float(DH) ** -0.5),
                                        reads=[Bpb], writes=[Bd])

                    gcount = 0

                    def load_w5(h):
                        sl = h % 2
                        for j in range(5):
                            c0 = j * DI + h * DH
                            S.dma("pool", [(W5[sl][:, :, j, :], w_in_a[:, c0:c0 + DH].rearrange("(k p) c -> p k c", p=128))],
                                  B_W[sl][j], writes=[B_W[sl][j]])
                    load_w5(0)
                    NG = (T + 511) // 512
                    qTs = sbt(p1, "qTs", [128, 4, NST], BF16)
                    kTs = sbt(p1, "kTs", [128, 4, NST], BF16)
                    B_qTs, B_kTs = S.buf("qTs"), S.buf("kTs")
                    spos = [(j + 1) * NC // (NS + 1) for j in range(NS)]

                    def sample_load(h, j):
                        S.dma("sp", [(Css[:], sC[j, h].rearrange("(dc p) e -> p dc e", p=128))], B_Cs[0], writes=B_Cs)
                        S.dma("sp", [(nss[:], sn[j, h].rearrange("(dc p) -> p dc", p=128))], B_ns, writes=[B_ns],
                              allow_slow_non_contiguous=True)

                    def sample_seq(h, j, W5c, B_Wc):
                        chunk(h, DS, slice(T + j * DS, T + (j + 1) * DS), qTs, kTs, B_qTs, B_kTs,
                              slice(j * DS, (j + 1) * DS),
                              ufs[:, j * 8 + h:j * 8 + h + 1], ufs[:, j * 8 + 4 + h:j * 8 + 5 + h],
                              ubs[:, j * 8 + h:j * 8 + h + 1], a_bc[:, h * NCH + NC + j:h * NCH + NC + j + 1],
                              Css, B_Cs, nss, B_ns, slice(T + j * DS, T + (j + 1) * DS), W5c, B_Wc)
                        S.dma("sp", [(C_s[j, h].rearrange("(dc p) e -> p dc e", p=128), Css[:])], B_Cs[0], reads=B_Cs)
                        S.dma("sp", [(n_s[j, h].rearrange("(dc p) -> p dc", p=128), nss[:])], B_ns, reads=[B_ns],
                              allow_slow_non_contiguous=True)

                    for h in range(H):
                        W5c, B_Wc = W5[h % 2], B_W[h % 2]
                        if h + 1 < H:
                            load_w5(h + 1)
                        for dc in range(4):
                            S.op("pool", lambda e, dc=dc: e.memset(Cst[:, dc, :], 0.0), writes=[B_C[dc]])
                        S.op("pool", lambda e: e.memset(nst[:], 0.0), writes=[B_n])
                        slot = gcount % 2
                        gcount += 1
                        qkproj(h, slot, 0, min(512, T), W5c, B_Wc)
                        qkproj(h, None, T, NST, W5c, B_Wc, dsts=((0, qTs, B_qTs), (1, kTs, B_kTs)))
                        sample_load(h, 0)
                        for grp in range(NG):
                            col = grp * 512
                            w = min(512, T - col)
                            nch = w // 128
                            nslot = slot
                            for cc in range(nch):
                                c = grp * 4 + cc
                                if cc == nch - 1 and grp + 1 < NG:
                                    nslot = gcount % 2
                                    gcount += 1
                                    qkproj(h, nslot, col + 512, min(512, T - col - 512), W5c, B_Wc)
                                if zero_fill:
                                    zero_fill.pop()()
                                chunk(h, 128, slice(c * 128, (c + 1) * 128), qT[slot], kT[slot], B_qT[slot], B_kT[slot],
                                      slice(cc * 128, (cc + 1) * 128),
                                      uf_tok[:, c * 8 + h:c * 8 + h + 1], uf_tok[:, c * 8 + 4 + h:c * 8 + 5 + h],
                                      ub_tok[:, c * 8 + h:c * 8 + h + 1], a_bc[:, h * NCH + c:h * NCH + c + 1],
                                      Cst, B_C, nst, B_n, slice(c * 128, (c + 1) * 128), W5c, B_Wc)
                                for j in range(NS):
                                    if spos[j] == c:
                                        sample_seq(h, j, W5c, B_Wc)
                                        if j + 1 < NS:
                                            sample_load(h, j + 1)
                            slot = nslot
                        S.dma("sp", [(C_p[h].rearrange("(dc p) e -> p dc e", p=128), Cst[:])], B_C[0], reads=B_C)
                        S.dma("sp", [(n_p[h].rearrange("(dc p) -> p dc", p=128), nst[:])], B_n, reads=[B_n],
                              allow_slow_non_contiguous=True)
                    while zero_fill:
                        zero_fill.pop()()
                    S.barrier()
        if dbg and "hf" in dbg:
            o = dout("dbg_hf", [TT, DI], BF16)
            dbg_out["hf"] = o
            Bd = S.buf("dbghf")
            S.dma("sp", [(o[:, :], hf[:, :])], Bd)
            S.barrier()


        gk_bc = sbt(top, "gk_bc", [128, HD])
        gq_bc = sbt(top, "gq_bc", [128, HD])
        B_gqk = S.buf("gqk")
        S.dma("sp", [(gk_bc[:], k_norm[0:1, :].to_broadcast([128, HD])),
                     (gq_bc[:], q_norm[0:1, :].to_broadcast([128, HD]))], B_gqk, writes=[B_gqk])

        gcolK = sbt(top, "gcolK", [128, 1])
        gcolQ = sbt(top, "gcolQ", [128, 1])
        S.dma("sp", [(gcolK[0:64, :], k_norm[0:1, :].rearrange("o d -> d o")), (gcolK[64:128, :], k_norm[0:1, :].rearrange("o d -> d o")),
                     (gcolQ[0:64, :], q_norm[0:1, :].rearrange("o d -> d o")), (gcolQ[64:128, :], q_norm[0:1, :].rearrange("o d -> d o"))],
              B_gqk, writes=[B_gqk])
        def tile_rows(i):
            return (128 if i < NT else NST), i * 128

        if "B" in phases:
            with contextlib.ExitStack() as pb_:
                x1nT = sbt(pb_, "x1nT", [128, 8, TT], BF16)
                B_x1nT = S.buf("x1nT")
                gkv = sbt(pb_, "gkv", [128, 8])
                gnb = sbt(pb_, "gnb", [128, 8])
                B_gn = S.buf("gn")
                S.dma("sp", [(gkv[:], norm_kv.rearrange("(k p) -> p k", p=128)),
                             (gnb[:], norm_b.rearrange("(k p) -> p k", p=128))], B_gn, writes=[B_gn],
                      allow_slow_non_contiguous=True)
                Wt0 = sbt(pb_, "WtB0", [128, 8, QW], BF16)
                stgB = [sbt(pb_, "stgB%d" % i, [128, QW]) for i in range(2)]
                B_stgB = [S.buf() for _ in range(2)]
                B_Wt0 = S.buf("WtB0")

                def prefetch_k_weights():
                    for kc in range(8):
                        s2_ = kc % 2
                        S.dma("sp", [(stgB[s2_][:, 0:QW], w_kv[kc * 128:(kc + 1) * 128, 0:QW])], B_stgB[s2_], writes=[B_stgB[s2_]])
                        S.op("act", lambda e, kc=kc, s2_=s2_: e.activation(out=Wt0[:, kc, 0:QW], in_=stgB[s2_][:, 0:QW],
                                                                           func=AF.Copy, scale=gkv[:, kc:kc + 1]),
                             reads=[B_stgB[s2_], B_gn], writes=[B_Wt0])
                with contextlib.ExitStack() as p1:
                    woA = sbt(p1, "woA", [128, 16, D], BF16)
                    B_woA = S.buf("woA")
                    stg = [sbt(p1, "stgA%d" % i, [128, D]) for i in range(2)]
                    B_stg = [S.buf() for _ in range(2)]
                    hng = sbt(p1, "hng", [128, 16])
                    B_hng = S.buf("hng")
                    S.dma("sp", [(hng[:], hnorm.rearrange("(k p) -> p k", p=128))], B_hng, writes=[B_hng],
                          allow_slow_non_contiguous=True)
                    for ec in range(16):
                        s2 = ec % 2
                        S.dma("sp", [(stg[s2][:], w_out_a[ec * 128:(ec + 1) * 128, :])], B_stg[s2], writes=[B_stg[s2]])
                        S.op("act", lambda e, ec=ec, s2=s2: e.activation(out=woA[:, ec, :], in_=stg[s2][:], func=AF.Copy,
                                                                         scale=hng[:, ec:ec + 1]),
                             reads=[B_stg[s2], B_hng], writes=[B_woA])
                    hft = [sbt(p1, "hft%d" % i, [128, DI], BF16) for i in range(3)]
                    hfT = [sbt(p1, "hfT%d" % i, [128, 16, 128], BF16) for i in range(2)]
                    xt = [sbt(p1, "xtB%d" % i, [128, D]) for i in range(4)]
                    x1 = [sbt(p1, "x1B%d" % i, [128, D]) for i in range(3)]
                    x1n = [sbt(p1, "x1n%d" % i, [128, D], BF16) for i in range(2)]
                    junkb = sbt(p1, "junkB", [128, D], BF16)
                    smb = sbt(p1, "smB", [128, 9])
                    B_hft = [S.buf("hft%d" % i) for i in range(3)]
                    B_hfT = [[S.buf(), S.buf()] for _ in range(2)]
                    B_xtb = [S.buf("xtB%d" % i) for i in range(4)]
                    B_x1 = [S.buf("x1B%d" % i) for i in range(3)]
                    B_x1n = [S.buf() for _ in range(2)]
                    B_junkb = S.buf()
                    B_smb = [[S.buf() for _ in range(3)] for _ in range(3)]

                    def run_skewed1(stages, n):
                        ns = len(stages)
                        for step in range(n + ns - 1):
                            for st in range(ns - 1, -1, -1):
                                i = step - st
                                if 0 <= i < n:
                                    stages[st](i)

                    def b1_load(i):
                        rows, c0 = tile_rows(i)
                        S.dma("sp", [(hft[i % 3][:rows], hf[c0:c0 + rows, :])], B_hft[i % 3], writes=[B_hft[i % 3]])
                        src = xp[c0:c0 + rows, :] if i < NT else xs[:, :]
                        S.dma("sp", [(xt[i % 4][:rows], src)], B_xtb[i % 4], writes=[B_xtb[i % 4]])

                    def b1_s0(i):
                        rows, c0 = tile_rows(i)
                        s2, s3 = i % 2, i % 3
                        for hb in range(2):
                            pb, Bpb = bank()
                            pbb = pb[:].bitcast(BF16)

                            def trh(e, hb=hb, pbb=pbb):
                                ins = None
                                for j in range(8):
                                    ec = hb * 8 + j
                                    ins = e.transpose(out=pbb[:, j * 128:j * 128 + rows],
                                                      in_=hft[s3][:rows, ec * 128:(ec + 1) * 128], identity=identb[:rows, :rows])
                                return ins
                            S.op("pe", trh, reads=[B_hft[s3], B_identb], writes=[Bpb])
                            S.op("act" if hb == 0 else "dve", lambda e, hb=hb, pbb=pbb: (e.copy if hb == 0 else e.tensor_copy)(
                                out=hfT[s2][:, hb * 8:(hb + 1) * 8, :rows],
                                in_=pbb.rearrange("p (k t) -> p k t", t=128)[:, :, :rows]),
                                reads=[Bpb], writes=[B_hfT[s2][hb]])

                    def b1_s1(i):
                        rows, c0 = tile_rows(i)
                        s2, s3, s4 = i % 2, i % 3, i % 4
                        for half in range(2):
                            pb, Bpb = bank()

                            def ym(e, half=half, pb=pb):
                                ins = None
                                for ec in range(16):
                                    ins = e.matmul(pb[:rows, :], lhsT=hfT[s2][:, ec, :rows],
                                                   rhs=woA[:, ec, half * 512:(half + 1) * 512], start=(ec == 0), stop=(ec == 15))
                                return ins
                            S.op("pe", ym, reads=[B_hfT[s2][0], B_hfT[s2][1], B_woA], writes=[Bpb])
                            S.op("dve", lambda e, half=half, pb=pb: e.tensor_tensor(
                                out=x1[s3][:rows, half * 512:(half + 1) * 512], in0=pb[:rows, :],
                                in1=xt[s4][:rows, half * 512:(half + 1) * 512], op=ALU.add),
                                reads=[Bpb, B_xtb[s4]], writes=[B_x1[s3]])
                        S.dma("pool", [(x1s[c0:c0 + rows, :], x1[s3][:rows])], B_x1[s3], reads=[B_x1[s3]])
                        S.op("act", lambda e: e.activation(out=junkb[:rows], in_=x1[s3][:rows], func=AF.Square,
                                                           accum_out=smb[:rows, s3 * 3:s3 * 3 + 1]),
                             reads=[B_x1[s3]], writes=[B_junkb, B_smb[s3][0]])

                    def b1_s2(i):
                        rows, c0 = tile_rows(i)
                        s2, s3 = i % 2, i % 3
                        rsqrt_small(rows, smb[:rows, s3 * 3 + 2:s3 * 3 + 3], smb[:rows, s3 * 3:s3 * 3 + 1], 1.0 / D, EPS,
                                    B_smb[s3][0], B_smb[s3][2], smb[:rows, s3 * 3 + 1:s3 * 3 + 2], B_smb[s3][1])
                        S.op("act", lambda e: e.activation(out=x1n[s2][:rows], in_=x1[s3][:rows], func=AF.Copy,
                                                           scale=smb[:rows, s3 * 3 + 2:s3 * 3 + 3]),
                             reads=[B_x1[s3], B_smb[s3][2]], writes=[B_x1n[s2]])

                    def b1_s3(i):
                        rows, c0 = tile_rows(i)
                        s2 = i % 2
                        pb, Bpb = bank()
                        pbb = pb[:].bitcast(BF16)

                        def trx(e, pbb=pbb):
                            ins = None
                            for kc in range(8):
                                ins = e.transpose(out=pbb[:, kc * 128:kc * 128 + rows],
                                                  in_=x1n[s2][:rows, kc * 128:(kc + 1) * 128], identity=identb[:rows, :rows])
                            return ins
                        S.op("pe", trx, reads=[B_x1n[s2], B_identb], writes=[Bpb])
                        S.op("dve", lambda e, pbb=pbb: e.tensor_copy(out=x1nT[:, :, c0:c0 + rows],
                                                                     in_=pbb.rearrange("p (k t) -> p k t", t=128)[:, :, :rows]),
                             reads=[Bpb], writes=[B_x1nT])
                    pk = [prefetch_k_weights]

                    def b1_load_pk(i):
                        b1_load(i)
                        if pk and i >= min(6, NTT - 1):
                            pk.pop()()
                    run_skewed1([b1_load_pk, b1_s0, b1_s1, b1_s2, b1_s3], NTT)
                    S.barrier()

                with contextlib.ExitStack() as p2:
                    Wts = [Wt0, sbt(p2, "WtB1", [128, 8, QW], BF16)]
                    B_Wts = [B_Wt0, S.buf("WtB1")]
                    stg = stgB
                    B_stg = B_stgB
                    NSL = 4
                    raw = [sbt(p2, "rawB%d" % i, [128, QW]) for i in range(NSL)]
                    B_raw = [S.buf() for _ in range(NSL)]
                    sqt = [sbt(p2, "sqtB%d" % i, [128, QW]) for i in range(2)]
                    B_sqt = [S.buf() for _ in range(2)]
                    nb16 = [sbt(p2, "nb16_%d" % i, [128, QW], BF16) for i in range(NSL)]
                    B_nb16 = [S.buf() for _ in range(NSL)]
                    TTt = [sbt(p2, "TTt%d" % i, [128, 12, 128], BF16) for i in range(2)]
                    B_TTt = [S.buf("TTt%d" % i) for i in range(2)]
                    vb = [sbt(p2, "vb%d" % i, [128, NQH, HD + 1], BF16) for i in range(2)]
                    B_vb = [S.buf("vb%d" % i) for i in range(2)]
                    sm2 = sbt(p2, "sm2", [128, 2 * NSL * NQH])
                    B_sm2 = [[S.buf() for _ in range(2)] for _ in range(NSL)]
                    zsb = [sbt(p2, "zsb%d" % i, [128, 512], BF16) for i in range(2)]
                    sgz = [sbt(p2, "sgz%d" % i, [128, 512]) for i in range(2)]
                    B_zsb = [S.buf("zsb%d" % i) for i in range(2)]
                    B_sgz = [S.buf() for _ in range(2)]
                    for i in range(2):
                        S.op("pool", lambda e, i=i: e.memset(vb[i][:, :, HD:HD + 1], 1.0), writes=[B_vb[i]])

                    def run_skewed(stages, n):
                        ns = len(stages)
                        for step in range(n + ns - 1):
                            for st in range(ns - 1, -1, -1):
                                i = step - st
                                if 0 <= i < n:
                                    stages[st](i)

                    def load_w(wsl, wsrc, c0, ncols, gain):
                        Wt, B_Wt = Wts[wsl], B_Wts[wsl]
                        for kc in range(8):
                            s2 = kc % 2
                            S.dma("sp", [(stg[s2][:, 0:ncols], wsrc[kc * 128:(kc + 1) * 128, c0:c0 + ncols])], B_stg[s2],
                                  writes=[B_stg[s2]])
                            S.op("act", lambda e, kc=kc, s2=s2: e.activation(out=Wt[:, kc, 0:ncols], in_=stg[s2][:, 0:ncols],
                                                                             func=AF.Copy, scale=gain[:, kc:kc + 1]),
                                 reads=[B_stg[s2], B_gn], writes=[B_Wt])

                    def proj(i, nblk, evac, wsl):
                        Wt, B_Wt = Wts[wsl], B_Wts[wsl]
                        rows, c0 = tile_rows(i)
                        for nb in range(nblk):
                            pb, Bpb = bank()

                            def pm(e, nb=nb, pb=pb):
                                ins = None
                                for kc in range(8):
                                    ins = e.matmul(pb[:rows, :], lhsT=x1nT[:, kc, c0:c0 + rows],
                                                   rhs=Wt[:, kc, nb * 512:(nb + 1) * 512], start=(kc == 0), stop=(kc == 7))
                                return ins
                            S.op("pe", pm, reads=[B_x1nT, B_Wt], writes=[Bpb])
                            evac(nb, pb, Bpb)

                    def kv_out(i, which, src_tile, Bsrc):
                        rows, c0 = tile_rows(i)
                        pairs = []
                        for g, (win, dil) in enumerate(GROUPS):
                            if i < NT:
                                nrow = min(win, T)
                                r0 = c0 - (T - nrow)
                                if r0 < 0:
                                    continue
                                dst = kvp[g][r0:r0 + rows, which, :, :]
                            else:
                                dst = kvs[g][:, which, :, :]
                            pairs.append((dst, src_tile[:rows, g * 512:(g + 1) * 512].rearrange("p (h d) -> p h d", d=HD)))
                        if pairs:
                            S.dma("sp", pairs, Bsrc, reads=[Bsrc])

                    def qk_phase(wsl, g_bc, dstT, is_k, after_load=None):

                        def st0(i):
                            rows, c0 = tile_rows(i)
                            s4, s2 = i % NSL, i % 2

                            def ev(nb, pb, Bpb):
                                S.op("act", lambda e: e.copy(out=raw[s4][:rows, nb * 512:(nb + 1) * 512], in_=pb[:rows, :]),
                                     reads=[Bpb], writes=[B_raw[s4]])
                                S.op("act", lambda e: e.activation(out=sqt[s2][:rows, nb * 512:(nb + 1) * 512], in_=pb[:rows, :],
                                                                   func=AF.Square), reads=[Bpb], writes=[B_sqt[s2]])
                            proj(i, 3, ev, wsl)
                            if i == 0 and after_load is not None:
                                after_load()

                        def st1(i):
                            rows, c0 = tile_rows(i)
                            s4, s2 = i % NSL, i % 2
                            ssq = sm2[:rows, s4 * 2 * NQH:s4 * 2 * NQH + NQH]
                            rst = sm2[:rows, s4 * 2 * NQH + NQH:(s4 + 1) * 2 * NQH]
                            S.op("dve", lambda e: e.tensor_reduce(out=ssq, in_=sqt[s2][:rows].rearrange("p (h d) -> p h d", d=HD),
                                                                  axis=AX.X, op=ALU.add),
                                 reads=[B_sqt[s2]], writes=[B_sm2[s4][0]])
                            S.op("dve", lambda e: e.tensor_scalar(out=ssq, in0=ssq, scalar1=1.0 / HD, scalar2=EPS,
                                                                  op0=ALU.mult, op1=ALU.add),
                                 reads=[B_sm2[s4][0]], writes=[B_sm2[s4][0]])
                            S.op("act", lambda e: e.activation(out=ssq, in_=ssq, func=AF.Sqrt),
                                 reads=[B_sm2[s4][0]], writes=[B_sm2[s4][0]])
                            S.op("dve", lambda e: e.reciprocal(out=rst, in_=ssq),
                                 reads=[B_sm2[s4][0]], writes=[B_sm2[s4][1]])

                        def st2(i):
                            rows, c0 = tile_rows(i)
                            s4 = i % NSL
                            rst = sm2[:rows, s4 * 2 * NQH + NQH:(s4 + 1) * 2 * NQH]
                            if is_k:
                                S.op("dve", lambda e: e.tensor_tensor(
                                    out=raw[s4][:rows].rearrange("p (h d) -> p h d", d=HD),
                                    in0=raw[s4][:rows].rearrange("p (h d) -> p h d", d=HD),
                                    in1=rst.unsqueeze(2).to_broadcast([rows, NQH, HD]), op=ALU.mult),
                                    reads=[B_raw[s4], B_sm2[s4][1]], writes=[B_raw[s4]])
                                S.op("act", lambda e: e.copy(out=nb16[s4][:rows], in_=raw[s4][:rows]),
                                     reads=[B_raw[s4]], writes=[B_nb16[s4]])
                                need = []
                                for g, (win, dil) in enumerate(GROUPS):
                                    if i >= NT or c0 - (T - min(win, T)) >= 0:
                                        need.append(g)
                                for g in need:
                                    S.op("dve", lambda e, g=g: e.tensor_tensor(
                                        out=raw[s4][:rows, g * 512:(g + 1) * 512].rearrange("p (h d) -> p h d", d=HD),
                                        in0=raw[s4][:rows, g * 512:(g + 1) * 512].rearrange("p (h d) -> p h d", d=HD),
                                        in1=g_bc[:rows].unsqueeze(1).to_broadcast([rows, 8, HD]), op=ALU.mult),
                                        reads=[B_raw[s4], B_gqk], writes=[B_raw[s4]])
                                if need:
                                    kv_out(i, 0, raw[s4], B_raw[s4])
                            else:
                                S.op("dve", lambda e: e.tensor_tensor(
                                    out=nb16[s4][:rows].rearrange("p (h d) -> p h d", d=HD),
                                    in0=raw[s4][:rows].rearrange("p (h d) -> p h d", d=HD),
                                    in1=rst.unsqueeze(2).to_broadcast([rows, NQH, HD]), op=ALU.mult),
                                    reads=[B_raw[s4], B_sm2[s4][1]], writes=[B_nb16[s4]])

                        def st3(i):
                            rows, c0 = tile_rows(i)
                            s4, s2 = i % NSL, i % 2
                            for hb, (j0, j1) in enumerate(((0, 8), (8, 12))):
                                pb, Bpb = bank()
                                pbb = pb[:].bitcast(BF16)

                                def trq(e, j0=j0, j1=j1, pbb=pbb):
                                    ins = None
                                    for j in range(j0, j1):
                                        ins = e.transpose(out=pbb[:, (j - j0) * 128:(j - j0) * 128 + rows],
                                                          in_=nb16[s4][:rows, j * 128:(j + 1) * 128], identity=identb[:rows, :rows])
                                    return ins
                                S.op("pe", trq, reads=[B_nb16[s4], B_identb], writes=[Bpb])
                                gcol = gcolK if is_k else gcolQ
                                if hb == 0:
                                    S.op("dve", lambda e, j0=j0, j1=j1, pbb=pbb: e.tensor_scalar(
                                        out=TTt[s2][:, j0:j1, :rows],
                                        in0=pbb[:, 0:(j1 - j0) * 128].rearrange("p (k t) -> p k t", t=128)[:, :, :rows],
                                        scalar1=gcol[:, 0:1], scalar2=None, op0=ALU.mult),
                                        reads=[Bpb, B_gqk], writes=[B_TTt[s2]])
                                else:
                                    S.op("act", lambda e, j0=j0, j1=j1, pbb=pbb: e.activation(
                                        out=TTt[s2][:, j0:j1, :rows],
                                        in_=pbb[:, 0:(j1 - j0) * 128].rearrange("p (k t) -> p k t", t=128)[:, :, :rows],
                                        func=AF.Copy, scale=gcol[:, 0:1]),
                                        reads=[Bpb, B_gqk], writes=[B_TTt[s2]])
                            if is_k:
                                S.dma("sp", [(dstT.rearrange("(rg p) t -> p rg t", p=128)[:, :, c0:c0 + rows], TTt[s2][:, :, :rows])],
                                      B_TTt[s2], reads=[B_TTt[s2]])
                            else:
                                S.dma("sp", [(dstT.rearrange("(rg p) t -> p rg t", p=128)[:, :, c0:c0 + rows], TTt[s2][:, :, :rows]),
                                             (QTse.rearrange("(rg p) t -> p rg t", p=128)[0:64, :, c0:c0 + rows], TTt[s2][0:64, :, :rows]),
                                             (QTso.rearrange("(rg p) t -> p rg t", p=128)[64:128, :, c0:c0 + rows], TTt[s2][64:128, :, :rows])],
                                      B_TTt[s2], reads=[B_TTt[s2]])
                        run_skewed([st0, st1, st2, st3], NTT)

                    qk_phase(0, gk_bc, KTs, True, after_load=lambda: load_w(1, w_kv, QW, QW, gkv))

                    def v0(i):
                        rows, c0 = tile_rows(i)
                        s4 = i % NSL

                        def evv(nb, pb, Bpb):
                            S.op("act", lambda e: e.copy(out=raw[s4][:rows, nb * 512:(nb + 1) * 512], in_=pb[:rows, :]),
                                 reads=[Bpb], writes=[B_raw[s4]])
                        proj(i, 3, evv, 1)
                        if i == 0:
                            load_w(0, w_in_b, 0, QW, gnb)

                    def v1(i):
                        rows, c0 = tile_rows(i)
                        s4, s2 = i % NSL, i % 2
                        S.op("dve", lambda e: e.tensor_copy(out=vb[s2][:rows, :, 0:HD],
                                                            in_=raw[s4][:rows].rearrange("p (h d) -> p h d", d=HD)),
                             reads=[B_raw[s4]], writes=[B_vb[s2]])
                        S.dma("sp", [(Vs[c0:c0 + rows, :, :], vb[s2][:rows])], B_vb[s2], reads=[B_vb[s2]])
                        kv_out(i, 1, raw[s4], B_raw[s4])
                    run_skewed([v0, v1], NTT)
                    qk_phase(0, gq_bc, QTs, False, after_load=lambda: load_w(1, w_in_b, QW, 512, gnb))
                    zbank = {}

                    def z0(i):
                        rows, c0 = tile_rows(i)
                        s2 = i % 2

                        def evz(nb, pb, Bpb):
                            zbank[i] = (pb, Bpb)
                            S.op("act", lambda e: e.activation(out=sgz[s2][:rows], in_=pb[:rows, :], func=AF.Sigmoid),
                                 reads=[Bpb], writes=[B_sgz[s2]])
                        proj(i, 1, evz, 1)

                    def z1(i):
                        rows, c0 = tile_rows(i)
                        s2 = i % 2
                        pb, Bpb = zbank.pop(i)
                        S.op("dve", lambda e: e.tensor_tensor(out=zsb[s2][:rows], in0=pb[:rows, :], in1=sgz[s2][:rows], op=ALU.mult),
                             reads=[Bpb, B_sgz[s2]], writes=[B_zsb[s2]])
                        S.dma("sp", [(zss[c0:c0 + rows, :], zsb[s2][:rows])], B_zsb[s2], reads=[B_zsb[s2]])
                    run_skewed([z0, z1], NTT)
                    S.barrier()

        scr = {"x1s": (x1s, [TT, D], F32), "KTs": (KTs, [QW, TT], BF16), "QTs": (QTs, [QW, TT], BF16),
               "Vs": (Vs, [TT, NQH, HD + 1], BF16), "zss": (zss, [TT, 512], BF16), "osc": (osc, [3, TT, 8 * (HD + 1)], F32),
               "vecs": (vecs, [NQH, 384], F32)}
        for name in dbg:
            if name in scr:
                ap_, shp, dt_ = scr[name]
                o = dout("dbg_" + name, shp, dt_)
                dbg_out[name] = o
                Bd = S.buf("dbg" + name)
                S.dma("sp", [(o, ap_)], Bd)
        if dbg:
            S.barrier()


        OW = 8 * (HD + 1)
        ps_ = contextlib.ExitStack()
        ET = sbt(ps_, "ET", [NBUCK, NQH])
        EBs = sbt(ps_, "EBs", [128, 13, 8, 8])
        EBn = sbt(ps_, "EBn", [8, 3, 8, 8])
        if "C" in phases or "c" in phases:
            with contextlib.ExitStack() as pc:
                B_ET = S.buf("ET")
                S.dma("sp", [(ET[:], rel_bias[:, :])], B_ET, writes=[B_ET])
                S.op("act", lambda e: e.activation(out=ET[:], in_=ET[:], func=AF.Exp), reads=[B_ET], writes=[B_ET])
                Tb = sbt(pc, "Tb", [128, NQH, 256])
                B_Tb = S.buf("Tb")
                B_EBs = S.buf("EBs")
                B_EBn = S.buf("EBn")
                with contextlib.ExitStack() as pc0:
                    ohp = sbt(pc0, "ohp", [NBUCK, 3 * 384])
                    ohs = sbt(pc0, "ohs", [NBUCK, 13 * 8 * 128])
                    ohn = sbt(pc0, "ohn", [NBUCK, 3 * 8 * 8])
                    Jf = sbt(pc0, "Jf", [128, 128])
                    B_oh = S.buf("oh")
                    S.dma("sp", [(ohp[:], c_ohp[:, :]), (ohs[:], c_ohs[:, :]), (ohn[:], c_ohn[:, :]), (Jf[:], c_J[:, :])],
                          B_oh, writes=[B_oh])
                    vtmp = sbt(pc0, "vtmp", [8, 3 * 384])
                    B_vtmp = S.buf("vtmp")
                    for g in range(3):
                        pb, Bpb = bank()
                        S.op("pe", lambda e: e.matmul(pb[0:8, 0:384], lhsT=ET[:, g * 8:(g + 1) * 8], rhs=ohp[:, g * 384:(g + 1) * 384],
                                                      start=True, stop=True), reads=[B_ET, B_oh], writes=[Bpb])
                        S.op("dve", lambda e: e.tensor_copy(out=vtmp[:, g * 384:(g + 1) * 384], in_=pb[0:8, 0:384]),
                             reads=[Bpb], writes=[B_vtmp])
                    B_vecs = S.buf("vecs")
                    S.dma("sp", [(vecs[g * 8:(g + 1) * 8, :], vtmp[:, g * 384:(g + 1) * 384]) for g in range(3)], B_vtmp,
                          reads=[B_vtmp], writes=[B_vecs])
                    Hk = sbt(pc0, "Hk", [128, NQH, 256])
                    B_Hk = S.buf("Hk")
                    S.dma("sp", [(Hk[:], bass.AP(vecs.tensor, 0, [[1, 128], [384, NQH], [1, 256]]))], B_Hk,
                          reads=[B_vecs], writes=[B_Hk])
                    for gh2 in range(NQH // 2):
                        pb, Bpb = bank()
                        S.op("pe", lambda e: e.matmul(pb[:, :], lhsT=Jf[:, :],
                                                      rhs=Hk[:, 2 * gh2:2 * gh2 + 2, :].rearrange("p a b -> p (a b)"),
                                                      start=True, stop=True), reads=[B_oh, B_Hk], writes=[Bpb])
                        S.op("dve" if gh2 % 2 else "act", lambda e: (e.tensor_copy if gh2 % 2 else e.copy)(
                            out=Tb[:, 2 * gh2:2 * gh2 + 2, :].rearrange("p a b -> p (a b)"), in_=pb[:, :]),
                            reads=[Bpb], writes=[B_Tb])
                    S.op("pool", lambda e: e.memset(EBs[:].rearrange("p a b c -> p (a b c)"), 0.0), writes=[B_EBs])
                    for gr, (g, r) in enumerate(CLASSES):
                        pb, Bpb = bank()

                        def ebm(e):
                            ins = None
                            for s_ in range(8):
                                o0 = (gr * 8 + s_) * 128
                                ins = e.matmul(pb[:, s_ * 8:s_ * 8 + 8], lhsT=ohs[:, o0:o0 + 128], rhs=ET[:, g * 8:(g + 1) * 8],
                                               start=True, stop=True)
                            return ins
                        S.op("pe", ebm, reads=[B_oh, B_ET], writes=[Bpb])
                        S.op("dve", lambda e: e.tensor_copy(out=EBs[:, gr, :, :],
                                                            in_=pb[:, 0:64].rearrange("p (s h) -> p h s", h=8)),
                             reads=[Bpb], writes=[B_EBs])
                    for g in range(3):
                        pb, Bpb = bank()

                        def ebn(e):
                            ins = None
                            for s_ in range(8):
                                o0 = (g * 8 + s_) * 8
                                ins = e.matmul(pb[0:8, s_ * 8:s_ * 8 + 8], lhsT=ohn[:, o0:o0 + 8], rhs=ET[:, g * 8:(g + 1) * 8],
                                               start=True, stop=True)
                            return ins
                        S.op("pe", ebn, reads=[B_oh, B_ET], writes=[Bpb])
                        S.op("dve", lambda e: e.tensor_copy(out=EBn[:, g, :, :],
                                                            in_=pb[0:8, 0:64].rearrange("p (s h) -> p h s", h=8)),
                             reads=[Bpb], writes=[B_EBn])
                    if "Tb" in dbg:
                        dump("Tb", Tb[:], [128, NQH, 256], B_Tb)
                        dump("EBs", EBs[:], [128, 13, 8, 8], B_EBs)
                        dump("EBn", EBn[:], [8, 3, 8, 8], B_EBn)
                    S.barrier()

                QK = [[sbt(pc, "QK%d_%d" % (i, k), [128, 2, TT], BF16) for k in range(3)] for i in range(2)]
                B_QK = [S.buf("QK%d" % i) for i in range(2)]
                NV = 8
                Vt = [sbt(pc, "Vt%d" % i, [128, 4, HD + 1], BF16) for i in range(NV)]
                B_Vt = [S.buf("Vt%d" % i) for i in range(NV)]
                W4 = 4 * (HD + 1)
                ob = [sbt(pc, "ob%d" % i, [128, W4]) for i in range(2)]
                B_ob = [S.buf("ob%d" % i) for i in range(2)]
                if T % 2048 == 0 and "C" in phases:
                    Et2 = [sbt(pc, "Et2_%d" % i, [128, 4, 256], BF16) for i in range(3)]
                    Tbb = sbt(pc, "Tbb", [128, NQH, 256], BF16)
                    S.op("pool", lambda e: e.tensor_copy(out=Tbb[:].rearrange("p a b -> p (a b)"), in_=Tb[:].rearrange("p a b -> p (a b)")),
                         reads=[B_Tb], writes=[B_Tb])
                    Pt2 = [sbt(pc, "Pt2_%d" % i, [128, 4, 256], BF16) for i in range(3)]
                    B_Et2 = [S.buf() for _ in range(3)]
                    B_Pt2 = [S.buf() for _ in range(3)]
                    B_S2 = [pbanks[2][1], pbanks[4][1], pbanks[6][1]]
                    B_S2b = [pbanks[3][1], pbanks[5][1], pbanks[7][1]]
                    sets = [(g, b) for g in range(3) for b in range(2)]

                    def load_set(si):
                        g, b = sets[si]
                        sl = si % 2
                        r0 = (g * 4 + 2 * b) * 128
                        S.dma("sp", [(QK[sl][2][:], KTs[r0:r0 + 256, :].rearrange("(rg p) t -> p rg t", p=128)),
                                     (QK[sl][0][:], QTse[r0:r0 + 256, :].rearrange("(rg p) t -> p rg t", p=128)),
                                     (QK[sl][1][:], QTso[r0:r0 + 256, :].rearrange("(rg p) t -> p rg t", p=128))],
                              B_QK[sl], writes=[B_QK[sl]])
                    batches = []
                    vcount = 0
                    for si, (g, b) in enumerate(sets):
                        win, dil = GROUPS[g]
                        nblk = T // (128 * dil)
                        for r in range(dil):
                            prev = None
                            for n in range(nblk):
                                t0 = n * 128 * dil + r
                                cols = slice(t0, t0 + 127 * dil + 1, dil)
                                vi = vcount % NV
                                vcount += 1
                                batches.append(dict(si=si, g=g, b=b, dil=dil, t0=t0, cols=cols, vi=vi, prev=prev,
                                                    first=(r == 0 and n == 0)))
                                prev = (cols, vi)
                    load_set(0)

                    def emit_S(t, bt):
                        g, b, si = bt["g"], bt["b"], bt["si"]
                        sl = si % 2
                        if bt["first"] and si + 1 < len(sets):
                            load_set(si + 1)
                        sp_ = t % 3
                        psS = psall[:, (2 + 2 * sp_) * 512:(4 + 2 * sp_) * 512].rearrange("p (k m) -> p k m", m=256)
                        prev, cols = bt["prev"], bt["cols"]
                        wk = 256 if prev is not None else 128
                        Qe, Qo, Kt_ = QK[sl]

                        def sm_(e):
                            ins = None
                            for k in range(4):
                                rg = k // 2
                                Qx = Qe if k % 2 == 0 else Qo
                                ins = e.matmul(psS[:, k, 0:128], lhsT=Kt_[:, rg, cols], rhs=Qx[:, rg, cols], start=True, stop=True)
                                if prev is not None:
                                    ins = e.matmul(psS[:, k, 128:256], lhsT=Kt_[:, rg, prev[0]], rhs=Qx[:, rg, cols],
                                                   start=True, stop=True)
                            return ins
                        S.op("pe", sm_, reads=[B_QK[sl]], writes=[B_S2[sp_], B_S2b[sp_]])
                        S.op("act", lambda e: e.activation(out=Et2[sp_][:, :, 0:wk], in_=psS[:, :, 0:wk], func=AF.Exp, scale=0.125),
                             reads=[B_S2[sp_], B_S2b[sp_]], writes=[B_Et2[sp_]])
                        h0 = g * 8 + b * 4
                        S.op("dve", lambda e: e.tensor_tensor(out=Pt2[sp_][:, :, 0:wk], in0=Et2[sp_][:, :, 0:wk],
                                                              in1=Tbb[:, h0:h0 + 4, 0:wk], op=ALU.mult),
                             reads=[B_Et2[sp_], B_Tb], writes=[B_Pt2[sp_]])

                    def emit_V(bt):
                        g, b = bt["g"], bt["b"]
                        t0, dil, vi = bt["t0"], bt["dil"], bt["vi"]
                        S.dma("sp", [(Vt[vi][:], Vs[t0:t0 + 127 * dil + 1:dil, g * 8 + 4 * b:g * 8 + 4 * b + 4, :])], B_Vt[vi],
                              writes=[B_Vt[vi]])

                    def emit_PV(t, bt):
                        sp_ = t % 3
                        g, vi, prev, b = bt["g"], bt["vi"], bt["prev"], bt["b"]
                        o2 = t % 2
                        pO, BpO = pbanks[o2]

                        def pvm(e):
                            ins = None
                            for k in range(4):
                                oc = k * (HD + 1)
                                ins = e.matmul(pO[:, oc:oc + HD + 1], lhsT=Pt2[sp_][:, k, 0:128], rhs=Vt[vi][:, k, :],
                                               start=True, stop=(prev is None))
                                if prev is not None:
                                    ins = e.matmul(pO[:, oc:oc + HD + 1], lhsT=Pt2[sp_][:, k, 128:256], rhs=Vt[prev[1]][:, k, :],
                                                   start=False, stop=True)
                            return ins
                        rd = [B_Pt2[sp_], B_Vt[vi]] + ([B_Vt[prev[1]]] if prev is not None else [])
                        S.op("pe", pvm, reads=rd, writes=[BpO])
                        if o2 == 0:
                            S.op("act", lambda e: e.copy(out=ob[o2][:], in_=pO[:, 0:W4]), reads=[BpO], writes=[B_ob[o2]])
                        else:
                            S.op("dve", lambda e: e.tensor_copy(out=ob[o2][:], in_=pO[:, 0:W4]), reads=[BpO], writes=[B_ob[o2]])
                        t0, dil = bt["t0"], bt["dil"]
                        S.dma("sp", [(osc[g, t0:t0 + 127 * dil + 1:dil, b * W4:(b + 1) * W4], ob[o2][:])], B_ob[o2], reads=[B_ob[o2]])

                    for t in range(min(2, len(batches))):
                        emit_V(batches[t])
                    for t in range(len(batches) + 2):
                        if t + 2 < len(batches):
                            emit_V(batches[t + 2])
                        if t < len(batches):
                            emit_S(t, batches[t])
                        if t >= 2:
                            emit_PV(t - 2, batches[t - 2])

                S.barrier()

        if "D" in phases:
            with contextlib.ExitStack() as pd:
                Kc = [sbt(pd, "Kc%d" % i, [128, 512]) for i in range(2)]
                Vc = [sbt(pd, "Vc%d" % i, [128, 512]) for i in range(2)]
                B_Kc = [S.buf("Kc%d" % i) for i in range(2)]
                B_Vc = [S.buf("Vc%d" % i) for i in range(2)]
                Kcb = [sbt(pd, "Kcb%d" % i, [128, 512], BF16) for i in range(2)]
                B_Kcb = [S.buf() for _ in range(2)]
                KcT = [sbt(pd, "KcT%d" % i, [128, 4, 128], BF16) for i in range(2)]
                B_KcT = [S.buf() for _ in range(2)]
                Vca = sbt(pd, "Vca", [128, 13, 8, HD + 1], BF16)
                B_Vca = S.buf("Vca")
                S.op("pool", lambda e: e.memset(Vca[:].rearrange("p a b c -> p (a b c)"), 1.0), writes=[B_Vca])
                Pall = sbt(pd, "Pall", [128, 13, 8, 8], BF16)
                B_Pall = S.buf("Pall")
                Es = [sbt(pd, "Es%d" % i, [128, 64]) for i in range(2)]
                B_Es = [S.buf() for _ in range(2)]
                QTn = [sbt(pd, "QTn%d" % i, [128, 12, NST], BF16) for i in range(2)]
                KTn = sbt(pd, "KTn", [128, 12, NST], BF16)
                B_QKn = S.buf("QKn")
                S.dma("sp", [(QTn[0][:], QTse[:, T:TT].rearrange("(rg p) t -> p rg t", p=128)),
                             (QTn[1][:], QTso[:, T:TT].rearrange("(rg p) t -> p rg t", p=128)),
                             (KTn[:], KTs[:, T:TT].rearrange("(rg p) t -> p rg t", p=128))], B_QKn, writes=[B_QKn])
                Vn = sbt(pd, "Vn", [8, NQH, HD + 1], BF16)
                B_Vn = S.buf("Vn")
                Pn = sbt(pd, "Pn", [8, 3, 8, 8], BF16)
                En = sbt(pd, "En", [8, 3 * 64])
                B_Pn = S.buf("Pn")
                B_En = S.buf("En")
                obs = sbt(pd, "obs", [8, OW])
                B_obs = S.buf("obs")
                B_oscs = S.buf("oscs")
                kc_i = [0]

                def s_start(j):
                    S.dma("pool", [(Vn[:], Vs[T + j * DS:T + (j + 1) * DS, :, :])], B_Vn, writes=[B_Vn])

                def s_unit(j, gr):
                    g, r = CLASSES[gr]
                    qs_ = slice(j * DS, (j + 1) * DS)
                    dil = GROUPS[g][1]
                    s2 = kc_i[0] % 2
                    kc_i[0] += 1
                    S.dma("pool", [(Kc[s2][:], caches[g][j, r:r + 127 * dil + 1:dil, 0, :, :].rearrange("p h d -> p (h d)"))],
                          B_Kc[s2], writes=[B_Kc[s2]])
                    S.dma("pool", [(Vc[s2][:], caches[g][j, r:r + 127 * dil + 1:dil, 1, :, :].rearrange("p h d -> p (h d)"))],
                          B_Vc[s2], writes=[B_Vc[s2]])
                    S.op("act", lambda e: e.copy(out=Kcb[s2][:], in_=Kc[s2][:]), reads=[B_Kc[s2]], writes=[B_Kcb[s2]])
                    S.op("dve", lambda e: e.tensor_copy(out=Vca[:, gr, :, 0:HD], in_=Vc[s2][:].rearrange("p (h d) -> p h d", d=HD)),
                         reads=[B_Vc[s2]], writes=[B_Vca])
                    pb, Bpb = pbanks[4]
                    pbb = pb[:].bitcast(BF16)

                    def trk(e):
                        ins = None
                        for rg in range(4):
                            ins = e.transpose(out=pbb[:, rg * 128:(rg + 1) * 128], in_=Kcb[s2][:, rg * 128:(rg + 1) * 128],
                                              identity=identb[:, :])
                        return ins
                    S.op("pe", trk, reads=[B_Kcb[s2], B_identb], writes=[Bpb])
                    S.op("act", lambda e: e.copy(out=KcT[s2][:].rearrange("p a b -> p (a b)"), in_=pbb[:, 0:512]),
                         reads=[Bpb], writes=[B_KcT[s2]])
                    pS, BpS = pbanks[5]

                    def ssm(e):
                        ins = None
                        for hs in range(8):
                            rg = hs // 2
                            ins = e.matmul(pS[:, hs * 8:hs * 8 + 8], lhsT=KcT[s2][:, rg, :],
                                           rhs=QTn[hs % 2][:, g * 4 + rg, qs_], start=True, stop=True)
                        return ins
                    S.op("pe", ssm, reads=[B_KcT[s2], B_QKn], writes=[BpS])
                    S.op("act", lambda e: e.activation(out=Es[s2][:], in_=pS[:, 0:64], func=AF.Exp, scale=0.125),
                         reads=[BpS], writes=[B_Es[s2]])
                    S.op("dve", lambda e: e.tensor_tensor(out=Pall[:, gr, :, :].rearrange("p a b -> p (a b)"), in0=Es[s2][:],
                                                          in1=EBs[:, gr, :, :].rearrange("p a b -> p (a b)"), op=ALU.mult),
                         reads=[B_Es[s2], B_EBs], writes=[B_Pall])

                def s_finish(j):
                    qs_ = slice(j * DS, (j + 1) * DS)
                    pS, BpS = pbanks[5]

                    def snm(e):
                        ins = None
                        for g in range(3):
                            for hs in range(8):
                                rg = hs // 2
                                ins = e.matmul(pS[0:8, g * 64 + hs * 8:g * 64 + hs * 8 + 8], lhsT=KTn[:, g * 4 + rg, qs_],
                                               rhs=QTn[hs % 2][:, g * 4 + rg, qs_], start=True, stop=True)
                        return ins
                    S.op("pe", snm, reads=[B_QKn], writes=[BpS])
                    S.op("act", lambda e: e.activation(out=En[:], in_=pS[0:8, 0:192], func=AF.Exp, scale=0.125),
                         reads=[BpS], writes=[B_En])
                    S.op("dve", lambda e: e.tensor_tensor(out=Pn[:].rearrange("p a b c -> p (a b c)"), in0=En[:],
                                                          in1=EBn[:].rearrange("p a b c -> p (a b c)"), op=ALU.mult),
                         reads=[B_En, B_EBn], writes=[B_Pn])
                    pA, BpA = pbanks[6]
                    pB, BpB = pbanks[7]
                    for hs in range(8):
                        po_t, Bpo = (pA, BpA) if hs < 4 else (pB, BpB)
                        oc = (hs % 4) * (HD + 1)

                        def pvs(e):
                            ins = None
                            for gr, (g, r) in enumerate(CLASSES):
                                ins = e.matmul(po_t[0:8, oc:oc + HD + 1], lhsT=Pall[:, gr, hs, :], rhs=Vca[:, gr, hs, :],
                                               start=(gr == 0), stop=False)
                            for g in range(3):
                                ins = e.matmul(po_t[0:8, oc:oc + HD + 1], lhsT=Pn[:, g, hs, :], rhs=Vn[:, g * 8 + hs, :],
                                               start=False, stop=(g == 2))
                            return ins
                        S.op("pe", pvs, reads=[B_Pall, B_Vca, B_Pn, B_Vn], writes=[Bpo])
                    S.op("act", lambda e: e.copy(out=obs[:, 0:4 * (HD + 1)], in_=pA[0:8, 0:4 * (HD + 1)]), reads=[BpA], writes=[B_obs])
                    S.op("act", lambda e: e.copy(out=obs[:, 4 * (HD + 1):OW], in_=pB[0:8, 0:4 * (HD + 1)]), reads=[BpB], writes=[B_obs])
                    S.dma("sp", [(osc[0, T + j * DS:T + (j + 1) * DS, :], obs[:])], B_obs, reads=[B_obs], writes=[B_oscs])
                sitems = []
                for j in range(NS):
                    sitems.append((s_start, (j,)))
                    for gr in range(13):
                        sitems.append((s_unit, (j, gr)))
                    sitems.append((s_finish, (j,)))
                sitems.reverse()

                woB = sbt(pd, "woB", [128, 4, D], BF16)
                B_woB = S.buf("woB")
                S.dma("pool", [(woB[:], w_out_b.rearrange("(k p) c -> p k c", p=128))], B_woB, writes=[B_woB])
                o3 = [[sbt(pd, "o3_%d_%d" % (i, g), [128, OW]) for g in range(3)] for i in range(3)]
                B_o3 = [[S.buf("o3_%d_%d" % (i, g)) for g in range(3)] for i in range(3)]
                zt = [sbt(pd, "zt%d" % i, [128, 512], BF16) for i in range(3)]
                B_zt = [S.buf("zt%d" % i) for i in range(3)]
                x1t = [sbt(pd, "x1t%d" % i, [128, D]) for i in range(4)]
                B_x1t = [S.buf("x1t%d" % i) for i in range(4)]
                rden = [sbt(pd, "rden%d" % i, [128, 8]) for i in range(2)]
                B_rden = [S.buf() for _ in range(2)]
                om = [sbt(pd, "om%d" % i, [128, 512]) for i in range(2)]
                B_om = [S.buf() for _ in range(2)]
                og = [sbt(pd, "og%d" % i, [128, 512], BF16) for i in range(2)]
                B_og = [S.buf() for _ in range(2)]
                ogT = [sbt(pd, "ogT%d" % i, [128, 4, 128], BF16) for i in range(2)]
                B_ogT = [S.buf() for _ in range(2)]
                yt = [sbt(pd, "yt%d" % i, [128, D]) for i in range(2)]
                B_yt = [S.buf("yt%d" % i) for i in range(2)]

                def d_load(i):
                    rows, c0 = tile_rows(i)
                    s3, s4 = i % 3, i % 4
                    ng = 3 if i < NT else 1
                    for g in range(ng):
                        S.dma("sp", [(o3[s3][g][:rows], osc[g, c0:c0 + rows, :])], B_o3[s3][g], writes=[B_o3[s3][g]],
                              reads=([B_oscs] if i >= NT else []))
                    S.dma("sp", [(zt[s3][:rows], zss[c0:c0 + rows, :])], B_zt[s3], writes=[B_zt[s3]])
                    S.dma("sp", [(x1t[s4][:rows], x1s[c0:c0 + rows, :])], B_x1t[s4], writes=[B_x1t[s4]])

                def d_s0(i):
                    rows, c0 = tile_rows(i)
                    s2, s3 = i % 2, i % 3
                    if i < NT:
                        S.op("dve", lambda e: e.tensor_tensor(out=o3[s3][0][:rows], in0=o3[s3][0][:rows], in1=o3[s3][1][:rows], op=ALU.add),
                             reads=[B_o3[s3][0], B_o3[s3][1]], writes=[B_o3[s3][0]])
                        S.op("dve", lambda e: e.tensor_tensor(out=o3[s3][0][:rows], in0=o3[s3][0][:rows], in1=o3[s3][2][:rows], op=ALU.add),
                             reads=[B_o3[s3][0], B_o3[s3][2]], writes=[B_o3[s3][0]])
                    ov = o3[s3][0][:rows].rearrange("p (h d) -> p h d", d=HD + 1)
                    S.op("dve", lambda e: e.reciprocal(out=rden[s2][:rows].unsqueeze(2), in_=ov[:, :, HD:HD + 1]),
                         reads=[B_o3[s3][0]], writes=[B_rden[s2]])
                    S.op("dve", lambda e: e.tensor_tensor(out=om[s2][:rows].rearrange("p (h d) -> p h d", d=HD), in0=ov[:, :, 0:HD],
                                                          in1=rden[s2][:rows].unsqueeze(2).to_broadcast([rows, 8, HD]), op=ALU.mult),
                         reads=[B_o3[s3][0], B_rden[s2]], writes=[B_om[s2]])
                    S.op("dve", lambda e: e.tensor_tensor(out=og[s2][:rows], in0=om[s2][:rows], in1=zt[s3][:rows], op=ALU.mult),
                         reads=[B_om[s2], B_zt[s3]], writes=[B_og[s2]])

                drr = [0]

                def dbank():
                    t_, b_ = pbanks[drr[0] % 4]
                    drr[0] += 1
                    return t_, b_

                def d_s1(i):
                    rows, c0 = tile_rows(i)
                    s2 = i % 2
                    pb, Bpb = dbank()
                    pbb = pb[:].bitcast(BF16)

                    def tro(e):
                        ins = None
                        for kc in range(4):
                            ins = e.transpose(out=pbb[:, kc * 128:kc * 128 + rows], in_=og[s2][:rows, kc * 128:(kc + 1) * 128],
                                              identity=identb[:rows, :rows])
                        return ins
                    S.op("pe", tro, reads=[B_og[s2], B_identb], writes=[Bpb])
                    S.op("act", lambda e: e.copy(out=ogT[s2][:, :, :rows], in_=pbb[:, 0:512].rearrange("p (k t) -> p k t", t=128)[:, :, :rows]),
                         reads=[Bpb], writes=[B_ogT[s2]])

                def d_s2(i):
                    rows, c0 = tile_rows(i)
                    s2, s4 = i % 2, i % 4
                    for half in range(2):
                        pb, Bpb = dbank()

                        def ym2(e):
                            ins = None
                            for kc in range(4):
                                ins = e.matmul(pb[:rows, :], lhsT=ogT[s2][:, kc, :rows], rhs=woB[:, kc, half * 512:(half + 1) * 512],
                                               start=(kc == 0), stop=(kc == 3))
                            return ins
                        S.op("pe", ym2, reads=[B_ogT[s2], B_woB], writes=[Bpb])
                        S.op("dve" if half == 0 else "act", lambda e: (e.tensor_tensor(
                            out=yt[s2][:rows, half * 512:(half + 1) * 512], in0=pb[:rows, :],
                            in1=x1t[s4][:rows, half * 512:(half + 1) * 512], op=ALU.add)),
                            reads=[Bpb, B_x1t[s4]], writes=[B_yt[s2]]) if half == 0 else S.op("dve", lambda e: e.tensor_tensor(
                            out=yt[s2][:rows, half * 512:(half + 1) * 512], in0=pb[:rows, :],
                            in1=x1t[s4][:rows, half * 512:(half + 1) * 512], op=ALU.add),
                            reads=[Bpb, B_x1t[s4]], writes=[B_yt[s2]])
                    dst = y_p[c0:c0 + rows, :] if i < NT else y_s[:, :]
                    S.dma("sp", [(dst, yt[s2][:rows])], B_yt[s2], reads=[B_yt[s2]])
                stages = [d_load, d_s0, d_s1, d_s2]
                per_step = -(-len(sitems) // max(1, NTT - 4))
                for step in range(NTT + len(stages) - 1):
                    for _ in range(per_step):
                        if sitems:
                            fn_, args_ = sitems.pop()
                            fn_(*args_)
                    if step == NT:
                        while sitems:
                            fn_, args_ = sitems.pop()
                            fn_(*args_)
                    for st in range(len(stages) - 1, -1, -1):
                        i = step - st
                        if 0 <= i < NTT:
                            stages[st](i)
        ps_.close()
        S.finish()
        build.stats = dict(cnt=dict(S.cnt), nsem=len(S.dbufs_all) + 5, maxd=max([c for _, c in S.free_sems] + [b.dcnt for b in S.dbufs] + [0]))
    return nc, dbg_out


T_FULL = 4096
N_CORES = 8
_CACHE = {}


def _get_nc(T):
    if T not in _CACHE:
        _CACHE[T] = build(T)[0]
    return _CACHE[T]


def make_in_maps(T, x_prompt, x_sample, state_mlstm_C, state_mlstm_n, state_mlstm_m,
                 cache_kv_w128, cache_kv_w512, cache_kv_w2048,
                 norm_a, w_in_a, b_gates_a, hnorm_a, w_out_a, norm_kv, w_kv, k_norm,
                 norm_b, w_in_b, q_norm, rel_bias, w_out_b):
    f = lambda a: np.ascontiguousarray(np.asarray(a, dtype=np.float32))
    consts = make_consts(T)
    shared = dict(
        norm_a=f(norm_a).reshape(1, D), w_in_a=f(w_in_a)[0], b_gates=f(b_gates_a).reshape(1, 2 * H),
        hnorm=f(hnorm_a).reshape(DI), w_out_a=f(w_out_a)[0], norm_kv=f(norm_kv), w_kv=f(w_kv),
        k_norm=f(k_norm).reshape(1, HD), norm_b=f(norm_b).reshape(D), w_in_b=f(w_in_b)[0],
        q_norm=f(q_norm).reshape(1, HD), rel_bias=f(rel_bias), w_out_b=f(w_out_b)[0])
    shared.update(consts)
    xp = f(x_prompt)
    xs = f(x_sample)
    sC = f(state_mlstm_C)[0]
    sn_ = f(state_mlstm_n)[0]
    sm_ = f(state_mlstm_m)[0]
    c1, c5, c20 = f(cache_kv_w128), f(cache_kv_w512), f(cache_kv_w2048)
    maps = []
    for c in range(xp.shape[0]):
        sl = slice(c * NS, (c + 1) * NS)
        m = dict(shared)
        m.update(xp=xp[c], xs=xs[sl].reshape(NST, D), sC=sC[sl], sn=sn_[sl], sm=sm_[sl],
                 c128=c1[sl], c512=c5[sl], c2048=c20[sl])
        maps.append(m)
    return maps


def gather(results, T):
    n = len(results)
    cat = lambda k: np.stack([np.asarray(r[k], np.float32) for r in results], 0)
    y_p = cat("y_p")
    y_s = cat("y_s").reshape(n * NS, DS, D)
    C_p = cat("C_p")[None]
    n_p = cat("n_p")[None]
    m_p = cat("m_p").reshape(n, H)[None]
    C_s = cat("C_s").reshape(n * NS, H, DH, DH)[None]
    n_s = cat("n_s").reshape(n * NS, H, DH)[None]
    m_s = cat("m_s").reshape(n * NS, H)[None]
    kv = [cat(k) for k in ("kv128_p", "kv512_p", "kv2048_p")]
    kvs_ = [cat(k).reshape(n * NS, DS, 2, 8, HD) for k in ("kv128_s", "kv512_s", "kv2048_s")]
    return (y_p, y_s, C_p, n_p, m_p, C_s, n_s, m_s, kv[0], kv[1], kv[2], kvs_[0], kvs_[1], kvs_[2])


def kernel(**inputs):
    T = int(np.asarray(inputs["x_prompt"]).shape[1])
    n = int(np.asarray(inputs["x_prompt"]).shape[0])
    maps = make_in_maps(T, **inputs)
    nc = _get_nc(T)
    res = run_bass_kernel_spmd(nc, maps, core_ids=list(range(n)))
    return gather(res.results, T)
```
